# Optimizing a Trainium2 kernel written in Bass

```python
import jax, jax.numpy as jnp
from jax import lax
import numpy as np

D_MODEL = 1024
BATCH = 2
SEQ = 8192
DEPTH = 1
DEC_BATCH = 32
DEC_SEQ = 1
PAST_LEN = 8192
PAGE_SIZE = 128

NSA_HEADS = 8
NSA_KV_HEADS = 2
NSA_GROUP = NSA_HEADS // NSA_KV_HEADS
HEAD_DIM = 64
NSA_WIDTH = NSA_HEADS * HEAD_DIM
KV_WIDTH = NSA_KV_HEADS * HEAD_DIM
CMP_BLOCK = 32
CMP_STRIDE = 16
CMP_RATIO = CMP_BLOCK // CMP_STRIDE
CMP_HIDDEN = 128
SEL_BLOCK = 64
SEL_RATIO = SEL_BLOCK // CMP_STRIDE
N_SEL = 16
WINDOW = 512
Q_BLOCK = 128
HG_HEADS = 4
HG_DIM = 128
HG_WIDTH = HG_HEADS * HG_DIM
HG_CHUNK = 64
MIX_WIDTH = NSA_WIDTH + HG_WIDTH
PLE_DIM = 256
EPS = 1e-6
NEG = -1e30
FORCE_BONUS = 1e6

IN_SPLITS = (('q_a', NSA_WIDTH), ('k_cmp', KV_WIDTH), ('v_cmp', KV_WIDTH), ('k_slc', KV_WIDTH), ('v_slc', KV_WIDTH),
             ('k_win', KV_WIDTH), ('v_win', KV_WIDTH), ('gate_a', 3 * NSA_HEADS), ('z_a', NSA_WIDTH),
             ('q_b', HG_WIDTH), ('f_b', HG_WIDTH), ('i_b', HG_WIDTH), ('z_b', HG_WIDTH))
IN_WIDTH = 2 * NSA_WIDTH + 6 * KV_WIDTH + 3 * NSA_HEADS + 4 * HG_WIDTH

kernel_name = 'hymba_nsa_hgrn2_decode_step'


def _rmsnorm(x, g):
    x32 = x.astype(jnp.float32)
    y = x32 * lax.rsqrt(jnp.mean(x32 * x32, axis=-1, keepdims=True) + EPS)
    return (y * g.astype(jnp.float32)).astype(x.dtype)


def _masked_softmax(s, mask):
    s = jnp.where(mask, s, NEG)
    m = jnp.max(s, axis=-1, keepdims=True)
    e = jnp.where(mask, jnp.exp(s - m), 0.0)
    return e / jnp.maximum(jnp.sum(e, axis=-1, keepdims=True), 1e-30)


def _alibi_slopes():
    h = jnp.arange(1, NSA_HEADS + 1, dtype=jnp.float32)
    return jnp.exp2(-8.0 * h / NSA_HEADS)


def _split(z):
    out, off = {}, 0
    for name, width in IN_SPLITS:
        out[name] = z[..., off:off + width]
        off += width
    return out


def _project(x, g_pre, w_in):
    B, L, _ = x.shape
    z = _split(_rmsnorm(x, g_pre) @ w_in)
    kv = {n: z[n].reshape(B, L, NSA_KV_HEADS, HEAD_DIM)
          for n in ('k_cmp', 'v_cmp', 'k_slc', 'v_slc', 'k_win', 'v_win')}
    gates = jax.nn.sigmoid(z['gate_a'].astype(jnp.float32)).reshape(B, L, NSA_HEADS, 3)
    return z, kv, gates


def _compress(x, pe, w1, b1, w2):
    B, L = x.shape[:2]
    nseg = L // CMP_STRIDE
    nc = nseg - CMP_RATIO + 1
    seg = x[:, :nseg * CMP_STRIDE].reshape(B, nseg, CMP_STRIDE, NSA_KV_HEADS, HEAD_DIM)
    blocks = jnp.concatenate([seg[:, j:j + nc] for j in range(CMP_RATIO)], axis=2)
    blocks = blocks + pe[None, None, :, None, :]
    flat = blocks.transpose(0, 1, 3, 2, 4).reshape(B, nc, NSA_KV_HEADS, CMP_BLOCK * HEAD_DIM)
    return jax.nn.silu(flat @ w1 + b1) @ w2


def _cmp_end(nc):
    return jnp.arange(nc) * CMP_STRIDE + CMP_BLOCK - 1


def _sel_blocks(x):
    B, L = x.shape[:2]
    ns = -(-L // SEL_BLOCK)
    x = jnp.pad(x, ((0, 0), (0, ns * SEL_BLOCK - L), (0, 0), (0, 0)))
    return x.reshape(B, ns, SEL_BLOCK, NSA_KV_HEADS, HEAD_DIM)


def _nsa_attend(q, gates, t, k_c, v_c, c_end, kb_s, vb_s, k_w, v_w, w_pos):
    f32 = jnp.float32
    B, Q = q.shape[:2]
    slopes = _alibi_slopes().reshape(NSA_KV_HEADS, NSA_GROUP)
    qg = q.astype(f32).reshape(B, Q, NSA_KV_HEADS, NSA_GROUP, HEAD_DIM) * (HEAD_DIM ** -0.5)
    d_c = t[:, None] - c_end[None, :]
    s_c = jnp.einsum('bqgrd,bngd->bgrqn', qg, k_c.astype(f32)) - slopes[:, :, None, None] * d_c.astype(f32)
    p_c = _masked_softmax(s_c, d_c >= 0)
    o_c = jnp.einsum('bgrqn,bngd->bqgrd', p_c, v_c.astype(f32))
    ns = kb_s.shape[1]
    imp = jnp.sum(p_c, axis=2)
    wts = np.convolve(np.ones(SEL_RATIO), np.ones(CMP_RATIO))
    imp = jnp.pad(imp, ((0, 0), (0, 0), (0, 0), (0, SEL_RATIO * ns + len(wts) - imp.shape[-1])))
    p_s = sum(float(wk) * imp[..., k:k + SEL_RATIO * ns:SEL_RATIO] for k, wk in enumerate(wts))
    blk = jnp.arange(ns)[None, :]
    jt = (t // SEL_BLOCK)[:, None]
    valid = blk * SEL_BLOCK <= t[:, None]
    forced = (blk == 0) | (blk == jt) | (blk == jt - 1)
    score = jnp.where(valid, p_s + jnp.where(forced, FORCE_BONUS, 0.0), NEG)
    top_v, top_i = lax.top_k(score, min(N_SEL, ns))
    gather = jax.vmap(jax.vmap(lambda blocks, idx: blocks[idx]))
    k_sel = gather(kb_s.transpose(0, 3, 1, 2, 4), top_i).astype(f32)
    v_sel = gather(vb_s.transpose(0, 3, 1, 2, 4), top_i).astype(f32)
    pos = top_i[..., None] * SEL_BLOCK + jnp.arange(SEL_BLOCK)
    d_s = t[None, None, :, None, None] - pos
    mask_s = (top_v > NEG * 0.5)[..., None] & (d_s >= 0)
    s_s = (jnp.einsum('bqgrd,bgqnkd->bgrqnk', qg, k_sel)
           - slopes[None, :, :, None, None, None] * d_s[:, :, None].astype(f32))
    p_sel = _masked_softmax(s_s.reshape(B, NSA_KV_HEADS, NSA_GROUP, Q, -1),
                            mask_s[:, :, None].reshape(B, NSA_KV_HEADS, 1, Q, -1))
    o_s = jnp.einsum('bgrqm,bgqmd->bqgrd', p_sel, v_sel.reshape(B, NSA_KV_HEADS, Q, -1, HEAD_DIM))
    d_w = t[:, None] - w_pos[None, :]
    mask_w = (d_w >= 0) & (d_w <= WINDOW) & (w_pos[None, :] >= 0)
    s_w = jnp.einsum('bqgrd,bkgd->bgrqk', qg, k_w.astype(f32)) - slopes[:, :, None, None] * d_w.astype(f32)
    p_w = _masked_softmax(s_w, mask_w)
    o_w = jnp.einsum('bgrqk,bkgd->bqgrd', p_w, v_w.astype(f32))
    g = gates.astype(f32).reshape(B, Q, NSA_KV_HEADS, NSA_GROUP, 3)
    o = g[..., 0:1] * o_c + g[..., 1:2] * o_s + g[..., 2:3] * o_w
    return o.reshape(B, Q, NSA_WIDTH).astype(q.dtype)


def _hgrn_inputs(z, lb):
    f32 = jnp.float32
    B, L = z['q_b'].shape[:2]
    fl = z['f_b'].astype(f32)
    f = lb + (1.0 - lb) * jax.nn.sigmoid(fl)
    k = (1.0 - lb) * jax.nn.sigmoid(-fl)
    q = jax.nn.silu(z['q_b'].astype(f32))
    v = z['i_b'].astype(f32)
    heads = lambda a: a.reshape(B, L, HG_HEADS, HG_DIM).transpose(0, 2, 1, 3)
    return heads(q), heads(k), heads(jnp.log(f)), heads(v)


def _hgrn_chunk(S, q, k, logf, v):
    C = q.shape[2]
    b = jnp.cumsum(logf, axis=2)
    o_inter = jnp.einsum('bhtk,bhkv->bhtv', q * jnp.exp(b), S)
    causal = jnp.tril(jnp.ones((C, C), dtype=bool))[:, :, None]
    diff = b[:, :, :, None, :] - b[:, :, None, :, :]
    dec = jnp.where(causal, jnp.exp(jnp.where(causal, diff, 0.0)), 0.0)
    a = jnp.einsum('bhtk,bhtsk,bhsk->bhts', q, dec, k)
    o = o_inter + jnp.einsum('bhts,bhsv->bhtv', a, v)
    b_last = b[:, :, -1]
    S_new = jnp.exp(b_last)[..., None] * S + jnp.einsum('bhsk,bhsv->bhkv', k * jnp.exp(b_last[:, :, None] - b), v)
    return S_new, o


def _hgrn_scan(q, k, logf, v, S0):
    B, H, L, _ = q.shape
    C = min(HG_CHUNK, L)
    nc = L // C
    ch = lambda a: a.reshape(B, H, nc, C, a.shape[-1]).transpose(2, 0, 1, 3, 4)
    S, o = lax.scan(lambda s, xs: _hgrn_chunk(s, *xs), S0, (ch(q), ch(k), ch(logf), ch(v)))
    return S, o.transpose(1, 2, 0, 3, 4).reshape(B, H, L, HG_DIM)


def _hgrn_out(o, norm_w, zb):
    B, _, L, _ = o.shape
    o = o.transpose(0, 2, 1, 3)
    o = o * lax.rsqrt(jnp.mean(o * o, axis=-1, keepdims=True) + EPS) * norm_w.astype(jnp.float32)
    return (o.reshape(B, L, HG_WIDTH) * jax.nn.silu(zb.astype(jnp.float32))).astype(zb.dtype)


def _finish(x, y_a, y_b, p, g_post, w_out, ple_proj, ple_gate):
    mix = jnp.concatenate([y_a, y_b], axis=-1) @ w_out
    h = x + _rmsnorm(mix, g_post)
    return h + (p @ ple_proj) * jax.nn.sigmoid(h @ ple_gate)


def _prompt_layer(x, p, w_in, g_pre, cmp_pe, cmp_w1, cmp_b1, cmp_w2, lb, hg_norm, w_out, g_post, ple_proj, ple_gate):
    B, L, _ = x.shape
    z, kv, gates = _project(x, g_pre, w_in)
    k_c = _compress(kv['k_cmp'], cmp_pe[0], cmp_w1[0], cmp_b1[0], cmp_w2[0])
    v_c = _compress(kv['v_cmp'], cmp_pe[1], cmp_w1[1], cmp_b1[1], cmp_w2[1])
    c_end = _cmp_end(k_c.shape[1])
    kb_s, vb_s = _sel_blocks(kv['k_slc']), _sel_blocks(kv['v_slc'])
    pad = ((0, 0), (WINDOW, 0), (0, 0), (0, 0))
    kp, vp = jnp.pad(kv['k_win'], pad), jnp.pad(kv['v_win'], pad)
    qb = min(Q_BLOCK, L)
    nq = L // qb
    q_blocks = z['q_a'].reshape(B, nq, qb, NSA_WIDTH).transpose(1, 0, 2, 3)
    g_blocks = gates.reshape(B, nq, qb, NSA_HEADS, 3).transpose(1, 0, 2, 3, 4)
    starts = jnp.arange(nq) * qb

    def one_block(args):
        qbk, gbk, s0 = args
        t = s0 + jnp.arange(qb)
        kw = lax.dynamic_slice_in_dim(kp, s0, WINDOW + qb, axis=1)
        vw = lax.dynamic_slice_in_dim(vp, s0, WINDOW + qb, axis=1)
        w_pos = s0 - WINDOW + jnp.arange(WINDOW + qb)
        return _nsa_attend(qbk, gbk, t, k_c, v_c, c_end, kb_s, vb_s, kw, vw, w_pos)

    o_a = lax.map(one_block, (q_blocks, g_blocks, starts)).transpose(1, 0, 2, 3).reshape(B, L, NSA_WIDTH)
    y_a = o_a * jax.nn.silu(z['z_a'])
    q, k, logf, v = _hgrn_inputs(z, lb)
    S, o_b = _hgrn_scan(q, k, logf, v, jnp.zeros((B, HG_HEADS, HG_DIM, HG_DIM), jnp.float32))
    y_b = _hgrn_out(o_b, hg_norm, z['z_b'])
    h = _finish(x, y_a, y_b, p, g_post, w_out, ple_proj, ple_gate)
    wl = min(WINDOW, L)
    new_cmp = jnp.stack([kv['k_cmp'], kv['v_cmp']], axis=2)
    new_slc = jnp.stack([kv['k_slc'], kv['v_slc']], axis=2)
    new_win = jnp.stack([kv['k_win'][:, L - wl:], kv['v_win'][:, L - wl:]], axis=2)
    return h, new_cmp, new_slc, new_win, S


def _sample_layer(x, p, cmp_cache, slc_cache, win_cache, S0, page_table, w_in, g_pre, cmp_pe, cmp_w1, cmp_b1, cmp_w2,
                  lb, hg_norm, w_out, g_post, ple_proj, ple_gate):
    B, T, _ = x.shape
    past = page_table.shape[1] * cmp_cache.shape[1]
    z, kv, gates = _project(x, g_pre, w_in)
    new_cmp = jnp.stack([kv['k_cmp'], kv['v_cmp']], axis=2)
    new_slc = jnp.stack([kv['k_slc'], kv['v_slc']], axis=2)
    new_win = jnp.stack([kv['k_win'], kv['v_win']], axis=2)
    tail = (2, NSA_KV_HEADS, HEAD_DIM)
    full_cmp = jnp.concatenate([cmp_cache[page_table].reshape((B, past) + tail), new_cmp], axis=1)
    full_slc = jnp.concatenate([slc_cache[page_table].reshape((B, past) + tail), new_slc], axis=1)
    win_all = jnp.concatenate([win_cache, new_win], axis=1)
    wl = win_cache.shape[1]
    t = past + jnp.arange(T)
    k_c = _compress(full_cmp[:, :, 0], cmp_pe[0], cmp_w1[0], cmp_b1[0], cmp_w2[0])
    v_c = _compress(full_cmp[:, :, 1], cmp_pe[1], cmp_w1[1], cmp_b1[1], cmp_w2[1])
    c_end = _cmp_end(k_c.shape[1])
    kb_s, vb_s = _sel_blocks(full_slc[:, :, 0]), _sel_blocks(full_slc[:, :, 1])
    w_pos = past - wl + jnp.arange(wl + T)
    o_a = _nsa_attend(z['q_a'], gates, t, k_c, v_c, c_end, kb_s, vb_s, win_all[:, :, 0], win_all[:, :, 1], w_pos)
    y_a = o_a * jax.nn.silu(z['z_a'])
    q, k, logf, v = _hgrn_inputs(z, lb)
    S, o_b = _hgrn_chunk(S0.astype(jnp.float32), q, k, logf, v)
    y_b = _hgrn_out(o_b, hg_norm, z['z_b'])
    h = _finish(x, y_a, y_b, p, g_post, w_out, ple_proj, ple_gate)
    return h, new_cmp, new_slc, win_all[:, T:], S


def setup_inputs(seed: int = 0) -> dict:
    key = jax.random.key(seed)
    ks = jax.random.split(key, 24)
    f32 = jnp.float32
    n_pages = PAST_LEN // PAGE_SIZE
    n_used = DEC_BATCH * n_pages
    n_pool = n_used + max(1, n_used // 4)
    win_len = min(WINDOW, PAST_LEN)

    def nrm(k, shape, scale=1.0):
        return jax.random.normal(k, shape, f32) * scale

    page_table = jax.random.permutation(ks[0], n_pool)[:n_used].reshape(DEC_BATCH, n_pages).astype(jnp.int32)
    kv_tail = (2, NSA_KV_HEADS, HEAD_DIM)
    return {
        'x_prompt': nrm(ks[1], (BATCH, SEQ, D_MODEL)),
        'x_sample': nrm(ks[2], (DEC_BATCH, DEC_SEQ, D_MODEL)),
        'cache_cmp_kv': nrm(ks[3], (DEPTH, n_pool, PAGE_SIZE) + kv_tail),
        'cache_slc_kv': nrm(ks[4], (DEPTH, n_pool, PAGE_SIZE) + kv_tail),
        'cache_win_kv': nrm(ks[5], (DEPTH, DEC_BATCH, win_len) + kv_tail),
        'state_hgrn': nrm(ks[6], (DEPTH, DEC_BATCH, HG_HEADS, HG_DIM, HG_DIM), 0.3),
        'page_table': page_table,
        'p_prompt': nrm(ks[7], (DEPTH, BATCH, SEQ, PLE_DIM)),
        'p_sample': nrm(ks[8], (DEPTH, DEC_BATCH, DEC_SEQ, PLE_DIM)),
        'w_in': nrm(ks[9], (DEPTH, D_MODEL, IN_WIDTH), D_MODEL ** -0.5),
        'g_pre': 1.0 + nrm(ks[10], (DEPTH, D_MODEL), 0.1),
        'cmp_pe': nrm(ks[11], (DEPTH, 2, CMP_BLOCK, HEAD_DIM), 0.5),
        'cmp_w1': nrm(ks[12], (DEPTH, 2, CMP_BLOCK * HEAD_DIM, CMP_HIDDEN), (CMP_BLOCK * HEAD_DIM) ** -0.5),
        'cmp_b1': nrm(ks[13], (DEPTH, 2, CMP_HIDDEN), 0.01),
        'cmp_w2': nrm(ks[14], (DEPTH, 2, CMP_HIDDEN, HEAD_DIM), CMP_HIDDEN ** -0.5),
        'hg_lower': nrm(ks[15], (DEPTH + 1, HG_WIDTH), 0.5),
        'hg_norm': 1.0 + nrm(ks[16], (DEPTH, HG_DIM), 0.1),
        'w_out': nrm(ks[17], (DEPTH, MIX_WIDTH, D_MODEL), MIX_WIDTH ** -0.5),
        'g_post': 1.0 + nrm(ks[18], (DEPTH, D_MODEL), 0.1),
        'ple_proj': nrm(ks[19], (DEPTH, PLE_DIM, D_MODEL), PLE_DIM ** -0.5),
        'ple_gate': nrm(ks[20], (DEPTH, D_MODEL, D_MODEL), D_MODEL ** -0.5),
    }


def reference(x_prompt, x_sample, cache_cmp_kv, cache_slc_kv, cache_win_kv, state_hgrn, page_table, p_prompt, p_sample,
              w_in, g_pre, cmp_pe, cmp_w1, cmp_b1, cmp_w2, hg_lower, hg_norm, w_out, g_post, ple_proj, ple_gate):
    lb_all = jnp.cumsum(jax.nn.softmax(hg_lower.astype(jnp.float32), axis=0), axis=0)
    hp, hs = x_prompt, x_sample
    st_p, st_s = [], []
    for i in range(DEPTH):
        shared = (w_in[i], g_pre[i], cmp_pe[i], cmp_w1[i], cmp_b1[i], cmp_w2[i], lb_all[i], hg_norm[i],
                  w_out[i], g_post[i], ple_proj[i], ple_gate[i])
        hp, a0, a1, a2, a3 = _prompt_layer(hp, p_prompt[i], *shared)
        hs, b0, b1, b2, b3 = _sample_layer(hs, p_sample[i], cache_cmp_kv[i], cache_slc_kv[i], cache_win_kv[i],
                                           state_hgrn[i], page_table, *shared)
        st_p.append((a0, a1, a2, a3))
        st_s.append((b0, b1, b2, b3))
    cmp_p = jnp.stack([s[0] for s in st_p])
    slc_p = jnp.stack([s[1] for s in st_p])
    win_p = jnp.stack([s[2] for s in st_p])
    hg_p = jnp.stack([s[3] for s in st_p])
    cmp_s = jnp.stack([s[0] for s in st_s])
    slc_s = jnp.stack([s[1] for s in st_s])
    win_s = jnp.stack([s[2] for s in st_s])
    hg_s = jnp.stack([s[3] for s in st_s])
    return (hp, hs, cmp_p, slc_p, win_p, hg_p, cmp_s, slc_s, win_s, hg_s)
```

```python
import contextlib
import numpy as np
import ml_dtypes
import concourse.bass as bass
import concourse.mybir as mybir
from concourse.bass_utils import run_bass_kernel_spmd

F32 = mybir.dt.float32
BF16 = mybir.dt.bfloat16
I32 = mybir.dt.int32
AF = mybir.ActivationFunctionType
ALU = mybir.AluOpType
bf16 = ml_dtypes.bfloat16

NEGM = -30000.0
BIGA = -16384.0
EPS = 1e-6


class Op:
    __slots__ = ("eng", "idx", "fn", "waits", "dwaits", "signaled", "semval", "dma_sem")

    def __init__(self, eng, idx, fn):
        self.eng = eng
        self.idx = idx
        self.fn = fn
        self.waits = []
        self.dwaits = []
        self.signaled = False
        self.semval = None
        self.dma_sem = None


class _Rec:
    def __init__(self):
        self.call = None

    def __getattr__(self, name):
        def f(*a, **kw):
            self.call = (name, a, kw)
            return None
        return f


class DmaSem:
    def __init__(self):
        self.total = 0
        self.handle = None


class Buf:
    def __init__(self, name):
        self.name = name
        self.w = None
        self.r = []
        self.dsem = None
        self.dsem_sw = None
        self.excl = False


class Eng:
    def __init__(self, name):
        self.name = name
        self.ops = []
        self.known = {}
        self.dknown = {}
        self.sem = None


class Prog:
    ENG_NAMES = ("tensor", "vector", "scalar", "gpsimd", "sync")

    def __init__(self):
        self.eng = {n: Eng(n) for n in self.ENG_NAMES}
        self.dsems = []
        self.nbuf = 0

    def buf(self, name=None):
        self.nbuf += 1
        return Buf(name or f"b{self.nbuf}")

    def _need(self, e, tok, op, rr=False):
        if tok is None:
            return
        if tok[0] == "op":
            p = tok[1]
            if p.eng is e and (e.name == "tensor" or rr):
                return
            if e.known.get(p.eng.name, -1) >= p.idx:
                return
            e.known[p.eng.name] = p.idx
            p.signaled = True
            op.waits.append(p)
        else:
            ds = tok[1]
            val = ds.total
            if e.dknown.get(ds, 0) >= val:
                return
            e.dknown[ds] = val
            op.dwaits.append((ds, val))

    def emit(self, engname, fn, reads=(), writes=(), dma=False):
        e = self.eng[engname]
        rec = _Rec()
        fn(rec)
        assert rec.call is not None
        op = Op(e, len(e.ops), rec.call)
        for b in reads:
            self._need(e, b.w, op)
            if b.excl:
                for t in b.r:
                    self._need(e, t, op, rr=True)
        for b in writes:
            self._need(e, b.w, op)
            for t in b.r:
                self._need(e, t, op)
        if dma:
            owner = writes[0] if writes else reads[0]
            if engname == "gpsimd":
                if owner.dsem_sw is None:
                    owner.dsem_sw = DmaSem()
                    self.dsems.append(owner.dsem_sw)
                ds = owner.dsem_sw
            else:
                if owner.dsem is None:
                    owner.dsem = DmaSem()
                    self.dsems.append(owner.dsem)
                ds = owner.dsem
            ds.total += 16
            op.dma_sem = ds
            tok = ("dma", ds)
        else:
            tok = ("op", op)
        e.ops.append(op)
        for b in reads:
            b.r.append(tok)
            if len(b.r) > 64:
                b.r = b.r[-64:] if False else b.r
        for b in writes:
            b.w = tok
            b.r = []
        return op

    def final_wait(self, engname, bufs):
        e = self.eng[engname]
        op = Op(e, len(e.ops), None)
        for b in bufs:
            self._need(e, b.w, op)
            for t in b.r:
                self._need(e, t, op)
        e.ops.append(op)

    def build(self, nc, stack):
        for n, e in self.eng.items():
            e.sem = stack.enter_context(nc.semaphore(f"s_{n}"))
            c = 0
            for op in e.ops:
                if op.signaled:
                    c += 1
                    op.semval = c
        for i, ds in enumerate(self.dsems):
            ds.handle = stack.enter_context(nc.semaphore(f"d{i}"))
        block = stack.enter_context(nc.Block())

        def run(e):
            def body(h):
                for op in e.ops:
                    for p in op.waits:
                        h.wait_ge(p.eng.sem, p.semval)
                    for ds, val in op.dwaits:
                        h.wait_ge(ds.handle, val)
                    if op.fn is None:
                        continue
                    name, a, kw = op.fn
                    inst = getattr(h, name)(*a, **kw)
                    if op.dma_sem is not None:
                        inst.then_inc(op.dma_sem.handle, 16)
                    elif op.signaled:
                        inst.then_inc(e.sem, 1)
            return body

        block.tensor(run(self.eng["tensor"]))
        block.vector(run(self.eng["vector"]))
        block.scalar(run(self.eng["scalar"]))
        block.gpsimd(run(self.eng["gpsimd"]))
        block.sync(run(self.eng["sync"]))


D = 1024
NT = 64
NOWN = 16
HD = 64
NCOL = 3864
C_ALL = 1792
O_A0, O_A1, O_A2, O_F, O_I = 0, 256, 512, 768, 1280
O_Q, O_ZA, O_QB, O_ZB, O_G = 1792, 2304, 2816, 3328, 3840
NPOOLROWS = 2560 * 128


def _perm_cols():
    off = {}
    o = 0
    for name, w in (('q_a', 512), ('k_cmp', 128), ('v_cmp', 128), ('k_slc', 128), ('v_slc', 128),
                    ('k_win', 128), ('v_win', 128), ('gate_a', 24), ('z_a', 512),
                    ('q_b', 512), ('f_b', 512), ('i_b', 512), ('z_b', 512)):
        off[name] = o
        o += w
    r = lambda n, a, b: list(range(off[n] + a, off[n] + b))
    cols = []
    cols += r('k_cmp', 0, 64) + r('v_cmp', 0, 64) + r('k_cmp', 64, 128) + r('v_cmp', 64, 128)
    cols += r('k_slc', 0, 128) + r('k_win', 0, 128)
    cols += r('v_slc', 0, 128) + r('v_win', 0, 128)
    cols += r('f_b', 0, 512) + r('i_b', 0, 512)
    cols += r('q_a', 0, 512) + r('z_a', 0, 512) + r('q_b', 0, 512) + r('z_b', 0, 512) + r('gate_a', 0, 24)
    assert len(cols) == NCOL
    return np.array(cols)


def _consts(j):
    c = {}
    pad = 3 - j
    slopes = 2.0 ** (-(np.arange(1, 9)))
    def kaug(padt, ntile):
        a = np.repeat(np.arange(ntile), 128).astype(np.float32)
        a[: padt * 128] = BIGA
        ki = np.tile(np.arange(128), ntile).astype(np.float32)
        one = np.ones_like(a)
        return np.stack([a, ki, one, one]).astype(bf16)
    c['kaugP'] = kaug(pad, 64)
    c['kaugS'] = kaug(0, 65)
    def caug(ninv):
        cc = np.arange(512)
        a = (cc // 8).astype(np.float32)
        a[:ninv] = BIGA
        b = (16 * (cc % 8) + 15).astype(np.float32)
        one = np.ones(512, np.float32)
        return np.stack([a, b, one, one]).astype(bf16)
    c['caugP'] = caug(8 * pad + 1)
    c['caugS'] = caug(1)
    qa = np.zeros((17, 4, 2, 4, 128), np.float32)
    for i in range(17):
        n = 4 * i + 3 if i < 16 else 64
        for g in range(2):
            for r in range(4):
                s = slopes[g * 4 + r]
                qa[i, 0, g, r, :] = s * 128
                qa[i, 1, g, r, :] = s
                qa[i, 2, g, r, :] = -s * 128 * n
                qa[i, 3, g, r, :] = -s * np.arange(128)
    c['qaug'] = qa.astype(bf16)
    ki = np.arange(128)[:, None]
    qi = np.arange(128)[None, :]
    rep = lambda m: m.astype(np.float32).astype(bf16)
    c['m_lo'] = rep(np.where(ki > qi, NEGM, 0.0))
    c['m_hi'] = rep(np.where(ki < qi, NEGM, 0.0))
    cm = np.zeros((4, 128, 128), bf16)
    for v in range(4):
        vis = (16 * (ki - 32 * v - 24) + 15) <= qi
        cm[v] = rep(np.where(vis, 0.0, NEGM))
    c['cmask'] = cm
    em = np.zeros((128, 32, 128), np.float32)
    p = np.arange(128)[:, None, None]
    jj = np.arange(32)[None, :, None]
    k = np.arange(128)[None, None, :]
    em[(p % 64) == (2 * jj + k // 64)] = 1.0
    cidx = np.arange(8192)
    e32 = (np.arange(32)[:, None] == (2 * ((cidx // 128) % 16) + (cidx % 128) // 64)[None, :])
    c['emat32'] = e32.astype(np.float32).astype(bf16)
    mm = np.zeros((512, 128), np.float32)
    wts = [1, 2, 2, 2, 1]
    for jb in range(128):
        for kk, wk in enumerate(wts):
            cc = 4 * jb + kk + 1
            if cc < 512:
                mm[cc, jb] += wk
    c['mmat'] = mm.reshape(4, 128, 128).transpose(1, 0, 2).copy().astype(bf16)
    bon = np.zeros((17, 128, 128), np.float32)
    blk = np.arange(128)[None, :]
    q = np.arange(128)[:, None]
    for i in range(16):
        n = 4 * i + 3
        jt = 2 * n + (q >= 64)
        b = np.zeros((128, 128), np.float32)
        b[np.broadcast_to(blk > jt, (128, 128))] = -100.0
        b[np.broadcast_to((blk == jt) | (blk == jt - 1), (128, 128))] = 1e6
        b[:, 2 * pad] = 1e6
        b[:, : 2 * pad] = -100.0
        bon[i] = b
    b = np.zeros((128, 128), np.float32)
    b[:, 0] = 1e6
    b[:, 127] = 1e6
    bon[16] = b
    c['bonus'] = bon
    c['ident_f'] = np.eye(128, dtype=np.float32)
    c['ident_b'] = np.eye(128, dtype=np.float32).astype(bf16)
    s = np.arange(128)[:, None]
    t = np.arange(128)[None, :]
    same = (s // 64) == (t // 64)
    c['tri'] = (same & (s <= t)).astype(np.float32)
    c['upr'] = (same & (s > t)).astype(np.float32)
    c['hmask'] = (same & (s <= t)).astype(np.float32)
    return c


CONST_SPECS = [
    ('kaugP', [4, 8192], BF16), ('kaugS', [4, 8320], BF16), ('caugP', [4, 512], BF16), ('caugS', [4, 512], BF16),
    ('qaug', [17, 4, 2, 4, 128], BF16), ('m_lo', [128, 128], BF16), ('m_hi', [128, 128], BF16),
    ('cmask', [4, 128, 128], BF16), ('emat32', [32, 8192], BF16), ('mmat', [128, 4, 128], BF16),
    ('bonus', [17, 128, 128], F32), ('ident_f', [128, 128], F32), ('ident_b', [128, 128], BF16),
    ('tri', [128, 128], F32), ('upr', [128, 128], F32), ('hmask', [128, 128], F32),
]

IN_SPECS = [
    ('xloc', [8192, D], F32), ('pown', [2048, 256], F32), ('xs', [4, D], F32), ('psm', [4, 256], F32),
    ('cache_cmp', [NPOOLROWS, 256], F32), ('cache_slc', [NPOOLROWS, 256], F32),
    ('cache_win', [4, 512, 256], F32), ('state0', [4, 4, 128, 128], F32), ('ptab', [4, 64], I32),
    ('w_in', [D, NCOL], F32), ('g_pre', [1, D], F32), ('cmp_pe', [2, 2048], F32), ('cmp_w1', [2, 2048, 128], F32),
    ('cmp_b1', [2, 128], F32), ('cmp_w2', [2, 128, 64], F32), ('hg_lower', [2, 512], F32), ('hg_norm', [1, 128], F32),
    ('w_out', [D, D], F32), ('g_post', [1, D], F32), ('ple_proj', [256, D], F32), ('ple_gate', [D, D], F32),
]

OUT_SPECS = [
    ('y_p', [2048, D], F32), ('y_s', [4, D], F32),
    ('cmp_p', [2048, 256], F32), ('slc_p', [2048, 256], F32), ('win_p', [128, 256], F32),
    ('hg_p', [4, 128, 128], F32),
    ('cmp_s', [4, 256], F32), ('slc_s', [4, 256], F32), ('win_s', [4, 512, 256], F32), ('hg_s', [4, 4, 128, 128], F32),
]


def build_nc(stage=99, n_tiles=NT):
    nc = bass.Bass("TRN2", target_bir_lowering=False)
    T = {}
    for name, shape, dt in IN_SPECS + CONST_SPECS:
        if stage < 4 and name in ('cache_cmp', 'cache_slc'):
            shape = [128, 256]
        T[name] = nc.dram_tensor(name, shape, dt, kind="ExternalInput").ap()
    for name, shape, dt in OUT_SPECS:
        T[name] = nc.dram_tensor(name, shape, dt, kind="ExternalOutput").ap()
    P = Prog()
    with contextlib.ExitStack() as st:
        st.enter_context(nc.allow_non_contiguous_dma(reason="tiny constant / layout loads"))

        def sb(name, shape, dt=F32):
            t = st.enter_context(nc.sbuf_tensor("sb_" + name, shape, dt))
            return t, P.buf(name)

        def psb(name, shape, dt=F32):
            t = st.enter_context(nc.psum_tensor("ps_" + name, shape, dt))
            b = P.buf(name)
            b.excl = True
            return t, b

        E = P.emit
        outbufs = []
        import os as _os
        dbg_i = int(_os.environ.get('KDBG', '-1'))
        cur = {'i': -2}
        dbg_b = P.buf("dbgout")
        outbufs.append(dbg_b)

        def tap(name, ap, buf, shape, dt=F32):
            if cur['i'] != dbg_i:
                return
            t = nc.dram_tensor("dbg_" + name, shape, dt, kind="ExternalOutput").ap()
            E("sync", lambda h: h.dma_start(out=t, in_=ap), reads=[buf], writes=[dbg_b], dma=True)

        ident_f, b_identf = sb("ident_f", [128, 128])
        ident_b, b_identb = sb("ident_b", [128, 128], BF16)
        tri, b_tri = sb("tri", [128, 128])
        upr, b_upr = sb("upr", [128, 128])
        hmask, b_hmask = sb("hmask", [128, 128])
        m_lo, b_mlo = sb("m_lo", [128, 128], BF16)
        m_hi, b_mhi = sb("m_hi", [128, 128], BF16)
        cmask, b_cmask = sb("cmask", [128, 4, 128], BF16)
        ones_f, b_onesf = sb("ones_f", [128, 128])
        eps_t, b_eps = sb("eps_t", [128, 1])
        for nm, t_, b_ in (("ident_f", ident_f, b_identf), ("ident_b", ident_b, b_identb), ("tri", tri, b_tri),
                           ("upr", upr, b_upr), ("hmask", hmask, b_hmask), ("m_lo", m_lo, b_mlo),
                           ("m_hi", m_hi, b_mhi)):
            E("sync", lambda h, t_=t_, nm=nm: h.dma_start(out=t_[:], in_=T[nm]), writes=[b_], dma=True)
        E("sync", lambda h: h.dma_start(out=cmask[:], in_=T['cmask'].rearrange("v p n -> p v n")), writes=[b_cmask], dma=True)
        E("vector", lambda h: h.memset(ones_f[:], 1.0), writes=[b_onesf])
        E("vector", lambda h: h.memset(eps_t[:], EPS), writes=[b_eps])

        stage_t = []
        for i in range(2):
            stage_t.append(sb(f"stage{i}", [128, 1024]))
        stage_ctr = [0]

        def next_stage():
            s = stage_t[stage_ctr[0] % 2]
            stage_ctr[0] += 1
            return s

        cast_ctr = [0]

        def cast_eng():
            cast_ctr[0] += 1
            return "vector" if cast_ctr[0] % 2 else "gpsimd"

        w_in_b, b_win = sb("w_in_b", [128, 8, NCOL], BF16)
        wch = [sb(f"wch{i}", [128, D], BF16) for i in range(3)]
        wch_ctr = [0]
        gcol, b_gcol = sb("gcol", [128, 8])
        E("sync", lambda h: h.dma_start(out=gcol[:], in_=T['g_pre'].rearrange("o (k p) -> p (o k)", p=128)), writes=[b_gcol], dma=True)
        w1_b, b_w1 = sb("w1_b", [128, 32, 128], BF16)
        w2_b, b_w2 = sb("w2_b", [128, 2, 64], BF16)
        w2_f, b_w2f = sb("w2_f", [128, 2, 64])

        def load_cast(dst, bdst, dst_sl, src_ap, width, scal=None):
            s_t, s_b = next_stage()
            E("sync", lambda h: h.dma_start(out=s_t[:, 0:width], in_=src_ap), writes=[s_b], dma=True)
            if scal is None:
                E(cast_eng(), lambda h: h.tensor_copy(out=dst_sl, in_=s_t[:, 0:width]), reads=[s_b], writes=[bdst])
            else:
                E("vector", lambda h: h.tensor_scalar(out=dst_sl, in0=s_t[:, 0:width], scalar1=scal, scalar2=None, op0=ALU.mult),
                  reads=[s_b, b_gcol], writes=[bdst])

        wscr = nc.dram_tensor("wscr", [18, 128, 1024], BF16, kind="Internal").ap()
        b_wscr = P.buf("wscr")
        wsrc = [T['w_out'][kc * 128:(kc + 1) * 128, :] for kc in range(8)] + \
               [T['ple_gate'][kc * 128:(kc + 1) * 128, :] for kc in range(8)] + \
               [T['ple_proj'][kc * 128:(kc + 1) * 128, :] for kc in range(2)]

        def stream_w(ci):
            w_t, w_b = wch[wch_ctr[0] % 3]
            wch_ctr[0] += 1
            E("sync", lambda h: h.dma_start(out=w_t[:, :], in_=wscr[ci]), reads=[b_wscr], writes=[w_b], dma=True)
            return w_t, w_b

        def init_wscr():
            for ci in range(18):
                w_t, w_b = wch[wch_ctr[0] % 3]
                wch_ctr[0] += 1
                load_cast(w_t, w_b, w_t[:, :], wsrc[ci], 1024)
                E("sync", lambda h: h.dma_start(out=wscr[ci], in_=w_t[:, :]), reads=[w_b], writes=[b_wscr], dma=True)

        for kc in range(8):
            for c0 in range(0, NCOL, 1024):
                wd = min(1024, NCOL - c0)
                load_cast(w_in_b, b_win, w_in_b[:, kc, c0:c0 + wd], T['w_in'][kc * 128:(kc + 1) * 128, c0:c0 + wd], wd, scal=gcol[:, kc:kc + 1])
        init_wscr()
        for kv in range(2):
            E("sync", lambda h, kv=kv: h.dma_start(out=w2_f[:, kv, :], in_=T['cmp_w2'][kv]), writes=[b_w2f], dma=True)
        E("vector", lambda h: h.tensor_copy(out=w2_b[:], in_=w2_f[:]), reads=[b_w2f], writes=[b_w2])

        g_post_bc, b_gpost = sb("g_post_bc", [128, D])
        lb_bc, b_lb = sb("lb_bc", [128, 512])
        hgn_bc, b_hgn = sb("hgn_bc", [128, 128])
        Wbig, _ = sb("Wbig", [128, 8, 512])
        Wflat = Wbig[:, :, :].rearrange("p a n -> p (a n)")
        b_sgt = P.buf("W01"); b_junk = P.buf("W23"); b_h1 = P.buf("W45"); b_w67 = P.buf("W67")
        hl1, b_hl1 = Wbig[:, 0, :], b_sgt
        E("sync", lambda h: h.dma_start(out=g_post_bc[:], in_=T['g_post'].partition_broadcast(128)), writes=[b_gpost], dma=True)
        E("sync", lambda h: h.dma_start(out=hgn_bc[:], in_=T['hg_norm'].partition_broadcast(128)), writes=[b_hgn], dma=True)
        E("sync", lambda h: h.dma_start(out=lb_bc[:], in_=T['hg_lower'][0:1, :].partition_broadcast(128)), writes=[b_lb], dma=True)
        E("sync", lambda h: h.dma_start(out=hl1, in_=T['hg_lower'][1:2, :].partition_broadcast(128)), writes=[b_hl1], dma=True)
        E("vector", lambda h: h.tensor_sub(out=lb_bc[:], in0=hl1, in1=lb_bc[:]), reads=[b_hl1, b_lb], writes=[b_lb])
        E("scalar", lambda h: h.activation(out=lb_bc[:], in_=lb_bc[:], func=AF.Exp), reads=[b_lb], writes=[b_lb])
        E("vector", lambda h: h.tensor_scalar_add(out=lb_bc[:], in0=lb_bc[:], scalar1=1.0), reads=[b_lb], writes=[b_lb])
        E("vector", lambda h: h.reciprocal(out=lb_bc[:], in_=lb_bc[:]), reads=[b_lb], writes=[b_lb])

        pool_banks = [psb(f"pp{i}", [128, 512]) for i in range(5)]
        po_banks = [psb(f"po{i}", [128, 512]) for i in range(3)]
        pool_ctr = [0]

        def bank():
            b = pool_banks[pool_ctr[0] % 5]
            pool_ctr[0] += 1
            return b

        pe_t, b_pe = sb("pe_t", [128, 32])
        b1e, b_b1e = sb("b1e", [128, 2])
        b1raw, b_b1raw = sb("b1raw", [128, 2])
        for kv in range(2):
            E("sync", lambda h, kv=kv: h.dma_start(out=pe_t[kv * 64:(kv + 1) * 64, :],
                                                   in_=T['cmp_pe'][kv].rearrange("(r d) -> d r", d=64)),
              writes=[b_pe], dma=True)
        E("sync", lambda h: h.dma_start(out=b1raw[:], in_=T['cmp_b1'].rearrange("k m -> m k")), writes=[b_b1raw], dma=True)
        pbs = [bank(), bank()]
        for rc in range(4):
            s_t, s_b = next_stage()
            for kv in range(2):
                E("sync", lambda h, kv=kv, rc=rc, s_t=s_t: h.dma_start(
                    out=s_t[kv * 64:(kv + 1) * 64, :].rearrange("p (r m) -> p r m", r=8),
                    in_=T['cmp_w1'][kv, rc * 512:(rc + 1) * 512, :].rearrange("(r d) m -> d r m", d=64)), writes=[s_b], dma=True)
            E("vector", lambda h, rc=rc, s_t=s_t: h.tensor_copy(out=w1_b[:, rc * 8:(rc + 1) * 8, :], in_=s_t[:, :].rearrange("p (r m) -> p r m", r=8)),
              reads=[s_b], writes=[b_w1])
            for kv in range(2):
                for r8 in range(8):
                    r = rc * 8 + r8
                    E("tensor", lambda h, kv=kv, r=r, r8=r8, s_t=s_t: h.matmul(pbs[kv][0][:, 0:1], lhsT=s_t[kv * 64:(kv + 1) * 64, r8 * 128:(r8 + 1) * 128],
                                                                rhs=pe_t[kv * 64:(kv + 1) * 64, r:r + 1],
                                                                start=(r == 0), stop=(r == 31)),
                      reads=[s_b, b_pe], writes=[pbs[kv][1]])
        for kv in range(2):
            E("vector", lambda h: h.tensor_add(out=b1e[:, kv:kv + 1], in0=pbs[kv][0][:, 0:1], in1=b1raw[:, kv:kv + 1]), reads=[pbs[kv][1], b_b1raw], writes=[b_b1e])

        kslcT, b_kslcT = sb("kslcT", [100, 2, 8192], BF16)
        vslc, b_vslc = sb("vslc", [128, 64, 2, 65], BF16)
        kwinT, b_kwinT = sb("kwinT", [100, 8, 2, 128], BF16)
        vwin, b_vwin = sb("vwin", [128, 8, 2, 65], BF16)
        knewT, b_knewT = sb("knewT", [100, 2, 2, 1], BF16)
        vnew, b_vnew = sb("vnew", [1, 2, 2, 65], BF16)
        kc_raw, b_kcraw = sb("kc_raw", [128, 2, 528], BF16)
        kcT, b_kcT = sb("kcT", [100, 2, 512], BF16)
        hv_all, b_hvall = sb("hv_all", [128, 2, 512], BF16)
        vm, b_vm = sb("vm", [128, 4, 2, 193], BF16)
        S_t, b_S = sb("S_state", [128, 4, 128])
        E("gpsimd", lambda h: h.memset(vslc[:], 1.0), writes=[b_vslc])
        E("gpsimd", lambda h: h.memset(vwin[:], 1.0), writes=[b_vwin])
        E("gpsimd", lambda h: h.memset(vnew[:], 1.0), writes=[b_vnew])
        E("gpsimd", lambda h: h.memset(vm[:], 1.0), writes=[b_vm])
        E("gpsimd", lambda h: h.memset(hv_all[:], 0.0), writes=[b_hvall])
        E("gpsimd", lambda h: h.memset(kc_raw[:], 0.0), writes=[b_kcraw])
        E("gpsimd", lambda h: h.memset(kcT[:], 0.0), writes=[b_kcT])
        E("gpsimd", lambda h: h.memset(kwinT[:], 0.0), writes=[b_kwinT])
        E("gpsimd", lambda h: h.memset(knewT[:], 0.0), writes=[b_knewT])
        for g in range(2):
            E("sync", lambda h: h.dma_start(out=kslcT[64:96, g, :], in_=T['emat32']), writes=[b_kslcT], dma=True)
        for g in range(2):
            E("sync", lambda h, g=g: h.dma_start(out=vm[:, :, g, 65:193], in_=T['mmat']), writes=[b_vm], dma=True)

        xn_b, b_xn = sb("xn_b", [128, D], BF16)
        xnT, b_xnT = sb("xnT", [128, 8, 128], BF16)
        ssq, b_ssq = sb("ssq", [128, 4])
        junk = Wflat[:, 1024:2048]
        zt_b, b_ztb = sb("zt_b", [128, 512], BF16)
        kv_f, b_kvf = Wflat[:, 3072:3840], b_w67
        hw = {"e1": (Wbig[:, 0, :], b_sgt), "f": (Wbig[:, 1, :], b_sgt), "logf": (Wbig[:, 2, :], b_junk), "kk": (Wbig[:, 3, :], b_junk),
              "eD": (Wbig[:, 4, :], b_h1), "qs": (Wbig[:, 5, :], b_h1)}
        Kd_b, b_Kd = sb("Kd_b", [128, 512], BF16)
        v_b, b_vb = sb("v_b", [128, 512], BF16)
        e1buf = [sb(f"e1buf{i}", [128, 512]) for i in range(2)]
        vbuf = [(v_b, b_vb), sb("v_b1", [128, 512], BF16)]
        hgc = {}
        dec_t, b_dec = sb("dec_t", [128, 8])
        S0_b, b_S0b = sb("S0_b", [128, 4, 128], BF16)
        S1_b, b_S1b = sb("S1_b", [128, 4, 128], BF16)
        Qt_b, b_Qtb = zt_b, b_ztb
        QTh, b_QTh = sb("QTh", [128, 4, 3, 128], BF16)
        KTh, b_KTh = sb("KTh", [128, 4, 128], BF16)
        AT_b, b_ATb = sb("AT_b", [128, 4, 128], BF16)
        Kt_b, b_Ktb = AT_b[:, :, :].rearrange("p a t -> p (a t)"), b_ATb
        E("gpsimd", lambda h: h.memset(QTh[:], 0.0), writes=[b_QTh])
        qa_b, b_qab = sb("qa_b", [128, 512], BF16)
        za_f, b_zaf = Wbig[:, 6, :], b_w67
        zb_f, b_zbf = Wbig[:, 7, :], b_w67
        gate_f, b_gatef = sb("gate_f", [128, 24])
        QT, b_QT = sb("QTq", [100, 4, 512], BF16)
        E("gpsimd", lambda h: h.memset(QT[:], 0.0), writes=[b_QT])
        nm_pad, b_nmp = sb("nm_pad", [128, 4, 96], BF16)
        E("gpsimd", lambda h: h.memset(nm_pad[:], 0.0), writes=[b_nmp])
        PT = [sb(f"PT{i}", [128, 512], BF16) for i in range(2)]
        pt_ctr = [0]
        acc, b_acc = Wbig[:, 2, :].rearrange("p (a d) -> p a d", a=8), b_junk
        rden, b_rden = sb("rden", [128, 8])
        ps_t, b_pst = Wbig[:, 3, 0:128], b_junk
        sc_t, b_sct = Wbig[:, 3, 128:256], b_junk
        bon_t, b_bont = Wbig[:, 3, 256:384], b_junk
        mx8, b_mx8 = sb("mx8", [128, 16])
        Y_b, b_Yb = xn_b, b_xn
        YT, b_YT = xnT, b_xnT
        h1 = Wflat[:, 2048:3072]
        p_b, b_pb = qa_b[:, 0:256], b_qab
        pT_b, b_pTb = AT_b[:, 0:2, :], b_ATb
        sg_t = Wflat[:, 0:1024]

        def rmsnorm_T(x_t, x_b, TW):
            E("vector", lambda h: h.scalar_tensor_tensor(out=junk[0:TW, :], in0=x_t[0:TW, :], scalar=1.0, in1=x_t[0:TW, :], op0=ALU.mult, op1=ALU.mult,
                                                         accum_out=ssq[0:TW, 0:1]),
              reads=[x_b], writes=[b_junk, b_ssq])
            k2 = int(_os.environ.get('KCUT2', '99'))
            if k2 < -2:
                return
            E("scalar", lambda h: h.activation(out=ssq[0:TW, 0:1], in_=ssq[0:TW, 0:1], func=AF.Ln, scale=1.0 / D, bias=eps_t[0:TW, :]),
              reads=[b_ssq, b_eps], writes=[b_ssq])
            if k2 < -1:
                return
            E("scalar", lambda h: h.activation(out=ssq[0:TW, 0:1], in_=ssq[0:TW, 0:1], func=AF.Exp, scale=-0.5), reads=[b_ssq], writes=[b_ssq])
            if k2 < 0:
                return
            E("vector", lambda h: h.tensor_scalar(out=xn_b[0:TW, :], in0=x_t[0:TW, :], scalar1=ssq[0:TW, 0:1], scalar2=None, op0=ALU.mult),
              reads=[x_b, b_ssq], writes=[b_xn])
            if int(_os.environ.get('KCUT2', '99')) < 1:
                return
            transpose_k(xn_b, b_xn, xnT, b_xnT, TW, 8)

        def transpose_k(src, src_b, dst, dst_b, TW, nk):
            for k0 in range(0, nk, 4):
                kn = min(4, nk - k0)
                p_t, p_b_ = bank()
                for kk_ in range(kn):
                    kc = k0 + kk_
                    E("tensor", lambda h, kc=kc, kk_=kk_, p_t=p_t: h.matmul(p_t[:, kk_ * TW:(kk_ + 1) * TW], lhsT=src[0:TW, kc * 128:(kc + 1) * 128],
                                                                        rhs=ident_b[0:TW, 0:TW], start=True, stop=True),
                      reads=[src_b, b_identb], writes=[p_b_])
                eng = "vector" if (k0 // 4) % 2 == 0 else "scalar"
                if eng == "vector":
                    E("vector", lambda h, k0=k0, kn=kn, p_t=p_t: h.tensor_copy(out=dst[:, k0:k0 + kn, 0:TW], in_=p_t[:, 0:kn * TW].rearrange("p (k t) -> p k t", k=kn)),
                      reads=[p_b_], writes=[dst_b])
                else:
                    E("scalar", lambda h, k0=k0, kn=kn, p_t=p_t: h.activation(out=dst[:, k0:k0 + kn, 0:TW], in_=p_t[:, 0:kn * TW].rearrange("p (k t) -> p k t", k=kn), func=AF.Copy),
                      reads=[p_b_], writes=[dst_b])

        def proj(c0, width, TW):
            p_t, p_b_ = bank()
            for kc in range(8):
                E("tensor", lambda h, kc=kc: h.matmul(p_t[0:TW, 0:width], lhsT=xnT[:, kc, 0:TW], rhs=w_in_b[:, kc, c0:c0 + width],
                                                       start=(kc == 0), stop=(kc == 7)),
                  reads=[b_xnT, b_win], writes=[p_b_])
            return p_t, p_b_

        def transp(dst_ps, dst_b, src_ap, src_b, KW, M):
            E("tensor", lambda h: h.matmul(dst_ps, lhsT=src_ap, rhs=ident_b[0:KW, 0:KW], start=True, stop=True),
              reads=[src_b, b_identb], writes=[dst_b])

        def kv_from_tokmajor(TW, kslc_dst, kwin_dst, vslc_dst, vwin_dst, kcraw_off, own, kvout):
            p1, p1b = proj(O_A0, 512, TW)
            E("scalar", lambda h: h.activation(out=zt_b[0:TW, :], in_=p1[0:TW, :], func=AF.Copy), reads=[p1b], writes=[b_ztb])
            if own:
                E("vector", lambda h: h.tensor_copy(out=kv_f[0:TW, 0:512], in_=p1[0:TW, :]), reads=[p1b], writes=[b_kvf])
            k3 = int(_os.environ.get('KCUT3', '99'))
            if k3 < 1:
                return
            p2, p2b = proj(O_A2, 256, TW)
            E("vector", lambda h: h.tensor_copy(out=vslc_dst, in_=p2[0:TW, 0:128].rearrange("p (g d) -> p g d", g=2)),
              reads=[p2b], writes=[kv_from_tokmajor.vslc_b])
            E("vector", lambda h: h.tensor_copy(out=vwin_dst, in_=p2[0:TW, 128:256].rearrange("p (g d) -> p g d", g=2)),
              reads=[p2b], writes=[kv_from_tokmajor.vwin_b])
            if own:
                E("scalar", lambda h: h.activation(out=kv_f[0:TW, 512:768], in_=p2[0:TW, 0:256], func=AF.Copy), reads=[p2b], writes=[b_kvf])
            if k3 < 2:
                return
            pt_, ptb = bank()
            for q4 in range(4):
                transp(pt_[0:64, q4 * TW:(q4 + 1) * TW], ptb, zt_b[0:TW, 256 + q4 * 64:256 + (q4 + 1) * 64], b_ztb, TW, 64)
            E("vector", lambda h: h.tensor_copy(out=kslc_dst, in_=pt_[0:64, 0:2 * TW].rearrange("p (g t) -> p g t", g=2)),
              reads=[ptb], writes=[kv_from_tokmajor.kslc_b])
            E("scalar", lambda h: h.activation(out=kwin_dst, in_=pt_[0:64, 2 * TW:4 * TW].rearrange("p (g t) -> p g t", g=2), func=AF.Copy),
              reads=[ptb], writes=[kv_from_tokmajor.kwin_b])
            if k3 < 3:
                return
            if kcraw_off is not None:
                pc_, pcb = bank()
                for g in range(2):
                    transp(pc_[:, g * TW:(g + 1) * TW], pcb, zt_b[0:TW, g * 128:(g + 1) * 128], b_ztb, TW, 128)
                E("vector", lambda h: h.tensor_copy(out=kc_raw[:, :, kcraw_off:kcraw_off + TW],
                                                    in_=pc_[:, 0:2 * TW].rearrange("p (g t) -> p g t", g=2)),
                  reads=[pcb], writes=[b_kcraw])
            if own and kvout is not None:
                cmp_o, slc_o, win_o = kvout
                for s_ in range(2):
                    for g_ in range(2):
                        E("sync", lambda h, s_=s_, g_=g_: h.dma_start(out=cmp_o[:, (s_ * 2 + g_) * 64:(s_ * 2 + g_ + 1) * 64],
                                                                      in_=kv_f[0:TW, (g_ * 2 + s_) * 64:(g_ * 2 + s_ + 1) * 64]),
                          reads=[b_kvf], writes=[kv_from_tokmajor.ob], dma=True)
                E("sync", lambda h: h.dma_start(out=slc_o[:, 0:128], in_=kv_f[0:TW, 256:384]), reads=[b_kvf], writes=[kv_from_tokmajor.ob], dma=True)
                E("sync", lambda h: h.dma_start(out=slc_o[:, 128:256], in_=kv_f[0:TW, 512:640]), reads=[b_kvf], writes=[kv_from_tokmajor.ob], dma=True)
                if win_o is not None:
                    E("sync", lambda h: h.dma_start(out=win_o[:, 0:128], in_=kv_f[0:TW, 384:512]), reads=[b_kvf], writes=[kv_from_tokmajor.ob], dma=True)
                    E("sync", lambda h: h.dma_start(out=win_o[:, 128:256], in_=kv_f[0:TW, 640:768]), reads=[b_kvf], writes=[kv_from_tokmajor.ob], dma=True)

        kv_from_tokmajor.vslc_b = b_vslc
        kv_from_tokmajor.vwin_b = b_vwin
        kv_from_tokmajor.kslc_b = b_kslcT
        kv_from_tokmajor.kwin_b = b_kwinT
        kv_from_tokmajor.ob = P.buf("kvout")
        outbufs.append(kv_from_tokmajor.ob)

        def sigmoid_from(dst, dst_b, src, src_b, TW, W, eng2="vector"):
            E("scalar", lambda h: h.activation(out=dst, in_=src, func=AF.Sigmoid), reads=[src_b], writes=[dst_b])

        def compress_group(gi):
            phs = [bank(), bank()]
            for g in range(2):
                for kv in range(2):
                    col = (g * 2 + kv) * 32
                    ph, phb = phs[kv]
                    for r in range(32):
                        E("tensor", lambda h, g=g, kv=kv, r=r, col=col: h.matmul(
                            ph[:, col:col + 32], lhsT=w1_b[kv * 64:(kv + 1) * 64, r, :],
                            rhs=kc_raw[kv * 64:(kv + 1) * 64, g, r:r + 497:16], start=(r == 0), stop=(r == 31)),
                          reads=[b_w1, b_kcraw], writes=[phb])
            u_t, u_b = hw["e1"]
            s_t, s_b = hw["f"]
            H_t, H_b = Kd_b, b_Kd
            for kv in range(2):
                ph, phb = phs[kv]
                for g in range(2):
                    col = (g * 2 + kv) * 32
                    E("vector", lambda h, col=col, kv=kv: h.tensor_scalar(out=u_t[:, col:col + 32], in0=ph[:, col:col + 32],
                                                                        scalar1=b1e[:, kv:kv + 1], scalar2=None, op0=ALU.add),
                      reads=[phb, b_b1e], writes=[u_b])
            sigmoid_from(s_t[:, 0:128], s_b, u_t[:, 0:128], u_b, 128, 128)
            E("vector", lambda h: h.tensor_mul(out=H_t[:, 0:128], in0=u_t[:, 0:128], in1=s_t[:, 0:128]), reads=[u_b, s_b], writes=[H_b])
            pk, pkb = bank()
            for g in range(2):
                col = (g * 2 + 0) * 32
                E("tensor", lambda h, g=g, col=col: h.matmul(pk[0:64, g * 32:(g + 1) * 32], lhsT=w2_b[:, 0, :], rhs=H_t[:, col:col + 32],
                                                             start=True, stop=True), reads=[b_w2, H_b], writes=[pkb])
            E("vector", lambda h: h.tensor_copy(out=kcT[0:64, :, 32 * gi:32 * gi + 32], in_=pk[0:64, 0:64].rearrange("p (g c) -> p g c", g=2)),
              reads=[pkb], writes=[b_kcT])
            for g in range(2):
                col = (g * 2 + 1) * 32
                E("gpsimd", lambda h, g=g, col=col: h.tensor_copy(out=hv_all[:, g, 32 * gi:32 * gi + 32], in_=H_t[:, col:col + 32]),
                  reads=[H_b], writes=[b_hvall])
            E("vector", lambda h: h.tensor_copy(out=kc_raw[:, :, 0:16], in_=kc_raw[:, :, 512:528]), reads=[b_kcraw], writes=[b_kcraw])

        def vc_tile(ktc):
            pv, pvb = bank()
            for g in range(2):
                E("tensor", lambda h, g=g: h.matmul(pv[:, g * 64:(g + 1) * 64], lhsT=hv_all[:, g, ktc * 128:(ktc + 1) * 128], rhs=w2_b[:, 1, :],
                                                    start=True, stop=True), reads=[b_hvall, b_w2], writes=[pvb])
            E("vector", lambda h: h.tensor_copy(out=vm[:, ktc, :, 0:64], in_=pv[:, 0:128].rearrange("p (g d) -> p g d", g=2)),
              reads=[pvb], writes=[b_vm])

        def hgrn_tile(TW, own, e1b, vb):
            e1, be1 = e1b
            v_b, b_vb = vb
            hgc['e1'] = e1b
            hgc['vb'] = vb
            f_t, bf_ = hw["f"]
            lg, blg = hw["logf"]
            kk, bkk = hw["kk"]
            eD, beD = hw["eD"]
            nch = 2 if TW == 128 else 1
            CW = 64 if TW == 128 else 1
            E("vector", lambda h: h.tensor_scalar(out=f_t[0:TW, :], in0=e1[0:TW, :], scalar1=-1.0, scalar2=1.0, op0=ALU.mult, op1=ALU.add),
              reads=[be1], writes=[bf_])
            E("vector", lambda h: h.tensor_mul(out=f_t[0:TW, :], in0=f_t[0:TW, :], in1=lb_bc[0:TW, :]), reads=[bf_, b_lb], writes=[bf_])
            E("vector", lambda h: h.tensor_add(out=f_t[0:TW, :], in0=f_t[0:TW, :], in1=e1[0:TW, :]), reads=[bf_, be1], writes=[bf_])
            E("scalar", lambda h: h.activation(out=lg[0:TW, :], in_=f_t[0:TW, :], func=AF.Ln), reads=[bf_], writes=[blg])
            E("gpsimd", lambda h: h.tensor_scalar(out=kk[0:TW, :], in0=f_t[0:TW, :], scalar1=-1.0, scalar2=1.0, op0=ALU.mult, op1=ALU.add),
              reads=[bf_], writes=[bkk])
            pD, pDb = bank()
            E("tensor", lambda h: h.matmul(pD[0:TW, :], lhsT=upr[0:TW, 0:TW], rhs=lg[0:TW, :], start=True, stop=True),
              reads=[b_upr, blg], writes=[pDb])
            E("scalar", lambda h: h.activation(out=eD[0:TW, :], in_=pD[0:TW, :], func=AF.Exp), reads=[pDb], writes=[beD])
            E("vector", lambda h: h.tensor_mul(out=Kd_b[0:TW, :], in0=kk[0:TW, :], in1=eD[0:TW, :]), reads=[bkk, beD], writes=[b_Kd])
            for c in range(nch):
                pd, pdb = bank()
                for hh in range(4):
                    E("tensor", lambda h, c=c, hh=hh: h.matmul(pd[:, hh:hh + 1], lhsT=lg[c * CW:(c + 1) * CW, hh * 128:(hh + 1) * 128],
                                                                 rhs=ones_f[c * CW:(c + 1) * CW, 0:1], start=True, stop=True),
                      reads=[blg, b_onesf], writes=[pdb])
                E("scalar", lambda h: h.activation(out=dec_t[:, 4 * c:4 * c + 4], in_=pd[:, 0:4], func=AF.Exp), reads=[pdb], writes=[b_dec])
            if own:
                E("gpsimd", lambda h: h.tensor_copy(out=S0_b[:], in_=S_t[:]), reads=[b_S], writes=[b_S0b])
            for c in range(nch):
                pu, pub = bank()
                for hh in range(4):
                    E("tensor", lambda h, c=c, hh=hh: h.matmul(pu[:, hh * 128:(hh + 1) * 128], lhsT=Kd_b[c * CW:(c + 1) * CW, hh * 128:(hh + 1) * 128],
                                                                 rhs=v_b[c * CW:(c + 1) * CW, hh * 128:(hh + 1) * 128], start=True, stop=True),
                      reads=[b_Kd, b_vb], writes=[pub])
                E("vector", lambda h, c=c: h.tensor_mul(out=S_t[:], in0=S_t[:], in1=dec_t[:, c * 4:(c + 1) * 4].unsqueeze(2).to_broadcast([128, 4, 128])),
                  reads=[b_S, b_dec], writes=[b_S])
                E("vector", lambda h: h.tensor_add(out=S_t[:], in0=S_t[:], in1=pu[:, :].rearrange("p (a v) -> p a v", a=4)),
                  reads=[b_S, pub], writes=[b_S])
                if own and c == 0 and nch == 2:
                    E("gpsimd", lambda h: h.tensor_copy(out=S1_b[:], in_=S_t[:]), reads=[b_S], writes=[b_S1b])
            return None

        def hgrn_out(TW):
            e1, be1 = hgc['e1']
            v_b, b_vb = hgc['vb']
            lg, blg = hw["logf"]
            kk, bkk = hw["kk"]
            eD, beD = hw["eD"]
            nch = 2 if TW == 128 else 1
            CW = 64 if TW == 128 else 1
            qs, bqs = hw["qs"]
            pB, pBb = bank()
            E("tensor", lambda h: h.matmul(pB[0:TW, :], lhsT=tri[0:TW, 0:TW], rhs=lg[0:TW, :], start=True, stop=True),
              reads=[b_tri, blg], writes=[pBb])
            E("scalar", lambda h: h.activation(out=eD[0:TW, :], in_=pB[0:TW, :], func=AF.Exp), reads=[pBb], writes=[beD])
            E("vector", lambda h: h.tensor_mul(out=Qt_b[0:TW, :], in0=qs[0:TW, :], in1=eD[0:TW, :]), reads=[bqs, beD], writes=[b_Qtb])
            E("scalar", lambda h: h.activation(out=e1[0:TW, :], in_=pB[0:TW, :], func=AF.Exp, scale=-1.0), reads=[pBb], writes=[be1])
            E("vector", lambda h: h.tensor_mul(out=Kt_b[0:TW, :], in0=kk[0:TW, :], in1=e1[0:TW, :]), reads=[bkk, be1], writes=[b_Ktb])
            pq_, pqb_ = bank()
            pk_, pkb_ = bank()
            for hh in range(4):
                transp(pq_[:, hh * TW:(hh + 1) * TW], pqb_, Qt_b[0:TW, hh * 128:(hh + 1) * 128], b_Qtb, TW, 128)
                transp(pk_[:, hh * TW:(hh + 1) * TW], pkb_, Kt_b[0:TW, hh * 128:(hh + 1) * 128], b_Ktb, TW, 128)
            E("vector", lambda h: h.tensor_copy(out=QTh[:, :, 0, 0:TW], in_=pq_[:, 0:4 * TW].rearrange("p (a t) -> p a t", a=4)),
              reads=[pqb_], writes=[b_QTh])
            E("gpsimd", lambda h: h.tensor_copy(out=QTh[:, :, 1, 0:CW], in_=QTh[:, :, 0, 0:CW]), reads=[b_QTh], writes=[b_QTh])
            if nch == 2:
                E("gpsimd", lambda h: h.tensor_copy(out=QTh[:, :, 2, 64:128], in_=QTh[:, :, 0, 64:128]), reads=[b_QTh], writes=[b_QTh])
            E("scalar", lambda h: h.activation(out=KTh[:, :, 0:TW], in_=pk_[:, 0:4 * TW].rearrange("p (a t) -> p a t", a=4), func=AF.Copy),
              reads=[pkb_], writes=[b_KTh])
            pa, pab = bank()
            for hh in range(4):
                E("tensor", lambda h, hh=hh: h.matmul(pa[0:TW, hh * TW:(hh + 1) * TW], lhsT=KTh[:, hh, 0:TW], rhs=QTh[:, hh, 0, 0:TW], start=True, stop=True),
                  reads=[b_KTh, b_QTh], writes=[pab])
            E("vector", lambda h: h.tensor_mul(out=AT_b[0:TW, :, 0:TW], in0=pa[0:TW, 0:4 * TW].rearrange("p (a t) -> p a t", a=4),
                                               in1=hmask[0:TW, 0:TW].unsqueeze(1).to_broadcast([TW, 4, TW])),
              reads=[pab, b_hmask], writes=[b_ATb])
            po_, pob = bank()
            for hh in range(4):
                E("tensor", lambda h, hh=hh: h.matmul(po_[0:TW, hh * 128:(hh + 1) * 128], lhsT=AT_b[0:TW, hh, 0:TW], rhs=v_b[0:TW, hh * 128:(hh + 1) * 128],
                                                      start=True, stop=False), reads=[b_ATb, b_vb], writes=[pob])
                E("tensor", lambda h, hh=hh: h.matmul(po_[0:TW, hh * 128:(hh + 1) * 128], lhsT=QTh[:, hh, 1, 0:TW], rhs=S0_b[:, hh, :],
                                                      start=False, stop=(nch == 1)), reads=[b_QTh, b_S0b], writes=[pob])
                if nch == 2:
                    E("tensor", lambda h, hh=hh: h.matmul(po_[0:TW, hh * 128:(hh + 1) * 128], lhsT=QTh[:, hh, 2, 0:TW], rhs=S1_b[:, hh, :],
                                                          start=False, stop=True), reads=[b_QTh, b_S1b], writes=[pob])
            if cur['i'] == dbg_i:
                E("vector", lambda h: h.tensor_copy(out=qs[0:TW, :], in_=po_[0:TW, :]), reads=[pob], writes=[bqs])
                tap("ob", qs[0:TW, :], bqs, [128, 512])
            tap("S", S_t[:], b_S, [128, 4, 128])
            tap("Qt", Qt_b[0:TW, :], b_Qtb, [128, 512], BF16)
            tap("Kt", Kt_b[0:TW, :], b_Ktb, [128, 512], BF16)
            tap("AT", AT_b[0:TW, :, :], b_ATb, [128, 4, 128], BF16)
            tap("lg", lg[0:TW, :], blg, [128, 512])
            for hh in range(4):
                E("scalar", lambda h: h.activation(out=eD[0:TW, hh * 128:(hh + 1) * 128], in_=po_[0:TW, hh * 128:(hh + 1) * 128], func=AF.Square,
                                                   accum_out=ssq[0:TW, hh:hh + 1]), reads=[pob], writes=[beD, b_ssq])
            E("scalar", lambda h: h.activation(out=ssq[0:TW, 0:4], in_=ssq[0:TW, 0:4], func=AF.Sqrt, scale=1.0 / 128, bias=eps_t[0:TW, :]),
              reads=[b_ssq, b_eps], writes=[b_ssq])
            E("vector", lambda h: h.reciprocal(out=ssq[0:TW, 0:4], in_=ssq[0:TW, 0:4]), reads=[b_ssq], writes=[b_ssq])
            for hh in range(4):
                E("vector", lambda h: h.scalar_tensor_tensor(out=qs[0:TW, hh * 128:(hh + 1) * 128], in0=po_[0:TW, hh * 128:(hh + 1) * 128],
                                                             scalar=ssq[0:TW, hh:hh + 1], in1=hgn_bc[0:TW, :], op0=ALU.mult, op1=ALU.mult),
                  reads=[pob, b_ssq, b_hgn], writes=[bqs])
            return qs, bqs

        def attn_steps(TW, g, steps, out_banks, out_w):
            NQ = 4 * TW
            hpb = 2 if out_w > 128 else 4
            ns = len(steps)
            def emit_scores(sp):
                KW = sp['KW']
                s_t, s_b = bank()
                nmm = 1 + len(sp['masks'])
                perhead = any(mr is None for (_, mr, _) in sp['masks'])
                regions = [(hh * TW, (hh + 1) * TW, hh) for hh in range(4)] if perhead else [(0, NQ, None)]
                for (c0_, c1_, hh) in regions:
                    qt_ = sp.get('qt', 0)
                    if hh is None:
                        qrhs = QT[:, qt_, 0:NQ]
                    else:
                        qrhs = QT[:, qt_, hh * TW:(hh + 1) * TW]
                    E("tensor", lambda h: h.matmul(s_t[0:KW, c0_:c1_], lhsT=sp['lhsT'][0], rhs=qrhs, start=True, stop=(nmm == 1)),
                      reads=[sp['lhsT'][1], b_QT], writes=[s_b])
                    for mi, (ml, mr, mbufs) in enumerate(sp['masks']):
                        last = (mi == nmm - 2)
                        if mr is None:
                            E("tensor", lambda h: h.matmul(s_t[0:KW, c0_:c1_], lhsT=ml[0], rhs=ml[1], start=False, stop=last),
                              reads=list(mbufs), writes=[s_b])
                        else:
                            E("tensor", lambda h: h.matmul(s_t[0:KW, c0_:c1_], lhsT=ml, rhs=mr[:, c0_:c1_], start=False, stop=last),
                              reads=list(mbufs), writes=[s_b])
                return s_t, s_b

            def emit_exp_pv(si, sp, s_t, s_b):
                KW = sp['KW']
                p_t, p_b_ = PT[pt_ctr[0] % 2]
                pt_ctr[0] += 1
                E("scalar", lambda h: h.activation(out=p_t[0:KW, 0:NQ], in_=s_t[0:KW, 0:NQ], func=AF.Exp), reads=[s_b], writes=[p_b_])
                for hh in range(4):
                    ob_t, ob_b = out_banks[hh // hpb]
                    off = (hh % hpb) * out_w
                    E("tensor", lambda h: h.matmul(ob_t[0:TW, off:off + out_w], lhsT=p_t[0:KW, hh * TW:(hh + 1) * TW], rhs=sp['pv'][0],
                                                   start=(si == 0 and (hh % hpb) == 0), stop=(si == ns - 1), skip_group_check=True),
                      reads=[p_b_, sp['pv'][1]], writes=[ob_b])

            pend = emit_scores(steps[0])
            for si, sp in enumerate(steps):
                nxt = emit_scores(steps[si + 1]) if si + 1 < ns else None
                emit_exp_pv(si, sp, *pend)
                pend = nxt

        def attention(TW, oi, n_keyt_slc, win_slots, cmp_nkt, cmp_maskv, is_sample):
            sigmoid_from(gate_f[0:TW, :], b_gatef, gate_f[0:TW, :], b_gatef, TW, 24)
            E("sync", lambda h: h.dma_start(out=bon_t[:], in_=T['bonus'][oi]), writes=[b_bont], dma=True)
            NQ_ = 4 * TW

            def load_QT(g_):
                pq_, pqb_ = bank()
                for hd in range(4):
                    transp(pq_[0:64, hd * TW:(hd + 1) * TW], pqb_, qa_b[0:TW, (g_ * 4 + hd) * 64:(g_ * 4 + hd + 1) * 64], b_qab, TW, 64)
                E("vector", lambda h: h.tensor_copy(out=QT[0:64, :, 0:NQ_], in_=pq_[0:64, 0:NQ_].unsqueeze(1).to_broadcast([64, 4, NQ_])),
                  reads=[pqb_], writes=[b_QT])
                for qt_ in range(4):
                    E("sync", lambda h: h.dma_start(out=QT[96:100, qt_, 0:NQ_].rearrange("p (a t) -> p a t", a=4), in_=T['qaug'][oi][:, g_, :, 0:TW]),
                      writes=[b_QT], dma=True)

            def combine(g, banks, out_w, gidx):
                hpb = 2 if out_w > 128 else 4
                gate3 = gate_f[0:TW, :].rearrange("p (a x) -> p a x", x=3)
                for bi_, (ob_t, ob_b) in enumerate(banks):
                    hg0 = 4 * g + bi_ * hpb
                    ob3 = ob_t[0:TW, 0:hpb * out_w].rearrange("p (a w) -> p a w", a=hpb)
                    rv = rden[0:TW, hg0:hg0 + hpb].unsqueeze(2)
                    E("vector", lambda h: h.tensor_scalar_max(out=rv, in0=ob3[:, :, 64:65], scalar1=1e-30), reads=[ob_b], writes=[b_rden])
                    E("vector", lambda h: h.reciprocal(out=rv, in_=rv), reads=[b_rden], writes=[b_rden])
                    E("vector", lambda h: h.tensor_mul(out=rv, in0=rv, in1=gate3[:, hg0:hg0 + hpb, gidx:gidx + 1]), reads=[b_rden, b_gatef], writes=[b_rden])
                    av = acc[0:TW, hg0:hg0 + hpb, :]
                    rbc = rden[0:TW, hg0:hg0 + hpb].unsqueeze(2).to_broadcast([TW, hpb, 64])
                    if gidx == 0:
                        E("vector", lambda h: h.tensor_mul(out=av, in0=ob3[:, :, 0:64], in1=rbc), reads=[ob_b, b_rden], writes=[b_acc])
                    else:
                        tmpv = Wbig[0:TW, 0, 0:hpb * 64].rearrange("p (a d) -> p a d", a=hpb)
                        E("vector", lambda h: h.tensor_mul(out=tmpv, in0=ob3[:, :, 0:64], in1=rbc), reads=[ob_b, b_rden], writes=[b_sgt])
                        E("vector", lambda h: h.tensor_add(out=av, in0=av, in1=tmpv), reads=[b_acc, b_sgt], writes=[b_acc])

            for g in range(2):
                load_QT(g)
                steps = []
                for kt in range(cmp_nkt):
                    masks = []
                    if cmp_maskv is not None and kt == cmp_nkt - 1:
                        masks.append(((ident_b[:, :], cmask[:, cmp_maskv, 0:TW]), None, (b_identb, b_cmask)))
                    steps.append(dict(lhsT=(kcT[:, g, kt * 128:(kt + 1) * 128], b_kcT), KW=128, masks=masks,
                                      pv=(vm[:, kt, g, :], b_vm)))
                attn_steps(TW, g, steps, po_banks[0:2], 193)
                steps = []
                for slot, mk in win_slots:
                    masks = []
                    if mk == 'hi':
                        masks.append(((ident_b[:, :], m_hi[:, 0:TW]), None, (b_identb, b_mhi)))
                    elif mk == 'lo':
                        masks.append(((ident_b[:, :], m_lo[:, 0:TW]), None, (b_identb, b_mlo)))
                    steps.append(dict(lhsT=(kwinT[:, slot, g, :], b_kwinT), KW=128, masks=masks, pv=(vwin[:, slot, g, :], b_vwin)))
                if is_sample:
                    steps.append(dict(lhsT=(knewT[:, 1, g, :], b_knewT), KW=1, masks=[], pv=(vnew[0:1, 1, g, :], b_vnew)))
                attn_steps(TW, g, steps, [po_banks[2]], 65)
                for hh in range(4):
                    ob_t, ob_b = po_banks[hh // 2]
                    off = (hh % 2) * 193
                    E("vector", lambda h, ob_t=ob_t, off=off, hh=hh: h.tensor_scalar_max(out=rden[0:TW, 4 * g + hh:4 * g + hh + 1], in0=ob_t[0:TW, off + 64:off + 65], scalar1=1e-30),
                      reads=[ob_b], writes=[b_rden])
                    E("vector", lambda h, hh=hh: h.reciprocal(out=rden[0:TW, 4 * g + hh:4 * g + hh + 1], in_=rden[0:TW, 4 * g + hh:4 * g + hh + 1]),
                      reads=[b_rden], writes=[b_rden])
                    if hh == 0:
                        E("vector", lambda h, ob_t=ob_t, off=off: h.tensor_scalar(out=ps_t[0:TW, :], in0=ob_t[0:TW, off + 65:off + 193],
                                                                               scalar1=rden[0:TW, 4 * g:4 * g + 1], scalar2=None, op0=ALU.mult),
                          reads=[ob_b, b_rden], writes=[b_pst])
                    else:
                        E("vector", lambda h, ob_t=ob_t, off=off, hh=hh: h.scalar_tensor_tensor(out=ps_t[0:TW, :], in0=ob_t[0:TW, off + 65:off + 193],
                                                                                             scalar=rden[0:TW, 4 * g + hh:4 * g + hh + 1], in1=ps_t[0:TW, :],
                                                                                             op0=ALU.mult, op1=ALU.add),
                          reads=[ob_b, b_rden, b_pst], writes=[b_pst])
                combine(g, po_banks[0:2], 193, 0)
                tap(f"acc_c{g}", acc[0:TW, :, :], b_acc, [128, 8, 64])
                tap(f"ps{g}", ps_t[0:TW, :], b_pst, [128, 128])
                E("vector", lambda h: h.tensor_add(out=ps_t[0:TW, :], in0=ps_t[0:TW, :], in1=bon_t[0:TW, :]), reads=[b_pst, b_bont], writes=[b_pst])
                E("vector", lambda h: h.max(out=mx8[0:TW, 0:8], in_=ps_t[0:TW, :]), reads=[b_pst], writes=[b_mx8])
                E("vector", lambda h: h.match_replace(out=sc_t[0:TW, :], in_to_replace=mx8[0:TW, 0:8], in_values=ps_t[0:TW, :], imm_value=-1e9),
                  reads=[b_pst, b_mx8], writes=[b_sct])
                E("vector", lambda h: h.max(out=mx8[0:TW, 8:16], in_=sc_t[0:TW, :]), reads=[b_sct], writes=[b_mx8])
                kth = 14 if is_sample else 15
                E("vector", lambda h: h.tensor_scalar(out=sc_t[0:TW, :], in0=ps_t[0:TW, :], scalar1=mx8[0:TW, kth:kth + 1], scalar2=None, op0=ALU.is_ge),
                  reads=[b_pst, b_mx8], writes=[b_sct])
                E("vector", lambda h: h.tensor_scalar(out=ps_t[0:TW, :], in0=ps_t[0:TW, :], scalar1=-50.0, scalar2=None, op0=ALU.is_gt),
                  reads=[b_pst], writes=[b_pst])
                E("vector", lambda h: h.tensor_mul(out=sc_t[0:TW, :], in0=sc_t[0:TW, :], in1=ps_t[0:TW, :]), reads=[b_sct, b_pst], writes=[b_sct])
                E("vector", lambda h: h.tensor_scalar(out=nm_pad[0:TW, :, 64:96], in0=sc_t[0:TW, :].rearrange("p (a c) -> p a c", a=4),
                                                      scalar1=-1.0, scalar2=-NEGM, op0=ALU.add, op1=ALU.mult),
                  reads=[b_sct], writes=[b_nmp])
                pn, pnb = bank()
                for qt_ in range(4):
                    transp(pn[0:96, qt_ * TW:(qt_ + 1) * TW], pnb, nm_pad[0:TW, qt_, :], b_nmp, TW, 96)
                E("vector", lambda h: h.tensor_copy(out=QT[64:96, :, 0:4 * TW].rearrange("p c (a t) -> p c a t", a=4),
                                                    in_=pn[64:96, 0:4 * TW].rearrange("p (c t) -> p c t", c=4).unsqueeze(2).to_broadcast([32, 4, 4, TW])),
                  reads=[pnb], writes=[b_QT])
                combine(g, [po_banks[2]], 65, 2)
                steps = []
                for kt in range(n_keyt_slc):
                    masks = []
                    if (not is_sample) and kt == n_keyt_slc - 1:
                        masks.append(((ident_b[:, :], m_lo[:, 0:TW]), None, (b_identb, b_mlo)))
                    steps.append(dict(lhsT=(kslcT[:, g, kt * 128:(kt + 1) * 128], b_kslcT), KW=128, masks=masks, qt=kt // 16,
                                      pv=(vslc[:, kt, g, :], b_vslc)))
                if is_sample:
                    steps.append(dict(lhsT=(knewT[:, 0, g, :], b_knewT), KW=1, masks=[], pv=(vnew[0:1, 0, g, :], b_vnew)))
                attn_steps(TW, g, steps, [po_banks[0]], 65)
                combine(g, [po_banks[0]], 65, 1)
                tap(f"acc_s{g}", acc[0:TW, :, :], b_acc, [128, 8, 64])

        def finish(TW, x_t, x_b, po_hg, p_src_ap, y_out_ap):
            sg, bsg = sg_t, b_sgt
            sigmoid_from(sg[0:TW, 0:512], bsg, za_f[0:TW, :], b_zaf, TW, 512)
            E("vector", lambda h: h.tensor_mul(out=sg[0:TW, 0:512], in0=sg[0:TW, 0:512], in1=za_f[0:TW, :]), reads=[bsg, b_zaf], writes=[bsg])
            E("vector", lambda h: h.tensor_mul(out=Y_b[0:TW, 0:512], in0=sg[0:TW, 0:512], in1=acc[0:TW, :, :].rearrange("p a d -> p (a d)")),
              reads=[bsg, b_acc], writes=[b_Yb])
            pot, pob = po_hg
            sigmoid_from(sg[0:TW, 512:1024], bsg, zb_f[0:TW, :], b_zbf, TW, 512)
            E("vector", lambda h: h.tensor_mul(out=sg[0:TW, 512:1024], in0=sg[0:TW, 512:1024], in1=zb_f[0:TW, :]), reads=[bsg, b_zbf], writes=[bsg])
            E("vector", lambda h: h.tensor_mul(out=Y_b[0:TW, 512:1024], in0=pot[0:TW, :], in1=sg[0:TW, 512:1024]), reads=[pob, bsg], writes=[b_Yb])
            tap("Y", Y_b[0:TW, :], b_Yb, [128, 1024], BF16)
            tap("gate", gate_f[0:TW, :], b_gatef, [128, 24])
            transpose_k(Y_b, b_Yb, YT, b_YT, TW, 8)
            pm = [bank(), bank()]
            for kc in range(8):
                w_t, w_b = stream_w(kc)
                for hf in range(2):
                    E("tensor", lambda h, hf=hf, kc=kc, w_t=w_t: h.matmul(pm[hf][0][0:TW, :], lhsT=YT[:, kc, 0:TW], rhs=w_t[:, hf * 512:(hf + 1) * 512],
                                                                  start=(kc == 0), stop=(kc == 7)), reads=[b_YT, w_b], writes=[pm[hf][1]])
            for hf in range(2):
                E("scalar", lambda h, hf=hf: h.activation(out=junk[0:TW, hf * 512:(hf + 1) * 512], in_=pm[hf][0][0:TW, :], func=AF.Square,
                                                          accum_out=ssq[0:TW, hf:hf + 1]), reads=[pm[hf][1]], writes=[b_junk, b_ssq])
            E("vector", lambda h: h.tensor_add(out=ssq[0:TW, 0:1], in0=ssq[0:TW, 0:1], in1=ssq[0:TW, 1:2]), reads=[b_ssq], writes=[b_ssq])
            E("scalar", lambda h: h.activation(out=ssq[0:TW, 0:1], in_=ssq[0:TW, 0:1], func=AF.Sqrt, scale=1.0 / D, bias=eps_t[0:TW, :]),
              reads=[b_ssq, b_eps], writes=[b_ssq])
            E("vector", lambda h: h.reciprocal(out=ssq[0:TW, 0:1], in_=ssq[0:TW, 0:1]), reads=[b_ssq], writes=[b_ssq])
            for hf in range(2):
                E("vector", lambda h, hf=hf: h.scalar_tensor_tensor(out=h1[0:TW, hf * 512:(hf + 1) * 512], in0=pm[hf][0][0:TW, :], scalar=ssq[0:TW, 0:1],
                                                                    in1=g_post_bc[0:TW, hf * 512:(hf + 1) * 512], op0=ALU.mult, op1=ALU.mult),
                  reads=[pm[hf][1], b_ssq, b_gpost], writes=[b_h1])
            x2_t, x2_b = next_stage()
            E("sync", lambda h: h.dma_start(out=x2_t[0:TW, :], in_=x_t), writes=[x2_b], dma=True)
            E("gpsimd", lambda h: h.tensor_add(out=h1[0:TW, :], in0=h1[0:TW, :], in1=x2_t[0:TW, :]), reads=[b_h1, x2_b], writes=[b_h1])
            tap("h1", h1[0:TW, :], b_h1, [128, 1024])
            E("vector", lambda h: h.tensor_copy(out=Y_b[0:TW, :], in_=h1[0:TW, :]), reads=[b_h1], writes=[b_Yb])
            transpose_k(Y_b, b_Yb, YT, b_YT, TW, 8)
            E("sync", lambda h: h.dma_start(out=junk[0:TW, 0:256], in_=p_src_ap), writes=[b_junk], dma=True)
            E("vector", lambda h: h.tensor_copy(out=p_b[0:TW, :], in_=junk[0:TW, 0:256]), reads=[b_junk], writes=[b_pb])
            transpose_k(p_b, b_pb, pT_b, b_pTb, TW, 2)
            pgs = [bank(), bank()]
            pps = [bank(), bank()]
            for kc in range(8):
                w_t, w_b = stream_w(8 + kc)
                for hf in range(2):
                    E("tensor", lambda h, hf=hf, kc=kc, w_t=w_t: h.matmul(pgs[hf][0][0:TW, :], lhsT=YT[:, kc, 0:TW], rhs=w_t[:, hf * 512:(hf + 1) * 512],
                                                                         start=(kc == 0), stop=(kc == 7)), reads=[b_YT, w_b], writes=[pgs[hf][1]])
            for kc in range(2):
                w_t, w_b = stream_w(16 + kc)
                for hf in range(2):
                    E("tensor", lambda h, hf=hf, kc=kc, w_t=w_t: h.matmul(pps[hf][0][0:TW, :], lhsT=pT_b[:, kc, 0:TW], rhs=w_t[:, hf * 512:(hf + 1) * 512],
                                                                           start=(kc == 0), stop=(kc == 1)), reads=[b_pTb, w_b], writes=[pps[hf][1]])
            for hf in range(2):
                pg, pgb = pgs[hf]
                pp_, ppb = pps[hf]
                sl = slice(hf * 512, (hf + 1) * 512)
                sigmoid_from(sg[0:TW, sl], bsg, pg[0:TW, :], pgb, TW, 512)
                E("vector", lambda h, sl=sl, pp_=pp_: h.tensor_mul(out=sg[0:TW, sl], in0=sg[0:TW, sl], in1=pp_[0:TW, :]), reads=[bsg, ppb], writes=[bsg])
                E("gpsimd", lambda h, sl=sl: h.tensor_add(out=h1[0:TW, sl], in0=h1[0:TW, sl], in1=sg[0:TW, sl]), reads=[b_h1, bsg], writes=[b_h1])
            E("sync", lambda h: h.dma_start(out=y_out_ap, in_=h1[0:TW, :]), reads=[b_h1], writes=[finish.ob], dma=True)

        finish.ob = P.buf("yout")
        outbufs.append(finish.ob)

        def own_proj(TW):
            pq, pqb = proj(O_Q, 512, TW)
            E("scalar", lambda h: h.activation(out=qa_b[0:TW, :], in_=pq[0:TW, :], func=AF.Copy, scale=0.125), reads=[pqb], writes=[b_qab])
            pz, pzb = proj(O_ZA, 512, TW)
            E("vector", lambda h: h.tensor_copy(out=za_f[0:TW, :], in_=pz[0:TW, :]), reads=[pzb], writes=[b_zaf])
            pz2, pz2b = proj(O_ZB, 512, TW)
            E("scalar", lambda h: h.activation(out=zb_f[0:TW, :], in_=pz2[0:TW, :], func=AF.Copy), reads=[pz2b], writes=[b_zbf])
            pgt, pgtb = proj(O_G, 24, TW)
            E("vector", lambda h: h.tensor_copy(out=gate_f[0:TW, :], in_=pgt[0:TW, 0:24]), reads=[pgtb], writes=[b_gatef])
            pq, pqb = proj(O_QB, 512, TW)
            qs, bqs = hw["qs"]
            sigmoid_from(qs[0:TW, :], bqs, pq[0:TW, :], pqb, TW, 512)
            E("vector", lambda h: h.tensor_mul(out=qs[0:TW, :], in0=qs[0:TW, :], in1=pq[0:TW, :]), reads=[bqs, pqb], writes=[bqs])

        def proj_fi(TW, par):
            pf, pfb = proj(O_F, 512, TW)
            sigmoid_from(e1buf[par][0][0:TW, :], e1buf[par][1], pf[0:TW, :], pfb, TW, 512)
            pi, pib = proj(O_I, 512, TW)
            E("scalar", lambda h: h.activation(out=vbuf[par][0][0:TW, :], in_=pi[0:TW, :], func=AF.Copy), reads=[pib], writes=[vbuf[par][1]])

        E("gpsimd", lambda h: h.memset(S_t[:], 0.0), writes=[b_S])
        for g in range(2):
            E("sync", lambda h, g=g: h.dma_start(out=kslcT[96:100, g, :], in_=T['kaugP']), writes=[b_kslcT], dma=True)
            E("sync", lambda h, g=g: h.dma_start(out=kcT[96:100, g, :], in_=T['caugP']), writes=[b_kcT], dma=True)
        def front(n):
            own = (n % 4 == 3)
            i = n // 4
            x_t, x_b = next_stage()
            E("sync", lambda h: h.dma_start(out=x_t[:], in_=T['xloc'][n * 128:(n + 1) * 128, :]), writes=[x_b], dma=True)
            rmsnorm_T(x_t, x_b, 128)
            slot = n % 8
            for g in range(2):
                E("sync", lambda h: h.dma_start(out=kwinT[96:100, slot, g, :], in_=T['kaugP'][:, n * 128:(n + 1) * 128]), writes=[b_kwinT], dma=True)
            kvout = None
            if own:
                kvout = (T['cmp_p'][i * 128:(i + 1) * 128, :], T['slc_p'][i * 128:(i + 1) * 128, :], T['win_p'] if i == 15 else None)
            kv_from_tokmajor(128, kslcT[0:64, :, n * 128:(n + 1) * 128], kwinT[0:64, slot, :, :],
                             vslc[:, n, :, 0:64], vwin[:, slot, :, 0:64], 16 + (n % 4) * 128, own, kvout)
            proj_fi(128, n % 2)

        if n_tiles > 0:
            front(0)
        for n in range(n_tiles):
            own = (n % 4 == 3)
            i = n // 4
            cur['i'] = i if own else -2
            if own:
                own_proj(128)
                compress_group(i)
                vc_tile(i // 4)
            cur['i'] = -2
            if n + 1 < n_tiles:
                front(n + 1)
            cur['i'] = i if own else -2
            hgrn_tile(128, own, e1buf[n % 2], vbuf[n % 2])
            if own:
                po_hg = hgrn_out(128)
                if stage >= 3:
                    win_slots = []
                    for kt in range(n - 4, n + 1):
                        if kt < 0:
                            continue
                        win_slots.append((kt % 8, 'hi' if kt == n - 4 else ('lo' if kt == n else None)))
                    attention(128, i, n + 1, win_slots, i // 4 + 1, i % 4, False)
                else:
                    E("vector", lambda h: h.memset(acc[:], 0.0), writes=[b_acc])
                finish(128, T['xloc'][n * 128:(n + 1) * 128, :], None, po_hg, T['pown'][i * 128:(i + 1) * 128, :], T['y_p'][i * 128:(i + 1) * 128, :])
        hg_ob = P.buf("hgout")
        outbufs.append(hg_ob)
        E("sync", lambda h: h.dma_start(out=T['hg_p'].rearrange("a k v -> k a v"), in_=S_t[:]), reads=[b_S], writes=[hg_ob], dma=True)

        if stage >= 4:
            idx_i, b_idxi = sb("idx_i", [128, 64], I32)
            idx_f, b_idxf = sb("idx_f", [128, 64])
            iota_p, b_iota = sb("iota_p", [128, 1])
            E("gpsimd", lambda h: h.iota(iota_p[:], pattern=[[0, 1]], base=0, channel_multiplier=1, allow_small_or_imprecise_dtypes=True), writes=[b_iota])
            pg_t = [(stage_t[i][0][:, 0:256], stage_t[i][1]) for i in range(2)]
            pgb_t, b_pgb = zt_b[:, 0:256], b_ztb
            pgc = [0]
            for bi in range(4):
                E("sync", lambda h, bi=bi: h.dma_start(out=idx_i[:], in_=T['ptab'][bi:bi + 1, :].partition_broadcast(128)), writes=[b_idxi], dma=True)
                E("vector", lambda h: h.tensor_copy(out=idx_f[:], in_=idx_i[:]), reads=[b_idxi], writes=[b_idxf])
                E("vector", lambda h: h.tensor_scalar(out=idx_f[:], in0=idx_f[:], scalar1=128.0, scalar2=iota_p[:, 0:1], op0=ALU.mult, op1=ALU.add),
                  reads=[b_idxf, b_iota], writes=[b_idxf])
                E("vector", lambda h: h.tensor_copy(out=idx_i[:], in_=idx_f[:]), reads=[b_idxf], writes=[b_idxi])
                for g in range(2):
                    E("sync", lambda h, g=g: h.dma_start(out=kslcT[96:100, g, :], in_=T['kaugS'][:, 0:8192]), writes=[b_kslcT], dma=True)
                    E("sync", lambda h, g=g: h.dma_start(out=kcT[96:100, g, :], in_=T['caugS']), writes=[b_kcT], dma=True)
                    for sw in range(2):
                        E("sync", lambda h, g=g, sw=sw: h.dma_start(out=knewT[96:100, sw, g, :], in_=T['kaugS'][:, 8192:8193]), writes=[b_knewT], dma=True)
                E("gpsimd", lambda h: h.memset(kc_raw[:, :, 0:16], 0.0), writes=[b_kcraw])

                rawv = vslc[:, :, :, :].rearrange("p a g d -> p (a g d)")
                R = rawv.rearrange("p (g c) -> p g c", g=2)
                castb = [(xn_b, b_xn), (xnT[:, :, :].rearrange("p k t -> p (k t)"), b_xnT)]
                cb_ctr = [0]

                pstage = [stage_t[0], stage_t[1], (Wflat[:, 2048:3072], b_h1), (Wflat[:, 3072:4096], b_w67)]
                ps_ctr = [0]

                def load4(src_kind, pg0):
                    st_t, st_b = pstage[ps_ctr[0] % 4]
                    ps_ctr[0] += 1
                    if src_kind == 'win':
                        E("sync", lambda h: h.dma_start(out=st_t[:, :].rearrange("p (k c) -> p k c", k=4),
                                                        in_=T['cache_win'][bi].rearrange("(k p) c -> p k c", p=128)), writes=[st_b], dma=True)
                    else:
                        src = T['cache_cmp'] if src_kind == 'cmp' else T['cache_slc']
                        for k in range(4):
                            E("gpsimd", lambda h: h.indirect_dma_start(out=st_t[:, k * 256:(k + 1) * 256], out_offset=None, in_=src[:, :],
                                                                       in_offset=bass.IndirectOffsetOnAxis(ap=idx_i[:, pg0 + k:pg0 + k + 1], axis=0)),
                              reads=[b_idxi], writes=[st_b], dma=True)
                    return st_t, st_b

                def cast4(st_t, st_b, permute):
                    c_t, c_b = castb[cb_ctr[0] % 2]
                    cb_ctr[0] += 1
                    if permute:
                        for k in range(4):
                            E("vector",
                              lambda h: h.tensor_copy(out=c_t[:, k * 256:(k + 1) * 256].rearrange("p (g s d) -> p g s d", g=2, s=2),
                                                      in_=st_t[:, k * 256:(k + 1) * 256].rearrange("p (s g d) -> p g s d", s=2, g=2)),
                              reads=[st_b], writes=[c_b])
                    else:
                        E("vector", lambda h: h.tensor_copy(out=c_t[:, 0:512], in_=st_t[:, 0:512]), reads=[st_b], writes=[c_b])
                        E("scalar", lambda h: h.activation(out=c_t[:, 512:1024], in_=st_t[:, 512:1024], func=AF.Copy), reads=[st_b], writes=[c_b])
                    return c_t, c_b

                for g in range(2):
                    E("gpsimd", lambda h: h.memset(R[:, g, 0:16], 0.0), writes=[b_vslc])
                DEPTH = 3
                pendq = [load4('cmp', qq * 4) for qq in range(DEPTH)]
                for half in range(2):
                    if half == 1:
                        E("vector", lambda h: h.tensor_copy(out=R[:, :, 0:16], in_=R[:, :, 4096:4112]), reads=[b_vslc], writes=[b_vslc])
                    for q in range(8):
                        st_t, st_b = pendq.pop(0)
                        nq_ = half * 8 + q + DEPTH
                        if nq_ < 16:
                            pendq.append(load4('cmp', nq_ * 4))
                        c_t, c_b = cast4(st_t, st_b, True)
                        for k0 in (0, 2):
                            pc_, pcb = bank()
                            for kk_ in range(2):
                                for g in range(2):
                                    k = k0 + kk_
                                    E("tensor", lambda h: h.matmul(pc_[:, (kk_ * 2 + g) * 128:(kk_ * 2 + g + 1) * 128],
                                                                   lhsT=c_t[:, k * 256 + g * 128:k * 256 + (g + 1) * 128], rhs=ident_b[:, :], start=True, stop=True),
                                      reads=[c_b, b_identb], writes=[pcb])
                            c0 = 16 + (q * 4 + k0) * 128
                            eng = "scalar" if k0 == 0 else "vector"
                            outv = R[:, :, c0:c0 + 256].rearrange("p g (k t) -> p k g t", k=2)
                            inv = pc_[:, 0:512].rearrange("p (k g t) -> p k g t", k=2, g=2)
                            if eng == "scalar":
                                E("scalar", lambda h: h.activation(out=outv, in_=inv, func=AF.Copy), reads=[pcb], writes=[b_vslc])
                            else:
                                E("vector", lambda h: h.tensor_copy(out=outv, in_=inv), reads=[pcb], writes=[b_vslc])
                    phs = [bank(), bank()]
                    for g in range(2):
                        for kv in range(2):
                            ph, phb = phs[kv]
                            for r in range(32):
                                E("tensor", lambda h: h.matmul(ph[:, g * 256:(g + 1) * 256], lhsT=w1_b[kv * 64:(kv + 1) * 64, r, :],
                                                               rhs=R[kv * 64:(kv + 1) * 64, g, r:r + 4081:16], start=(r == 0), stop=(r == 31)),
                                  reads=[b_w1, b_vslc], writes=[phb])
                    for kv in range(2):
                        ph, phb = phs[kv]
                        E("vector", lambda h: h.tensor_scalar(out=sg_t[:, kv * 512:(kv + 1) * 512], in0=ph[:, :], scalar1=b1e[:, kv:kv + 1], scalar2=None, op0=ALU.add),
                          reads=[phb, b_b1e], writes=[b_sgt])
                    E("scalar", lambda h: h.activation(out=junk[:, :], in_=sg_t[:, :], func=AF.Sigmoid), reads=[b_sgt], writes=[b_junk])
                    E("vector", lambda h: h.tensor_mul(out=Kd_b[:, :], in0=sg_t[:, 0:512], in1=junk[:, 0:512]), reads=[b_sgt, b_junk], writes=[b_Kd])
                    E("vector", lambda h: h.tensor_mul(out=v_b[:, :], in0=sg_t[:, 512:1024], in1=junk[:, 512:1024]), reads=[b_sgt, b_junk], writes=[b_vb])
                    pk, pkb = bank()
                    for g in range(2):
                        E("tensor", lambda h: h.matmul(pk[0:64, g * 256:(g + 1) * 256], lhsT=w2_b[:, 0, :], rhs=Kd_b[:, g * 256:(g + 1) * 256], start=True, stop=True),
                          reads=[b_w2, b_Kd], writes=[pkb])
                    E("vector", lambda h: h.tensor_copy(out=kcT[0:64, :, half * 256:(half + 1) * 256], in_=pk[0:64, 0:512].rearrange("p (g c) -> p g c", g=2)),
                      reads=[pkb], writes=[b_kcT])
                    E("vector", lambda h: h.tensor_copy(out=hv_all[:, :, half * 256:(half + 1) * 256], in_=v_b[:, :].rearrange("p (g c) -> p g c", g=2)),
                      reads=[b_vb], writes=[b_hvall])
                for ktc in range(4):
                    vc_tile(ktc)
                E("gpsimd", lambda h: h.memset(vslc[:, :, :, 64:65], 1.0), writes=[b_vslc])
                pendq = [load4('slc', qq * 4) for qq in range(DEPTH)]
                for q in range(16):
                    st_t, st_b = pendq.pop(0)
                    if q + DEPTH < 16:
                        pendq.append(load4('slc', (q + DEPTH) * 4))
                    c_t, c_b = cast4(st_t, st_b, False)
                    pg0 = q * 4
                    for k0 in (0, 2):
                        pt_, ptb = bank()
                        for kk_ in range(2):
                            for g in range(2):
                                k = k0 + kk_
                                transp(pt_[0:64, (kk_ * 2 + g) * 128:(kk_ * 2 + g + 1) * 128], ptb, c_t[:, k * 256 + g * 64:k * 256 + (g + 1) * 64], c_b, 128, 64)
                        outv = kslcT[0:64, :, (pg0 + k0) * 128:(pg0 + k0) * 128 + 256].rearrange("p g (k t) -> p k g t", k=2)
                        inv = pt_[0:64, 0:512].rearrange("p (k g t) -> p k g t", k=2, g=2)
                        if k0 == 0:
                            E("scalar", lambda h: h.activation(out=outv, in_=inv, func=AF.Copy), reads=[ptb], writes=[b_kslcT])
                        else:
                            E("vector", lambda h: h.tensor_copy(out=outv, in_=inv), reads=[ptb], writes=[b_kslcT])
                    E("vector", lambda h: h.tensor_copy(out=vslc[:, pg0:pg0 + 4, :, 0:64],
                                                        in_=c_t[:, :].rearrange("p (k c) -> p k c", k=4)[:, :, 128:256].rearrange("p k (g d) -> p k g d", g=2)),
                      reads=[c_b], writes=[b_vslc])
                wo_b = P.buf("winout")
                outbufs.append(wo_b)
                E("sync", lambda h: h.dma_start(out=T['win_s'][bi, 0:511, :], in_=T['cache_win'][bi, 1:512, :]), writes=[wo_b], dma=True)
                st_t, st_b = load4('win', 0)
                c_t, c_b = cast4(st_t, st_b, False)
                for k0 in (0, 2):
                    pt_, ptb = bank()
                    for kk_ in range(2):
                        for g in range(2):
                            k = k0 + kk_
                            transp(pt_[0:64, (kk_ * 2 + g) * 128:(kk_ * 2 + g + 1) * 128], ptb, c_t[:, k * 256 + g * 64:k * 256 + (g + 1) * 64], c_b, 128, 64)
                    E("scalar", lambda h: h.activation(out=kwinT[0:64, 4 + k0:6 + k0, :, :], in_=pt_[0:64, 0:512].rearrange("p (k g t) -> p k g t", k=2, g=2), func=AF.Copy),
                      reads=[ptb], writes=[b_kwinT])
                E("vector", lambda h: h.tensor_copy(out=vwin[:, 4:8, :, 0:64],
                                                    in_=c_t[:, :].rearrange("p (k c) -> p k c", k=4)[:, :, 128:256].rearrange("p k (g d) -> p k g d", g=2)),
                  reads=[c_b], writes=[b_vwin])
                for pgi in range(4):
                    for g in range(2):
                        E("sync", lambda h: h.dma_start(out=kwinT[96:100, 4 + pgi, g, :], in_=T['kaugS'][:, (60 + pgi) * 128:(61 + pgi) * 128]),
                          writes=[b_kwinT], dma=True)
                xs_t, b_xs = next_stage()
                E("sync", lambda h, bi=bi, xs_t=xs_t: h.dma_start(out=xs_t[0:1, :], in_=T['xs'][bi:bi + 1, :]), writes=[b_xs], dma=True)
                rmsnorm_T(xs_t, b_xs, 1)
                kv_from_tokmajor(1, knewT[0:64, 0, :, :], knewT[0:64, 1, :, :], vnew[0:1, 0, :, 0:64], vnew[0:1, 1, :, 0:64], None, True,
                                 (T['cmp_s'][bi:bi + 1, :], T['slc_s'][bi:bi + 1, :], T['win_s'][bi, 511:512, :]))
                E("sync", lambda h, bi=bi: h.dma_start(out=S_t[:], in_=T['state0'][bi].rearrange("a k v -> k a v")), writes=[b_S], dma=True)
                proj_fi(1, 0)
                hgrn_tile(1, True, e1buf[0], vbuf[0])
                own_proj(1)
                po_hg = hgrn_out(1)
                E("sync", lambda h, bi=bi: h.dma_start(out=T['hg_s'][bi].rearrange("a k v -> k a v"), in_=S_t[:]), reads=[b_S], writes=[hg_ob], dma=True)
                attention(1, 16, 64, [(4, None), (5, None), (6, None), (7, None)], 4, None, True)
                finish(1, T['xs'][bi:bi + 1, :], None, po_hg, T['psm'][bi:bi + 1, :], T['y_s'][bi:bi + 1, :])

        P.final_wait("sync", outbufs)
        P.build(nc, st)
    return nc, P


_NC_CACHE = {}


def _prep_inputs(inp, small_cache=False):
    perm = _perm_cols()
    w_in_p = np.ascontiguousarray(inp['w_in'][0][:, perm])
    shared = {
        'cache_cmp': inp['cache_cmp_kv'][0].reshape(NPOOLROWS, 256),
        'cache_slc': inp['cache_slc_kv'][0].reshape(NPOOLROWS, 256),
        'w_in': w_in_p, 'g_pre': inp['g_pre'], 'cmp_pe': inp['cmp_pe'][0].reshape(2, 2048),
        'cmp_w1': inp['cmp_w1'][0], 'cmp_b1': inp['cmp_b1'][0], 'cmp_w2': inp['cmp_w2'][0],
        'hg_lower': inp['hg_lower'], 'hg_norm': inp['hg_norm'], 'w_out': inp['w_out'][0], 'g_post': inp['g_post'],
        'ple_proj': inp['ple_proj'][0], 'ple_gate': inp['ple_gate'][0],
    }
    consts = [_consts(j) for j in range(4)]
    if small_cache:
        shared['cache_cmp'] = shared['cache_cmp'][:128]
        shared['cache_slc'] = shared['cache_slc'][:128]
    maps = []
    for c in range(8):
        b, j = c // 4, c % 4
        pad = 3 - j
        xloc = np.zeros((8192, D), np.float32)
        xloc[pad * 128:] = inp['x_prompt'][b, :8192 - pad * 128]
        own_rows = np.concatenate([np.arange((4 * i + j) * 128, (4 * i + j + 1) * 128) for i in range(16)])
        m = dict(shared)
        m['xloc'] = xloc
        m['pown'] = np.ascontiguousarray(inp['p_prompt'][0, b][own_rows])
        m['xs'] = np.ascontiguousarray(inp['x_sample'][4 * c:4 * c + 4, 0])
        m['psm'] = np.ascontiguousarray(inp['p_sample'][0, 4 * c:4 * c + 4, 0])
        m['cache_win'] = np.ascontiguousarray(inp['cache_win_kv'][0, 4 * c:4 * c + 4].reshape(4, 512, 256))
        m['state0'] = np.ascontiguousarray(inp['state_hgrn'][0, 4 * c:4 * c + 4])
        m['ptab'] = np.ascontiguousarray(inp['page_table'][4 * c:4 * c + 4]).astype(np.int32)
        for k, v in consts[j].items():
            m[k] = v
        maps.append({k: np.ascontiguousarray(v) for k, v in m.items()})
    return maps


def _assemble(res):
    y_p = np.zeros((2, 8192, D), np.float32)
    cmp_p = np.zeros((1, 2, 8192, 2, 2, 64), np.float32)
    slc_p = np.zeros_like(cmp_p)
    win_p = np.zeros((1, 2, 512, 2, 2, 64), np.float32)
    hg_p = np.zeros((1, 2, 4, 128, 128), np.float32)
    y_s = np.zeros((32, 1, D), np.float32)
    cmp_s = np.zeros((1, 32, 1, 2, 2, 64), np.float32)
    slc_s = np.zeros_like(cmp_s)
    win_s = np.zeros((1, 32, 512, 2, 2, 64), np.float32)
    hg_s = np.zeros((1, 32, 4, 128, 128), np.float32)
    for c in range(8):
        r = res[c]
        b, j = c // 4, c % 4
        for i in range(16):
            gt = 4 * i + j
            sl = slice(gt * 128, (gt + 1) * 128)
            y_p[b, sl] = r['y_p'][i * 128:(i + 1) * 128]
            cmp_p[0, b, sl] = r['cmp_p'][i * 128:(i + 1) * 128].reshape(128, 2, 2, 64)
            slc_p[0, b, sl] = r['slc_p'][i * 128:(i + 1) * 128].reshape(128, 2, 2, 64)
        win_p[0, b, j * 128:(j + 1) * 128] = r['win_p'].reshape(128, 2, 2, 64)
        if j == 3:
            hg_p[0, b] = r['hg_p']
        y_s[4 * c:4 * c + 4, 0] = r['y_s']
        cmp_s[0, 4 * c:4 * c + 4, 0] = r['cmp_s'].reshape(4, 2, 2, 64)
        slc_s[0, 4 * c:4 * c + 4, 0] = r['slc_s'].reshape(4, 2, 2, 64)
        win_s[0, 4 * c:4 * c + 4] = r['win_s'].reshape(4, 512, 2, 2, 64)
        hg_s[0, 4 * c:4 * c + 4] = r['hg_s']
    return (y_p, y_s, cmp_p, slc_p, win_p, hg_p, cmp_s, slc_s, win_s, hg_s)


def kernel(**inputs):
    inp = {k: np.asarray(v) for k, v in inputs.items()}
    if 'nc' not in _NC_CACHE:
        _NC_CACHE['nc'] = build_nc()[0]
    nc = _NC_CACHE['nc']
    maps = _prep_inputs(inp)
    res = run_bass_kernel_spmd(nc, maps, core_ids=list(range(8)))
    return _assemble(res.results)
```

```python
import contextlib
import numpy as np
import ml_dtypes
import concourse.bass as bass
import concourse.mybir as mybir
from concourse.bass_utils import run_bass_kernel_spmd

F32 = mybir.dt.float32
BF16 = mybir.dt.bfloat16
I32 = mybir.dt.int32
AF = mybir.ActivationFunctionType
ALU = mybir.AluOpType
bf16 = ml_dtypes.bfloat16

NEGM = -30000.0
BIGA = -16384.0
EPS = 1e-6


class Op:
    __slots__ = ("eng", "idx", "fn", "waits", "dwaits", "signaled", "semval", "dma_sem")

    def __init__(self, eng, idx, fn):
        self.eng = eng
        self.idx = idx
        self.fn = fn
        self.waits = []
        self.dwaits = []
        self.signaled = False
        self.semval = None
        self.dma_sem = None


class _Rec:
    def __init__(self):
        self.call = None

    def __getattr__(self, name):
        def f(*a, **kw):
            self.call = (name, a, kw)
            return None
        return f


class DmaSem:
    def __init__(self):
        self.total = 0
        self.handle = None


class Buf:
    def __init__(self, name):
        self.name = name
        self.w = None
        self.r = []
        self.dsem = None
        self.dsem_sw = None
        self.excl = False


class Eng:
    def __init__(self, name):
        self.name = name
        self.ops = []
        self.known = {}
        self.dknown = {}
        self.sem = None


class Prog:
    ENG_NAMES = ("tensor", "vector", "scalar", "gpsimd", "sync")

    def __init__(self):
        self.eng = {n: Eng(n) for n in self.ENG_NAMES}
        self.dsems = []
        self.nbuf = 0

    def buf(self, name=None):
        self.nbuf += 1
        return Buf(name or f"b{self.nbuf}")

    def _need(self, e, tok, op, rr=False):
        if tok is None:
            return
        if tok[0] == "op":
            p = tok[1]
            if p.eng is e and (e.name == "tensor" or rr):
                return
            if e.known.get(p.eng.name, -1) >= p.idx:
                return
            e.known[p.eng.name] = p.idx
            p.signaled = True
            op.waits.append(p)
        else:
            ds = tok[1]
            val = ds.total
            if e.dknown.get(ds, 0) >= val:
                return
            e.dknown[ds] = val
            op.dwaits.append((ds, val))

    def emit(self, engname, fn, reads=(), writes=(), dma=False):
        e = self.eng[engname]
        rec = _Rec()
        fn(rec)
        assert rec.call is not None
        op = Op(e, len(e.ops), rec.call)
        for b in reads:
            self._need(e, b.w, op)
            if b.excl:
                for t in b.r:
                    self._need(e, t, op, rr=True)
        for b in writes:
            self._need(e, b.w, op)
            for t in b.r:
                self._need(e, t, op)
        if dma:
            owner = writes[0] if writes else reads[0]
            if engname == "gpsimd":
                if owner.dsem_sw is None:
                    owner.dsem_sw = DmaSem()
                    self.dsems.append(owner.dsem_sw)
                ds = owner.dsem_sw
            else:
                if owner.dsem is None:
                    owner.dsem = DmaSem()
                    self.dsems.append(owner.dsem)
                ds = owner.dsem
            ds.total += 16
            op.dma_sem = ds
            tok = ("dma", ds)
        else:
            tok = ("op", op)
        e.ops.append(op)
        for b in reads:
            b.r.append(tok)
            if len(b.r) > 64:
                b.r = b.r[-64:] if False else b.r
        for b in writes:
            b.w = tok
            b.r = []
        return op

    def final_wait(self, engname, bufs):
        e = self.eng[engname]
        op = Op(e, len(e.ops), None)
        for b in bufs:
            self._need(e, b.w, op)
            for t in b.r:
                self._need(e, t, op)
        e.ops.append(op)

    def build(self, nc, stack):
        for n, e in self.eng.items():
            e.sem = stack.enter_context(nc.semaphore(f"s_{n}"))
            c = 0
            for op in e.ops:
                if op.signaled:
                    c += 1
                    op.semval = c
        for i, ds in enumerate(self.dsems):
            ds.handle = stack.enter_context(nc.semaphore(f"d{i}"))
        block = stack.enter_context(nc.Block())

        def run(e):
            def body(h):
                for op in e.ops:
                    for p in op.waits:
                        h.wait_ge(p.eng.sem, p.semval)
                    for ds, val in op.dwaits:
                        h.wait_ge(ds.handle, val)
                    if op.fn is None:
                        continue
                    name, a, kw = op.fn
                    inst = getattr(h, name)(*a, **kw)
                    if op.dma_sem is not None:
                        inst.then_inc(op.dma_sem.handle, 16)
                    elif op.signaled:
                        inst.then_inc(e.sem, 1)
            return body

        block.tensor(run(self.eng["tensor"]))
        block.vector(run(self.eng["vector"]))
        block.scalar(run(self.eng["scalar"]))
        block.gpsimd(run(self.eng["gpsimd"]))
        block.sync(run(self.eng["sync"]))


D = 1024
NT = 64
NOWN = 16
HD = 64
NCOL = 3864
C_ALL = 1792
O_A0, O_A1, O_A2, O_F, O_I = 0, 256, 512, 768, 1280
O_Q, O_ZA, O_QB, O_ZB, O_G = 1792, 2304, 2816, 3328, 3840
NPOOLROWS = 2560 * 128


def _perm_cols():
    off = {}
    o = 0
    for name, w in (('q_a', 512), ('k_cmp', 128), ('v_cmp', 128), ('k_slc', 128), ('v_slc', 128),
                    ('k_win', 128), ('v_win', 128), ('gate_a', 24), ('z_a', 512),
                    ('q_b', 512), ('f_b', 512), ('i_b', 512), ('z_b', 512)):
        off[name] = o
        o += w
    r = lambda n, a, b: list(range(off[n] + a, off[n] + b))
    cols = []
    cols += r('k_cmp', 0, 64) + r('v_cmp', 0, 64) + r('k_cmp', 64, 128) + r('v_cmp', 64, 128)
    cols += r('k_slc', 0, 128) + r('k_win', 0, 128)
    cols += r('v_slc', 0, 128) + r('v_win', 0, 128)
    cols += r('f_b', 0, 512) + r('i_b', 0, 512)
    cols += r('q_a', 0, 512) + r('z_a', 0, 512) + r('q_b', 0, 512) + r('z_b', 0, 512) + r('gate_a', 0, 24)
    assert len(cols) == NCOL
    return np.array(cols)


def _consts(j):
    c = {}
    pad = 3 - j
    slopes = 2.0 ** (-(np.arange(1, 9)))
    def kaug(padt, ntile):
        a = np.repeat(np.arange(ntile), 128).astype(np.float32)
        a[: padt * 128] = BIGA
        ki = np.tile(np.arange(128), ntile).astype(np.float32)
        one = np.ones_like(a)
        return np.stack([a, ki, one, one]).astype(bf16)
    c['kaugP'] = kaug(pad, 64)
    c['kaugS'] = kaug(0, 65)
    def caug(ninv):
        cc = np.arange(512)
        a = (cc // 8).astype(np.float32)
        a[:ninv] = BIGA
        b = (16 * (cc % 8) + 15).astype(np.float32)
        one = np.ones(512, np.float32)
        return np.stack([a, b, one, one]).astype(bf16)
    c['caugP'] = caug(8 * pad + 1)
    c['caugS'] = caug(1)
    qa = np.zeros((17, 4, 2, 4, 128), np.float32)
    for i in range(17):
        n = 4 * i + 3 if i < 16 else 64
        for g in range(2):
            for r in range(4):
                s = slopes[g * 4 + r]
                qa[i, 0, g, r, :] = s * 128
                qa[i, 1, g, r, :] = s
                qa[i, 2, g, r, :] = -s * 128 * n
                qa[i, 3, g, r, :] = -s * np.arange(128)
    c['qaug'] = qa.astype(bf16)
    ki = np.arange(128)[:, None]
    qi = np.arange(128)[None, :]
    rep = lambda m: m.astype(np.float32).astype(bf16)
    c['m_lo'] = rep(np.where(ki > qi, NEGM, 0.0))
    c['m_hi'] = rep(np.where(ki < qi, NEGM, 0.0))
    cm = np.zeros((4, 128, 128), bf16)
    for v in range(4):
        vis = (16 * (ki - 32 * v - 24) + 15) <= qi
        cm[v] = rep(np.where(vis, 0.0, NEGM))
    c['cmask'] = cm
    em = np.zeros((128, 32, 128), np.float32)
    p = np.arange(128)[:, None, None]
    jj = np.arange(32)[None, :, None]
    k = np.arange(128)[None, None, :]
    em[(p % 64) == (2 * jj + k // 64)] = 1.0
    cidx = np.arange(8192)
    e32 = (np.arange(32)[:, None] == (2 * ((cidx // 128) % 16) + (cidx % 128) // 64)[None, :])
    c['emat32'] = e32.astype(np.float32).astype(bf16)
    mm = np.zeros((512, 128), np.float32)
    wts = [1, 2, 2, 2, 1]
    for jb in range(128):
        for kk, wk in enumerate(wts):
            cc = 4 * jb + kk + 1
            if cc < 512:
                mm[cc, jb] += wk
    c['mmat'] = mm.reshape(4, 128, 128).transpose(1, 0, 2).copy().astype(bf16)
    bon = np.zeros((17, 128, 128), np.float32)
    blk = np.arange(128)[None, :]
    q = np.arange(128)[:, None]
    for i in range(16):
        n = 4 * i + 3
        jt = 2 * n + (q >= 64)
        b = np.zeros((128, 128), np.float32)
        b[np.broadcast_to(blk > jt, (128, 128))] = -100.0
        b[np.broadcast_to((blk == jt) | (blk == jt - 1), (128, 128))] = 1e6
        b[:, 2 * pad] = 1e6
        b[:, : 2 * pad] = -100.0
        bon[i] = b
    b = np.zeros((128, 128), np.float32)
    b[:, 0] = 1e6
    b[:, 127] = 1e6
    bon[16] = b
    c['bonus'] = bon
    c['ident_f'] = np.eye(128, dtype=np.float32)
    c['ident_b'] = np.eye(128, dtype=np.float32).astype(bf16)
    s = np.arange(128)[:, None]
    t = np.arange(128)[None, :]
    same = (s // 64) == (t // 64)
    c['tri'] = (same & (s <= t)).astype(np.float32)
    c['upr'] = (same & (s > t)).astype(np.float32)
    c['hmask'] = (same & (s <= t)).astype(np.float32)
    return c


CONST_SPECS = [
    ('kaugP', [4, 8192], BF16), ('kaugS', [4, 8320], BF16), ('caugP', [4, 512], BF16), ('caugS', [4, 512], BF16),
    ('qaug', [17, 4, 2, 4, 128], BF16), ('m_lo', [128, 128], BF16), ('m_hi', [128, 128], BF16),
    ('cmask', [4, 128, 128], BF16), ('emat32', [32, 8192], BF16), ('mmat', [128, 4, 128], BF16),
    ('bonus', [17, 128, 128], F32), ('ident_f', [128, 128], F32), ('ident_b', [128, 128], BF16),
    ('tri', [128, 128], F32), ('upr', [128, 128], F32), ('hmask', [128, 128], F32),
]

IN_SPECS = [
    ('xloc', [8192, D], F32), ('pown', [2048, 256], F32), ('xs', [4, D], F32), ('psm', [4, 256], F32),
    ('cache_cmp', [NPOOLROWS, 256], F32), ('cache_slc', [NPOOLROWS, 256], F32),
    ('cache_win', [4, 512, 256], F32), ('state0', [4, 4, 128, 128], F32), ('ptab', [4, 64], I32),
    ('w_in', [D, NCOL], F32), ('g_pre', [1, D], F32), ('cmp_pe', [2, 2048], F32), ('cmp_w1', [2, 2048, 128], F32),
    ('cmp_b1', [2, 128], F32), ('cmp_w2', [2, 128, 64], F32), ('hg_lower', [2, 512], F32), ('hg_norm', [1, 128], F32),
    ('w_out', [D, D], F32), ('g_post', [1, D], F32), ('ple_proj', [256, D], F32), ('ple_gate', [D, D], F32),
]

OUT_SPECS = [
    ('y_p', [2048, D], F32), ('y_s', [4, D], F32),
    ('cmp_p', [2048, 256], F32), ('slc_p', [2048, 256], F32), ('win_p', [128, 256], F32),
    ('hg_p', [4, 128, 128], F32),
    ('cmp_s', [4, 256], F32), ('slc_s', [4, 256], F32), ('win_s', [4, 512, 256], F32), ('hg_s', [4, 4, 128, 128], F32),
]


def build_nc(stage=99, n_tiles=NT):
    nc = bass.Bass("TRN2", target_bir_lowering=False)
    T = {}
    for name, shape, dt in IN_SPECS + CONST_SPECS:
        if stage < 4 and name in ('cache_cmp', 'cache_slc'):
            shape = [128, 256]
        T[name] = nc.dram_tensor(name, shape, dt, kind="ExternalInput").ap()
    for name, shape, dt in OUT_SPECS:
        T[name] = nc.dram_tensor(name, shape, dt, kind="ExternalOutput").ap()
    P = Prog()
    with contextlib.ExitStack() as st:
        st.enter_context(nc.allow_non_contiguous_dma(reason="tiny constant / layout loads"))

        def sb(name, shape, dt=F32):
            t = st.enter_context(nc.sbuf_tensor("sb_" + name, shape, dt))
            return t, P.buf(name)

        def psb(name, shape, dt=F32):
            t = st.enter_context(nc.psum_tensor("ps_" + name, shape, dt))
            b = P.buf(name)
            b.excl = True
            return t, b

        E = P.emit
        outbufs = []
        import os as _os
        dbg_i = int(_os.environ.get('KDBG', '-1'))
        cur = {'i': -2}
        dbg_b = P.buf("dbgout")
        outbufs.append(dbg_b)

        def tap(name, ap, buf, shape, dt=F32):
            if cur['i'] != dbg_i:
                return
            t = nc.dram_tensor("dbg_" + name, shape, dt, kind="ExternalOutput").ap()
            E("sync", lambda h: h.dma_start(out=t, in_=ap), reads=[buf], writes=[dbg_b], dma=True)

        ident_f, b_identf = sb("ident_f", [128, 128])
        ident_b, b_identb = sb("ident_b", [128, 128], BF16)
        tri, b_tri = sb("tri", [128, 128])
        upr, b_upr = sb("upr", [128, 128])
        hmask, b_hmask = sb("hmask", [128, 128])
        m_lo, b_mlo = sb("m_lo", [128, 128], BF16)
        m_hi, b_mhi = sb("m_hi", [128, 128], BF16)
        cmask, b_cmask = sb("cmask", [128, 4, 128], BF16)
        ones_f, b_onesf = sb("ones_f", [128, 128])
        eps_t, b_eps = sb("eps_t", [128, 1])
        for nm, t_, b_ in (("ident_f", ident_f, b_identf), ("ident_b", ident_b, b_identb), ("tri", tri, b_tri),
                           ("upr", upr, b_upr), ("hmask", hmask, b_hmask), ("m_lo", m_lo, b_mlo),
                           ("m_hi", m_hi, b_mhi)):
            E("sync", lambda h, t_=t_, nm=nm: h.dma_start(out=t_[:], in_=T[nm]), writes=[b_], dma=True)
        E("sync", lambda h: h.dma_start(out=cmask[:], in_=T['cmask'].rearrange("v p n -> p v n")), writes=[b_cmask], dma=True)
        E("vector", lambda h: h.memset(ones_f[:], 1.0), writes=[b_onesf])
        E("vector", lambda h: h.memset(eps_t[:], EPS), writes=[b_eps])

        stage_t = []
        for i in range(2):
            stage_t.append(sb(f"stage{i}", [128, 1024]))
        stage_ctr = [0]

        def next_stage():
            s = stage_t[stage_ctr[0] % 2]
            stage_ctr[0] += 1
            return s

        cast_ctr = [0]

        def cast_eng():
            cast_ctr[0] += 1
            return "vector" if cast_ctr[0] % 2 else "gpsimd"

        w_in_b, b_win = sb("w_in_b", [128, 8, NCOL], BF16)
        wch = [sb(f"wch{i}", [128, D], BF16) for i in range(3)]
        wch_ctr = [0]
        gcol, b_gcol = sb("gcol", [128, 8])
        E("sync", lambda h: h.dma_start(out=gcol[:], in_=T['g_pre'].rearrange("o (k p) -> p (o k)", p=128)), writes=[b_gcol], dma=True)
        w1_b, b_w1 = sb("w1_b", [128, 32, 128], BF16)
        w2_b, b_w2 = sb("w2_b", [128, 2, 64], BF16)
        w2_f, b_w2f = sb("w2_f", [128, 2, 64])

        def load_cast(dst, bdst, dst_sl, src_ap, width, scal=None):
            s_t, s_b = next_stage()
            E("sync", lambda h: h.dma_start(out=s_t[:, 0:width], in_=src_ap), writes=[s_b], dma=True)
            if scal is None:
                E(cast_eng(), lambda h: h.tensor_copy(out=dst_sl, in_=s_t[:, 0:width]), reads=[s_b], writes=[bdst])
            else:
                E("vector", lambda h: h.tensor_scalar(out=dst_sl, in0=s_t[:, 0:width], scalar1=scal, scalar2=None, op0=ALU.mult),
                  reads=[s_b, b_gcol], writes=[bdst])

        wscr = nc.dram_tensor("wscr", [18, 128, 1024], BF16, kind="Internal").ap()
        b_wscr = P.buf("wscr")
        wsrc = [T['w_out'][kc * 128:(kc + 1) * 128, :] for kc in range(8)] + \
               [T['ple_gate'][kc * 128:(kc + 1) * 128, :] for kc in range(8)] + \
               [T['ple_proj'][kc * 128:(kc + 1) * 128, :] for kc in range(2)]

        def stream_w(ci):
            w_t, w_b = wch[wch_ctr[0] % 3]
            wch_ctr[0] += 1
            E("sync", lambda h: h.dma_start(out=w_t[:, :], in_=wscr[ci]), reads=[b_wscr], writes=[w_b], dma=True)
            return w_t, w_b

        def init_wscr():
            for ci in range(18):
                w_t, w_b = wch[wch_ctr[0] % 3]
                wch_ctr[0] += 1
                load_cast(w_t, w_b, w_t[:, :], wsrc[ci], 1024)
                E("sync", lambda h: h.dma_start(out=wscr[ci], in_=w_t[:, :]), reads=[w_b], writes=[b_wscr], dma=True)

        for kc in range(8):
            for c0 in range(0, NCOL, 1024):
                wd = min(1024, NCOL - c0)
                load_cast(w_in_b, b_win, w_in_b[:, kc, c0:c0 + wd], T['w_in'][kc * 128:(kc + 1) * 128, c0:c0 + wd], wd, scal=gcol[:, kc:kc + 1])
        init_wscr()
        for kv in range(2):
            E("sync", lambda h, kv=kv: h.dma_start(out=w2_f[:, kv, :], in_=T['cmp_w2'][kv]), writes=[b_w2f], dma=True)
        E("vector", lambda h: h.tensor_copy(out=w2_b[:], in_=w2_f[:]), reads=[b_w2f], writes=[b_w2])

        g_post_bc, b_gpost = sb("g_post_bc", [128, D])
        lb_bc, b_lb = sb("lb_bc", [128, 512])
        hgn_bc, b_hgn = sb("hgn_bc", [128, 128])
        Wbig, _ = sb("Wbig", [128, 8, 512])
        Wflat = Wbig[:, :, :].rearrange("p a n -> p (a n)")
        b_sgt = P.buf("W01"); b_junk = P.buf("W23"); b_h1 = P.buf("W45"); b_w67 = P.buf("W67")
        hl1, b_hl1 = Wbig[:, 0, :], b_sgt
        E("sync", lambda h: h.dma_start(out=g_post_bc[:], in_=T['g_post'].partition_broadcast(128)), writes=[b_gpost], dma=True)
        E("sync", lambda h: h.dma_start(out=hgn_bc[:], in_=T['hg_norm'].partition_broadcast(128)), writes=[b_hgn], dma=True)
        E("sync", lambda h: h.dma_start(out=lb_bc[:], in_=T['hg_lower'][0:1, :].partition_broadcast(128)), writes=[b_lb], dma=True)
        E("sync", lambda h: h.dma_start(out=hl1, in_=T['hg_lower'][1:2, :].partition_broadcast(128)), writes=[b_hl1], dma=True)
        E("vector", lambda h: h.tensor_sub(out=lb_bc[:], in0=hl1, in1=lb_bc[:]), reads=[b_hl1, b_lb], writes=[b_lb])
        E("scalar", lambda h: h.activation(out=lb_bc[:], in_=lb_bc[:], func=AF.Exp), reads=[b_lb], writes=[b_lb])
        E("vector", lambda h: h.tensor_scalar_add(out=lb_bc[:], in0=lb_bc[:], scalar1=1.0), reads=[b_lb], writes=[b_lb])
        E("vector", lambda h: h.reciprocal(out=lb_bc[:], in_=lb_bc[:]), reads=[b_lb], writes=[b_lb])

        pool_banks = [psb(f"pp{i}", [128, 512]) for i in range(5)]
        po_banks = [psb(f"po{i}", [128, 512]) for i in range(3)]
        pool_ctr = [0]

        def bank():
            b = pool_banks[pool_ctr[0] % 5]
            pool_ctr[0] += 1
            return b

        pe_t, b_pe = sb("pe_t", [128, 32])
        b1e, b_b1e = sb("b1e", [128, 2])
        b1raw, b_b1raw = sb("b1raw", [128, 2])
        for kv in range(2):
            E("sync", lambda h, kv=kv: h.dma_start(out=pe_t[kv * 64:(kv + 1) * 64, :],
                                                   in_=T['cmp_pe'][kv].rearrange("(r d) -> d r", d=64)),
              writes=[b_pe], dma=True)
        E("sync", lambda h: h.dma_start(out=b1raw[:], in_=T['cmp_b1'].rearrange("k m -> m k")), writes=[b_b1raw], dma=True)
        pbs = [bank(), bank()]
        for rc in range(4):
            s_t, s_b = next_stage()
            for kv in range(2):
                E("sync", lambda h, kv=kv, rc=rc, s_t=s_t: h.dma_start(
                    out=s_t[kv * 64:(kv + 1) * 64, :].rearrange("p (r m) -> p r m", r=8),
                    in_=T['cmp_w1'][kv, rc * 512:(rc + 1) * 512, :].rearrange("(r d) m -> d r m", d=64)), writes=[s_b], dma=True)
            E("vector", lambda h, rc=rc, s_t=s_t: h.tensor_copy(out=w1_b[:, rc * 8:(rc + 1) * 8, :], in_=s_t[:, :].rearrange("p (r m) -> p r m", r=8)),
              reads=[s_b], writes=[b_w1])
            for kv in range(2):
                for r8 in range(8):
                    r = rc * 8 + r8
                    E("tensor", lambda h, kv=kv, r=r, r8=r8, s_t=s_t: h.matmul(pbs[kv][0][:, 0:1], lhsT=s_t[kv * 64:(kv + 1) * 64, r8 * 128:(r8 + 1) * 128],
                                                                rhs=pe_t[kv * 64:(kv + 1) * 64, r:r + 1],
                                                                start=(r == 0), stop=(r == 31)),
                      reads=[s_b, b_pe], writes=[pbs[kv][1]])
        for kv in range(2):
            E("vector", lambda h: h.tensor_add(out=b1e[:, kv:kv + 1], in0=pbs[kv][0][:, 0:1], in1=b1raw[:, kv:kv + 1]), reads=[pbs[kv][1], b_b1raw], writes=[b_b1e])

        kslcT, b_kslcT = sb("kslcT", [100, 2, 8192], BF16)
        vslc, b_vslc = sb("vslc", [128, 64, 2, 65], BF16)
        kwinT, b_kwinT = sb("kwinT", [100, 8, 2, 128], BF16)
        vwin, b_vwin = sb("vwin", [128, 8, 2, 65], BF16)
        knewT, b_knewT = sb("knewT", [100, 2, 2, 1], BF16)
        vnew, b_vnew = sb("vnew", [1, 2, 2, 65], BF16)
        kc_raw, b_kcraw = sb("kc_raw", [128, 2, 528], BF16)
        kcT, b_kcT = sb("kcT", [100, 2, 512], BF16)
        hv_all, b_hvall = sb("hv_all", [128, 2, 512], BF16)
        vm, b_vm = sb("vm", [128, 4, 2, 193], BF16)
        S_t, b_S = sb("S_state", [128, 4, 128])
        E("gpsimd", lambda h: h.memset(vslc[:], 1.0), writes=[b_vslc])
        E("gpsimd", lambda h: h.memset(vwin[:], 1.0), writes=[b_vwin])
        E("gpsimd", lambda h: h.memset(vnew[:], 1.0), writes=[b_vnew])
        E("gpsimd", lambda h: h.memset(vm[:], 1.0), writes=[b_vm])
        E("gpsimd", lambda h: h.memset(hv_all[:], 0.0), writes=[b_hvall])
        E("gpsimd", lambda h: h.memset(kc_raw[:], 0.0), writes=[b_kcraw])
        E("gpsimd", lambda h: h.memset(kcT[:], 0.0), writes=[b_kcT])
        E("gpsimd", lambda h: h.memset(kwinT[:], 0.0), writes=[b_kwinT])
        E("gpsimd", lambda h: h.memset(knewT[:], 0.0), writes=[b_knewT])
        for g in range(2):
            E("sync", lambda h: h.dma_start(out=kslcT[64:96, g, :], in_=T['emat32']), writes=[b_kslcT], dma=True)
        for g in range(2):
            E("sync", lambda h, g=g: h.dma_start(out=vm[:, :, g, 65:193], in_=T['mmat']), writes=[b_vm], dma=True)

        xn_b, b_xn = sb("xn_b", [128, D], BF16)
        xnT, b_xnT = sb("xnT", [128, 8, 128], BF16)
        ssq, b_ssq = sb("ssq", [128, 4])
        junk = Wflat[:, 1024:2048]
        zt_b, b_ztb = sb("zt_b", [128, 512], BF16)
        kv_f, b_kvf = Wflat[:, 3072:3840], b_w67
        hw = {"e1": (Wbig[:, 0, :], b_sgt), "f": (Wbig[:, 1, :], b_sgt), "logf": (Wbig[:, 2, :], b_junk), "kk": (Wbig[:, 3, :], b_junk),
              "eD": (Wbig[:, 4, :], b_h1), "qs": (Wbig[:, 5, :], b_h1)}
        Kd_b, b_Kd = sb("Kd_b", [128, 512], BF16)
        v_b, b_vb = sb("v_b", [128, 512], BF16)
        e1buf = [sb(f"e1buf{i}", [128, 512]) for i in range(2)]
        vbuf = [(v_b, b_vb), sb("v_b1", [128, 512], BF16)]
        hgc = {}
        dec_t, b_dec = sb("dec_t", [128, 8])
        S0_b, b_S0b = sb("S0_b", [128, 4, 128], BF16)
        S1_b, b_S1b = sb("S1_b", [128, 4, 128], BF16)
        Qt_b, b_Qtb = zt_b, b_ztb
        QTh, b_QTh = sb("QTh", [128, 4, 3, 128], BF16)
        KTh, b_KTh = sb("KTh", [128, 4, 128], BF16)
        AT_b, b_ATb = sb("AT_b", [128, 4, 128], BF16)
        Kt_b, b_Ktb = AT_b[:, :, :].rearrange("p a t -> p (a t)"), b_ATb
        E("gpsimd", lambda h: h.memset(QTh[:], 0.0), writes=[b_QTh])
        qa_b, b_qab = sb("qa_b", [128, 512], BF16)
        za_f, b_zaf = Wbig[:, 6, :], b_w67
        zb_f, b_zbf = Wbig[:, 7, :], b_w67
        gate_f, b_gatef = sb("gate_f", [128, 24])
        QT, b_QT = sb("QTq", [100, 4, 512], BF16)
        E("gpsimd", lambda h: h.memset(QT[:], 0.0), writes=[b_QT])
        nm_pad, b_nmp = sb("nm_pad", [128, 4, 96], BF16)
        E("gpsimd", lambda h: h.memset(nm_pad[:], 0.0), writes=[b_nmp])
        PT = [sb(f"PT{i}", [128, 512], BF16) for i in range(3)]
        pt_ctr = [0]
        acc, b_acc = Wbig[:, 2, :].rearrange("p (a d) -> p a d", a=8), b_junk
        rden, b_rden = sb("rden", [128, 8])
        ps_t, b_pst = Wbig[:, 3, 0:128], b_junk
        sc_t, b_sct = Wbig[:, 3, 128:256], b_junk
        bon_t, b_bont = Wbig[:, 3, 256:384], b_junk
        mx8, b_mx8 = sb("mx8", [128, 16])
        Y_b, b_Yb = xn_b, b_xn
        YT, b_YT = xnT, b_xnT
        h1 = Wflat[:, 2048:3072]
        p_b, b_pb = qa_b[:, 0:256], b_qab
        pT_b, b_pTb = AT_b[:, 0:2, :], b_ATb
        sg_t = Wflat[:, 0:1024]

        def rmsnorm_T(x_t, x_b, TW):
            E("scalar", lambda h: h.activation(out=junk[0:TW, :], in_=x_t[0:TW, :], func=AF.Square, accum_out=ssq[0:TW, 0:1]),
              reads=[x_b], writes=[b_junk, b_ssq])
            k2 = int(_os.environ.get('KCUT2', '99'))
            if k2 < -2:
                return
            E("scalar", lambda h: h.activation(out=ssq[0:TW, 0:1], in_=ssq[0:TW, 0:1], func=AF.Sqrt, scale=1.0 / D, bias=eps_t[0:TW, :]),
              reads=[b_ssq, b_eps], writes=[b_ssq])
            if k2 < -1:
                return
            E("vector", lambda h: h.reciprocal(out=ssq[0:TW, 0:1], in_=ssq[0:TW, 0:1]), reads=[b_ssq], writes=[b_ssq])
            if k2 < 0:
                return
            E("vector", lambda h: h.tensor_scalar(out=xn_b[0:TW, :], in0=x_t[0:TW, :], scalar1=ssq[0:TW, 0:1], scalar2=None, op0=ALU.mult),
              reads=[x_b, b_ssq], writes=[b_xn])
            if int(_os.environ.get('KCUT2', '99')) < 1:
                return
            transpose_k(xn_b, b_xn, xnT, b_xnT, TW, 8)

        def transpose_k(src, src_b, dst, dst_b, TW, nk):
            for k0 in range(0, nk, 4):
                kn = min(4, nk - k0)
                p_t, p_b_ = bank()
                for kk_ in range(kn):
                    kc = k0 + kk_
                    E("tensor", lambda h, kc=kc, kk_=kk_, p_t=p_t: h.matmul(p_t[:, kk_ * TW:(kk_ + 1) * TW], lhsT=src[0:TW, kc * 128:(kc + 1) * 128],
                                                                        rhs=ident_b[0:TW, 0:TW], start=True, stop=True),
                      reads=[src_b, b_identb], writes=[p_b_])
                eng = "vector" if (k0 // 4) % 2 == 0 else "scalar"
                if eng == "vector":
                    E("vector", lambda h, k0=k0, kn=kn, p_t=p_t: h.tensor_copy(out=dst[:, k0:k0 + kn, 0:TW], in_=p_t[:, 0:kn * TW].rearrange("p (k t) -> p k t", k=kn)),
                      reads=[p_b_], writes=[dst_b])
                else:
                    E("scalar", lambda h, k0=k0, kn=kn, p_t=p_t: h.activation(out=dst[:, k0:k0 + kn, 0:TW], in_=p_t[:, 0:kn * TW].rearrange("p (k t) -> p k t", k=kn), func=AF.Copy),
                      reads=[p_b_], writes=[dst_b])

        def proj(c0, width, TW):
            p_t, p_b_ = bank()
            for kc in range(8):
                E("tensor", lambda h, kc=kc: h.matmul(p_t[0:TW, 0:width], lhsT=xnT[:, kc, 0:TW], rhs=w_in_b[:, kc, c0:c0 + width],
                                                       start=(kc == 0), stop=(kc == 7)),
                  reads=[b_xnT, b_win], writes=[p_b_])
            return p_t, p_b_

        def transp(dst_ps, dst_b, src_ap, src_b, KW, M):
            E("tensor", lambda h: h.matmul(dst_ps, lhsT=src_ap, rhs=ident_b[0:KW, 0:KW], start=True, stop=True),
              reads=[src_b, b_identb], writes=[dst_b])

        def kv_from_tokmajor(TW, kslc_dst, kwin_dst, vslc_dst, vwin_dst, kcraw_off, own, kvout):
            p1, p1b = proj(O_A0, 512, TW)
            E("scalar", lambda h: h.activation(out=zt_b[0:TW, :], in_=p1[0:TW, :], func=AF.Copy), reads=[p1b], writes=[b_ztb])
            if own:
                E("vector", lambda h: h.tensor_copy(out=kv_f[0:TW, 0:512], in_=p1[0:TW, :]), reads=[p1b], writes=[b_kvf])
            k3 = int(_os.environ.get('KCUT3', '99'))
            if k3 < 1:
                return
            p2, p2b = proj(O_A2, 256, TW)
            E("vector", lambda h: h.tensor_copy(out=vslc_dst, in_=p2[0:TW, 0:128].rearrange("p (g d) -> p g d", g=2)),
              reads=[p2b], writes=[kv_from_tokmajor.vslc_b])
            E("vector", lambda h: h.tensor_copy(out=vwin_dst, in_=p2[0:TW, 128:256].rearrange("p (g d) -> p g d", g=2)),
              reads=[p2b], writes=[kv_from_tokmajor.vwin_b])
            if own:
                E("scalar", lambda h: h.activation(out=kv_f[0:TW, 512:768], in_=p2[0:TW, 0:256], func=AF.Copy), reads=[p2b], writes=[b_kvf])
            if k3 < 2:
                return
            pt_, ptb = bank()
            for q4 in range(4):
                transp(pt_[0:64, q4 * TW:(q4 + 1) * TW], ptb, zt_b[0:TW, 256 + q4 * 64:256 + (q4 + 1) * 64], b_ztb, TW, 64)
            E("vector", lambda h: h.tensor_copy(out=kslc_dst, in_=pt_[0:64, 0:2 * TW].rearrange("p (g t) -> p g t", g=2)),
              reads=[ptb], writes=[kv_from_tokmajor.kslc_b])
            E("scalar", lambda h: h.activation(out=kwin_dst, in_=pt_[0:64, 2 * TW:4 * TW].rearrange("p (g t) -> p g t", g=2), func=AF.Copy),
              reads=[ptb], writes=[kv_from_tokmajor.kwin_b])
            if k3 < 3:
                return
            if kcraw_off is not None:
                pc_, pcb = bank()
                for g in range(2):
                    transp(pc_[:, g * TW:(g + 1) * TW], pcb, zt_b[0:TW, g * 128:(g + 1) * 128], b_ztb, TW, 128)
                E("vector", lambda h: h.tensor_copy(out=kc_raw[:, :, kcraw_off:kcraw_off + TW],
                                                    in_=pc_[:, 0:2 * TW].rearrange("p (g t) -> p g t", g=2)),
                  reads=[pcb], writes=[b_kcraw])
            if own and kvout is not None:
                cmp_o, slc_o, win_o = kvout
                for s_ in range(2):
                    for g_ in range(2):
                        E("sync", lambda h, s_=s_, g_=g_: h.dma_start(out=cmp_o[:, (s_ * 2 + g_) * 64:(s_ * 2 + g_ + 1) * 64],
                                                                      in_=kv_f[0:TW, (g_ * 2 + s_) * 64:(g_ * 2 + s_ + 1) * 64]),
                          reads=[b_kvf], writes=[kv_from_tokmajor.ob], dma=True)
                E("sync", lambda h: h.dma_start(out=slc_o[:, 0:128], in_=kv_f[0:TW, 256:384]), reads=[b_kvf], writes=[kv_from_tokmajor.ob], dma=True)
                E("sync", lambda h: h.dma_start(out=slc_o[:, 128:256], in_=kv_f[0:TW, 512:640]), reads=[b_kvf], writes=[kv_from_tokmajor.ob], dma=True)
                if win_o is not None:
                    E("sync", lambda h: h.dma_start(out=win_o[:, 0:128], in_=kv_f[0:TW, 384:512]), reads=[b_kvf], writes=[kv_from_tokmajor.ob], dma=True)
                    E("sync", lambda h: h.dma_start(out=win_o[:, 128:256], in_=kv_f[0:TW, 640:768]), reads=[b_kvf], writes=[kv_from_tokmajor.ob], dma=True)

        kv_from_tokmajor.vslc_b = b_vslc
        kv_from_tokmajor.vwin_b = b_vwin
        kv_from_tokmajor.kslc_b = b_kslcT
        kv_from_tokmajor.kwin_b = b_kwinT
        kv_from_tokmajor.ob = P.buf("kvout")
        outbufs.append(kv_from_tokmajor.ob)

        def sigmoid_from(dst, dst_b, src, src_b, TW, W, eng2="vector"):
            E("scalar", lambda h: h.activation(out=dst, in_=src, func=AF.Sigmoid), reads=[src_b], writes=[dst_b])

        def compress_group(gi):
            phs = [bank(), bank()]
            for g in range(2):
                for kv in range(2):
                    col = (g * 2 + kv) * 32
                    ph, phb = phs[kv]
                    for r in range(32):
                        E("tensor", lambda h, g=g, kv=kv, r=r, col=col: h.matmul(
                            ph[:, col:col + 32], lhsT=w1_b[kv * 64:(kv + 1) * 64, r, :],
                            rhs=kc_raw[kv * 64:(kv + 1) * 64, g, r:r + 497:16], start=(r == 0), stop=(r == 31)),
                          reads=[b_w1, b_kcraw], writes=[phb])
            u_t, u_b = hw["e1"]
            s_t, s_b = hw["f"]
            H_t, H_b = Kd_b, b_Kd
            for kv in range(2):
                ph, phb = phs[kv]
                for g in range(2):
                    col = (g * 2 + kv) * 32
                    E("vector", lambda h, col=col, kv=kv: h.tensor_scalar(out=u_t[:, col:col + 32], in0=ph[:, col:col + 32],
                                                                        scalar1=b1e[:, kv:kv + 1], scalar2=None, op0=ALU.add),
                      reads=[phb, b_b1e], writes=[u_b])
            sigmoid_from(s_t[:, 0:128], s_b, u_t[:, 0:128], u_b, 128, 128)
            E("vector", lambda h: h.tensor_mul(out=H_t[:, 0:128], in0=u_t[:, 0:128], in1=s_t[:, 0:128]), reads=[u_b, s_b], writes=[H_b])
            pk, pkb = bank()
            for g in range(2):
                col = (g * 2 + 0) * 32
                E("tensor", lambda h, g=g, col=col: h.matmul(pk[0:64, g * 32:(g + 1) * 32], lhsT=w2_b[:, 0, :], rhs=H_t[:, col:col + 32],
                                                             start=True, stop=True), reads=[b_w2, H_b], writes=[pkb])
            E("vector", lambda h: h.tensor_copy(out=kcT[0:64, :, 32 * gi:32 * gi + 32], in_=pk[0:64, 0:64].rearrange("p (g c) -> p g c", g=2)),
              reads=[pkb], writes=[b_kcT])
            for g in range(2):
                col = (g * 2 + 1) * 32
                E("gpsimd", lambda h, g=g, col=col: h.tensor_copy(out=hv_all[:, g, 32 * gi:32 * gi + 32], in_=H_t[:, col:col + 32]),
                  reads=[H_b], writes=[b_hvall])
            E("vector", lambda h: h.tensor_copy(out=kc_raw[:, :, 0:16], in_=kc_raw[:, :, 512:528]), reads=[b_kcraw], writes=[b_kcraw])

        def vc_tile(ktc):
            pv, pvb = bank()
            for g in range(2):
                E("tensor", lambda h, g=g: h.matmul(pv[:, g * 64:(g + 1) * 64], lhsT=hv_all[:, g, ktc * 128:(ktc + 1) * 128], rhs=w2_b[:, 1, :],
                                                    start=True, stop=True), reads=[b_hvall, b_w2], writes=[pvb])
            E("vector", lambda h: h.tensor_copy(out=vm[:, ktc, :, 0:64], in_=pv[:, 0:128].rearrange("p (g d) -> p g d", g=2)),
              reads=[pvb], writes=[b_vm])

        def hgrn_tile(TW, own, e1b, vb):
            e1, be1 = e1b
            v_b, b_vb = vb
            hgc['e1'] = e1b
            hgc['vb'] = vb
            f_t, bf_ = hw["f"]
            lg, blg = hw["logf"]
            kk, bkk = hw["kk"]
            eD, beD = hw["eD"]
            nch = 2 if TW == 128 else 1
            CW = 64 if TW == 128 else 1
            E("vector", lambda h: h.tensor_scalar(out=f_t[0:TW, :], in0=e1[0:TW, :], scalar1=-1.0, scalar2=1.0, op0=ALU.mult, op1=ALU.add),
              reads=[be1], writes=[bf_])
            E("vector", lambda h: h.tensor_mul(out=f_t[0:TW, :], in0=f_t[0:TW, :], in1=lb_bc[0:TW, :]), reads=[bf_, b_lb], writes=[bf_])
            E("vector", lambda h: h.tensor_add(out=f_t[0:TW, :], in0=f_t[0:TW, :], in1=e1[0:TW, :]), reads=[bf_, be1], writes=[bf_])
            E("scalar", lambda h: h.activation(out=lg[0:TW, :], in_=f_t[0:TW, :], func=AF.Ln), reads=[bf_], writes=[blg])
            E("gpsimd", lambda h: h.tensor_scalar(out=kk[0:TW, :], in0=f_t[0:TW, :], scalar1=-1.0, scalar2=1.0, op0=ALU.mult, op1=ALU.add),
              reads=[bf_], writes=[bkk])
            pD, pDb = bank()
            E("tensor", lambda h: h.matmul(pD[0:TW, :], lhsT=upr[0:TW, 0:TW], rhs=lg[0:TW, :], start=True, stop=True),
              reads=[b_upr, blg], writes=[pDb])
            E("scalar", lambda h: h.activation(out=eD[0:TW, :], in_=pD[0:TW, :], func=AF.Exp), reads=[pDb], writes=[beD])
            E("vector", lambda h: h.tensor_mul(out=Kd_b[0:TW, :], in0=kk[0:TW, :], in1=eD[0:TW, :]), reads=[bkk, beD], writes=[b_Kd])
            for c in range(nch):
                pd, pdb = bank()
                for hh in range(4):
                    E("tensor", lambda h, c=c, hh=hh: h.matmul(pd[:, hh:hh + 1], lhsT=lg[c * CW:(c + 1) * CW, hh * 128:(hh + 1) * 128],
                                                                 rhs=ones_f[c * CW:(c + 1) * CW, 0:1], start=True, stop=True),
                      reads=[blg, b_onesf], writes=[pdb])
                E("scalar", lambda h: h.activation(out=dec_t[:, 4 * c:4 * c + 4], in_=pd[:, 0:4], func=AF.Exp), reads=[pdb], writes=[b_dec])
            if own:
                E("gpsimd", lambda h: h.tensor_copy(out=S0_b[:], in_=S_t[:]), reads=[b_S], writes=[b_S0b])
            for c in range(nch):
                pu, pub = bank()
                for hh in range(4):
                    E("tensor", lambda h, c=c, hh=hh: h.matmul(pu[:, hh * 128:(hh + 1) * 128], lhsT=Kd_b[c * CW:(c + 1) * CW, hh * 128:(hh + 1) * 128],
                                                                 rhs=v_b[c * CW:(c + 1) * CW, hh * 128:(hh + 1) * 128], start=True, stop=True),
                      reads=[b_Kd, b_vb], writes=[pub])
                E("vector", lambda h, c=c: h.tensor_mul(out=S_t[:], in0=S_t[:], in1=dec_t[:, c * 4:(c + 1) * 4].unsqueeze(2).to_broadcast([128, 4, 128])),
                  reads=[b_S, b_dec], writes=[b_S])
                E("vector", lambda h: h.tensor_add(out=S_t[:], in0=S_t[:], in1=pu[:, :].rearrange("p (a v) -> p a v", a=4)),
                  reads=[b_S, pub], writes=[b_S])
                if own and c == 0 and nch == 2:
                    E("gpsimd", lambda h: h.tensor_copy(out=S1_b[:], in_=S_t[:]), reads=[b_S], writes=[b_S1b])
            return None

        def hgrn_out(TW):
            e1, be1 = hgc['e1']
            v_b, b_vb = hgc['vb']
            lg, blg = hw["logf"]
            kk, bkk = hw["kk"]
            eD, beD = hw["eD"]
            nch = 2 if TW == 128 else 1
            CW = 64 if TW == 128 else 1
            qs, bqs = hw["qs"]
            pB, pBb = bank()
            E("tensor", lambda h: h.matmul(pB[0:TW, :], lhsT=tri[0:TW, 0:TW], rhs=lg[0:TW, :], start=True, stop=True),
              reads=[b_tri, blg], writes=[pBb])
            E("scalar", lambda h: h.activation(out=eD[0:TW, :], in_=pB[0:TW, :], func=AF.Exp), reads=[pBb], writes=[beD])
            E("vector", lambda h: h.tensor_mul(out=Qt_b[0:TW, :], in0=qs[0:TW, :], in1=eD[0:TW, :]), reads=[bqs, beD], writes=[b_Qtb])
            E("scalar", lambda h: h.activation(out=e1[0:TW, :], in_=pB[0:TW, :], func=AF.Exp, scale=-1.0), reads=[pBb], writes=[be1])
            E("vector", lambda h: h.tensor_mul(out=Kt_b[0:TW, :], in0=kk[0:TW, :], in1=e1[0:TW, :]), reads=[bkk, be1], writes=[b_Ktb])
            pq_, pqb_ = bank()
            pk_, pkb_ = bank()
            for hh in range(4):
                transp(pq_[:, hh * TW:(hh + 1) * TW], pqb_, Qt_b[0:TW, hh * 128:(hh + 1) * 128], b_Qtb, TW, 128)
                transp(pk_[:, hh * TW:(hh + 1) * TW], pkb_, Kt_b[0:TW, hh * 128:(hh + 1) * 128], b_Ktb, TW, 128)
            E("vector", lambda h: h.tensor_copy(out=QTh[:, :, 0, 0:TW], in_=pq_[:, 0:4 * TW].rearrange("p (a t) -> p a t", a=4)),
              reads=[pqb_], writes=[b_QTh])
            E("gpsimd", lambda h: h.tensor_copy(out=QTh[:, :, 1, 0:CW], in_=QTh[:, :, 0, 0:CW]), reads=[b_QTh], writes=[b_QTh])
            if nch == 2:
                E("gpsimd", lambda h: h.tensor_copy(out=QTh[:, :, 2, 64:128], in_=QTh[:, :, 0, 64:128]), reads=[b_QTh], writes=[b_QTh])
            E("scalar", lambda h: h.activation(out=KTh[:, :, 0:TW], in_=pk_[:, 0:4 * TW].rearrange("p (a t) -> p a t", a=4), func=AF.Copy),
              reads=[pkb_], writes=[b_KTh])
            pa, pab = bank()
            for hh in range(4):
                E("tensor", lambda h, hh=hh: h.matmul(pa[0:TW, hh * TW:(hh + 1) * TW], lhsT=KTh[:, hh, 0:TW], rhs=QTh[:, hh, 0, 0:TW], start=True, stop=True),
                  reads=[b_KTh, b_QTh], writes=[pab])
            E("vector", lambda h: h.tensor_mul(out=AT_b[0:TW, :, 0:TW], in0=pa[0:TW, 0:4 * TW].rearrange("p (a t) -> p a t", a=4),
                                               in1=hmask[0:TW, 0:TW].unsqueeze(1).to_broadcast([TW, 4, TW])),
              reads=[pab, b_hmask], writes=[b_ATb])
            po_, pob = bank()
            for hh in range(4):
                E("tensor", lambda h, hh=hh: h.matmul(po_[0:TW, hh * 128:(hh + 1) * 128], lhsT=AT_b[0:TW, hh, 0:TW], rhs=v_b[0:TW, hh * 128:(hh + 1) * 128],
                                                      start=True, stop=False), reads=[b_ATb, b_vb], writes=[pob])
                E("tensor", lambda h, hh=hh: h.matmul(po_[0:TW, hh * 128:(hh + 1) * 128], lhsT=QTh[:, hh, 1, 0:TW], rhs=S0_b[:, hh, :],
                                                      start=False, stop=(nch == 1)), reads=[b_QTh, b_S0b], writes=[pob])
                if nch == 2:
                    E("tensor", lambda h, hh=hh: h.matmul(po_[0:TW, hh * 128:(hh + 1) * 128], lhsT=QTh[:, hh, 2, 0:TW], rhs=S1_b[:, hh, :],
                                                          start=False, stop=True), reads=[b_QTh, b_S1b], writes=[pob])
            if cur['i'] == dbg_i:
                E("vector", lambda h: h.tensor_copy(out=qs[0:TW, :], in_=po_[0:TW, :]), reads=[pob], writes=[bqs])
                tap("ob", qs[0:TW, :], bqs, [128, 512])
            tap("S", S_t[:], b_S, [128, 4, 128])
            tap("Qt", Qt_b[0:TW, :], b_Qtb, [128, 512], BF16)
            tap("Kt", Kt_b[0:TW, :], b_Ktb, [128, 512], BF16)
            tap("AT", AT_b[0:TW, :, :], b_ATb, [128, 4, 128], BF16)
            tap("lg", lg[0:TW, :], blg, [128, 512])
            for hh in range(4):
                E("scalar", lambda h: h.activation(out=eD[0:TW, hh * 128:(hh + 1) * 128], in_=po_[0:TW, hh * 128:(hh + 1) * 128], func=AF.Square,
                                                   accum_out=ssq[0:TW, hh:hh + 1]), reads=[pob], writes=[beD, b_ssq])
            E("scalar", lambda h: h.activation(out=ssq[0:TW, 0:4], in_=ssq[0:TW, 0:4], func=AF.Sqrt, scale=1.0 / 128, bias=eps_t[0:TW, :]),
              reads=[b_ssq, b_eps], writes=[b_ssq])
            E("vector", lambda h: h.reciprocal(out=ssq[0:TW, 0:4], in_=ssq[0:TW, 0:4]), reads=[b_ssq], writes=[b_ssq])
            for hh in range(4):
                E("vector", lambda h: h.scalar_tensor_tensor(out=qs[0:TW, hh * 128:(hh + 1) * 128], in0=po_[0:TW, hh * 128:(hh + 1) * 128],
                                                             scalar=ssq[0:TW, hh:hh + 1], in1=hgn_bc[0:TW, :], op0=ALU.mult, op1=ALU.mult),
                  reads=[pob, b_ssq, b_hgn], writes=[bqs])
            return qs, bqs

        def attn_steps(TW, g, steps, out_banks, out_w):
            NQ = 4 * TW
            hpb = 2 if out_w > 128 else 4
            ns = len(steps)
            def emit_scores(sp):
                KW = sp['KW']
                s_t, s_b = bank()
                nmm = 1 + len(sp['masks'])
                perhead = any(mr is None for (_, mr, _) in sp['masks'])
                regions = [(hh * TW, (hh + 1) * TW, hh) for hh in range(4)] if perhead else [(0, NQ, None)]
                for (c0_, c1_, hh) in regions:
                    qt_ = sp.get('qt', 0)
                    if hh is None:
                        qrhs = QT[:, qt_, 0:NQ]
                    else:
                        qrhs = QT[:, qt_, hh * TW:(hh + 1) * TW]
                    E("tensor", lambda h: h.matmul(s_t[0:KW, c0_:c1_], lhsT=sp['lhsT'][0], rhs=qrhs, start=True, stop=(nmm == 1)),
                      reads=[sp['lhsT'][1], b_QT], writes=[s_b])
                    for mi, (ml, mr, mbufs) in enumerate(sp['masks']):
                        last = (mi == nmm - 2)
                        if mr is None:
                            E("tensor", lambda h: h.matmul(s_t[0:KW, c0_:c1_], lhsT=ml[0], rhs=ml[1], start=False, stop=last),
                              reads=list(mbufs), writes=[s_b])
                        else:
                            E("tensor", lambda h: h.matmul(s_t[0:KW, c0_:c1_], lhsT=ml, rhs=mr[:, c0_:c1_], start=False, stop=last),
                              reads=list(mbufs), writes=[s_b])
                return s_t, s_b

            def emit_exp_pv(si, sp, s_t, s_b):
                KW = sp['KW']
                p_t, p_b_ = PT[pt_ctr[0] % 3]
                pt_ctr[0] += 1
                E("scalar", lambda h: h.activation(out=p_t[0:KW, 0:NQ], in_=s_t[0:KW, 0:NQ], func=AF.Exp), reads=[s_b], writes=[p_b_])
                for hh in range(4):
                    ob_t, ob_b = out_banks[hh // hpb]
                    off = (hh % hpb) * out_w
                    E("tensor", lambda h: h.matmul(ob_t[0:TW, off:off + out_w], lhsT=p_t[0:KW, hh * TW:(hh + 1) * TW], rhs=sp['pv'][0],
                                                   start=(si == 0 and (hh % hpb) == 0), stop=(si == ns - 1), skip_group_check=True),
                      reads=[p_b_, sp['pv'][1]], writes=[ob_b])

            pendq = [emit_scores(steps[k_]) for k_ in range(min(2, ns))]
            for si, sp in enumerate(steps):
                if si + 2 < ns:
                    pendq.append(emit_scores(steps[si + 2]))
                emit_exp_pv(si, sp, *pendq.pop(0))

        def attention(TW, oi, n_keyt_slc, win_slots, cmp_nkt, cmp_maskv, is_sample):
            sigmoid_from(gate_f[0:TW, :], b_gatef, gate_f[0:TW, :], b_gatef, TW, 24)
            E("sync", lambda h: h.dma_start(out=bon_t[:], in_=T['bonus'][oi]), writes=[b_bont], dma=True)
            NQ_ = 4 * TW

            def load_QT(g_):
                pq_, pqb_ = bank()
                for hd in range(4):
                    transp(pq_[0:64, hd * TW:(hd + 1) * TW], pqb_, qa_b[0:TW, (g_ * 4 + hd) * 64:(g_ * 4 + hd + 1) * 64], b_qab, TW, 64)
                E("vector", lambda h: h.tensor_copy(out=QT[0:64, :, 0:NQ_], in_=pq_[0:64, 0:NQ_].unsqueeze(1).to_broadcast([64, 4, NQ_])),
                  reads=[pqb_], writes=[b_QT])
                for qt_ in range(4):
                    E("sync", lambda h: h.dma_start(out=QT[96:100, qt_, 0:NQ_].rearrange("p (a t) -> p a t", a=4), in_=T['qaug'][oi][:, g_, :, 0:TW]),
                      writes=[b_QT], dma=True)

            def combine(g, banks, out_w, gidx):
                hpb = 2 if out_w > 128 else 4
                gate3 = gate_f[0:TW, :].rearrange("p (a x) -> p a x", x=3)
                for bi_, (ob_t, ob_b) in enumerate(banks):
                    hg0 = 4 * g + bi_ * hpb
                    ob3 = ob_t[0:TW, 0:hpb * out_w].rearrange("p (a w) -> p a w", a=hpb)
                    rv = rden[0:TW, hg0:hg0 + hpb].unsqueeze(2)
                    E("vector", lambda h: h.tensor_scalar_max(out=rv, in0=ob3[:, :, 64:65], scalar1=1e-30), reads=[ob_b], writes=[b_rden])
                    E("vector", lambda h: h.reciprocal(out=rv, in_=rv), reads=[b_rden], writes=[b_rden])
                    E("vector", lambda h: h.tensor_mul(out=rv, in0=rv, in1=gate3[:, hg0:hg0 + hpb, gidx:gidx + 1]), reads=[b_rden, b_gatef], writes=[b_rden])
                    av = acc[0:TW, hg0:hg0 + hpb, :]
                    rbc = rden[0:TW, hg0:hg0 + hpb].unsqueeze(2).to_broadcast([TW, hpb, 64])
                    if gidx == 0:
                        E("vector", lambda h: h.tensor_mul(out=av, in0=ob3[:, :, 0:64], in1=rbc), reads=[ob_b, b_rden], writes=[b_acc])
                    else:
                        tmpv = Wbig[0:TW, 0, 0:hpb * 64].rearrange("p (a d) -> p a d", a=hpb)
                        E("vector", lambda h: h.tensor_mul(out=tmpv, in0=ob3[:, :, 0:64], in1=rbc), reads=[ob_b, b_rden], writes=[b_sgt])
                        E("vector", lambda h: h.tensor_add(out=av, in0=av, in1=tmpv), reads=[b_acc, b_sgt], writes=[b_acc])

            for g in range(2):
                load_QT(g)
                steps = []
                for kt in range(cmp_nkt):
                    masks = []
                    if cmp_maskv is not None and kt == cmp_nkt - 1:
                        masks.append(((ident_b[:, :], cmask[:, cmp_maskv, 0:TW]), None, (b_identb, b_cmask)))
                    steps.append(dict(lhsT=(kcT[:, g, kt * 128:(kt + 1) * 128], b_kcT), KW=128, masks=masks,
                                      pv=(vm[:, kt, g, :], b_vm)))
                attn_steps(TW, g, steps, po_banks[0:2], 193)
                steps = []
                for slot, mk in win_slots:
                    masks = []
                    if mk == 'hi':
                        masks.append(((ident_b[:, :], m_hi[:, 0:TW]), None, (b_identb, b_mhi)))
                    elif mk == 'lo':
                        masks.append(((ident_b[:, :], m_lo[:, 0:TW]), None, (b_identb, b_mlo)))
                    steps.append(dict(lhsT=(kwinT[:, slot, g, :], b_kwinT), KW=128, masks=masks, pv=(vwin[:, slot, g, :], b_vwin)))
                if is_sample:
                    steps.append(dict(lhsT=(knewT[:, 1, g, :], b_knewT), KW=1, masks=[], pv=(vnew[0:1, 1, g, :], b_vnew)))
                attn_steps(TW, g, steps, [po_banks[2]], 65)
                for hh in range(4):
                    ob_t, ob_b = po_banks[hh // 2]
                    off = (hh % 2) * 193
                    E("vector", lambda h, ob_t=ob_t, off=off, hh=hh: h.tensor_scalar_max(out=rden[0:TW, 4 * g + hh:4 * g + hh + 1], in0=ob_t[0:TW, off + 64:off + 65], scalar1=1e-30),
                      reads=[ob_b], writes=[b_rden])
                    E("vector", lambda h, hh=hh: h.reciprocal(out=rden[0:TW, 4 * g + hh:4 * g + hh + 1], in_=rden[0:TW, 4 * g + hh:4 * g + hh + 1]),
                      reads=[b_rden], writes=[b_rden])
                    if hh == 0:
                        E("vector", lambda h, ob_t=ob_t, off=off: h.tensor_scalar(out=ps_t[0:TW, :], in0=ob_t[0:TW, off + 65:off + 193],
                                                                               scalar1=rden[0:TW, 4 * g:4 * g + 1], scalar2=None, op0=ALU.mult),
                          reads=[ob_b, b_rden], writes=[b_pst])
                    else:
                        E("vector", lambda h, ob_t=ob_t, off=off, hh=hh: h.scalar_tensor_tensor(out=ps_t[0:TW, :], in0=ob_t[0:TW, off + 65:off + 193],
                                                                                             scalar=rden[0:TW, 4 * g + hh:4 * g + hh + 1], in1=ps_t[0:TW, :],
                                                                                             op0=ALU.mult, op1=ALU.add),
                          reads=[ob_b, b_rden, b_pst], writes=[b_pst])
                combine(g, po_banks[0:2], 193, 0)
                tap(f"acc_c{g}", acc[0:TW, :, :], b_acc, [128, 8, 64])
                tap(f"ps{g}", ps_t[0:TW, :], b_pst, [128, 128])
                E("vector", lambda h: h.tensor_add(out=ps_t[0:TW, :], in0=ps_t[0:TW, :], in1=bon_t[0:TW, :]), reads=[b_pst, b_bont], writes=[b_pst])
                E("vector", lambda h: h.max(out=mx8[0:TW, 0:8], in_=ps_t[0:TW, :]), reads=[b_pst], writes=[b_mx8])
                E("vector", lambda h: h.match_replace(out=sc_t[0:TW, :], in_to_replace=mx8[0:TW, 0:8], in_values=ps_t[0:TW, :], imm_value=-1e9),
                  reads=[b_pst, b_mx8], writes=[b_sct])
                E("vector", lambda h: h.max(out=mx8[0:TW, 8:16], in_=sc_t[0:TW, :]), reads=[b_sct], writes=[b_mx8])
                kth = 14 if is_sample else 15
                E("vector", lambda h: h.tensor_scalar(out=sc_t[0:TW, :], in0=ps_t[0:TW, :], scalar1=mx8[0:TW, kth:kth + 1], scalar2=None, op0=ALU.is_ge),
                  reads=[b_pst, b_mx8], writes=[b_sct])
                E("vector", lambda h: h.tensor_scalar(out=ps_t[0:TW, :], in0=ps_t[0:TW, :], scalar1=-50.0, scalar2=None, op0=ALU.is_gt),
                  reads=[b_pst], writes=[b_pst])
                E("vector", lambda h: h.tensor_mul(out=sc_t[0:TW, :], in0=sc_t[0:TW, :], in1=ps_t[0:TW, :]), reads=[b_sct, b_pst], writes=[b_sct])
                E("vector", lambda h: h.tensor_scalar(out=nm_pad[0:TW, :, 64:96], in0=sc_t[0:TW, :].rearrange("p (a c) -> p a c", a=4),
                                                      scalar1=-1.0, scalar2=-NEGM, op0=ALU.add, op1=ALU.mult),
                  reads=[b_sct], writes=[b_nmp])
                pn, pnb = bank()
                for qt_ in range(4):
                    transp(pn[0:96, qt_ * TW:(qt_ + 1) * TW], pnb, nm_pad[0:TW, qt_, :], b_nmp, TW, 96)
                E("vector", lambda h: h.tensor_copy(out=QT[64:96, :, 0:4 * TW].rearrange("p c (a t) -> p c a t", a=4),
                                                    in_=pn[64:96, 0:4 * TW].rearrange("p (c t) -> p c t", c=4).unsqueeze(2).to_broadcast([32, 4, 4, TW])),
                  reads=[pnb], writes=[b_QT])
                combine(g, [po_banks[2]], 65, 2)
                steps = []
                for kt in range(n_keyt_slc):
                    masks = []
                    if (not is_sample) and kt == n_keyt_slc - 1:
                        masks.append(((ident_b[:, :], m_lo[:, 0:TW]), None, (b_identb, b_mlo)))
                    steps.append(dict(lhsT=(kslcT[:, g, kt * 128:(kt + 1) * 128], b_kslcT), KW=128, masks=masks, qt=kt // 16,
                                      pv=(vslc[:, kt, g, :], b_vslc)))
                if is_sample:
                    steps.append(dict(lhsT=(knewT[:, 0, g, :], b_knewT), KW=1, masks=[], pv=(vnew[0:1, 0, g, :], b_vnew)))
                attn_steps(TW, g, steps, [po_banks[0]], 65)
                combine(g, [po_banks[0]], 65, 1)
                tap(f"acc_s{g}", acc[0:TW, :, :], b_acc, [128, 8, 64])

        def finish(TW, x_t, x_b, po_hg, p_src_ap, y_out_ap):
            sg, bsg = sg_t, b_sgt
            sigmoid_from(sg[0:TW, 0:512], bsg, za_f[0:TW, :], b_zaf, TW, 512)
            E("vector", lambda h: h.tensor_mul(out=sg[0:TW, 0:512], in0=sg[0:TW, 0:512], in1=za_f[0:TW, :]), reads=[bsg, b_zaf], writes=[bsg])
            E("vector", lambda h: h.tensor_mul(out=Y_b[0:TW, 0:512], in0=sg[0:TW, 0:512], in1=acc[0:TW, :, :].rearrange("p a d -> p (a d)")),
              reads=[bsg, b_acc], writes=[b_Yb])
            pot, pob = po_hg
            sigmoid_from(sg[0:TW, 512:1024], bsg, zb_f[0:TW, :], b_zbf, TW, 512)
            E("vector", lambda h: h.tensor_mul(out=sg[0:TW, 512:1024], in0=sg[0:TW, 512:1024], in1=zb_f[0:TW, :]), reads=[bsg, b_zbf], writes=[bsg])
            E("vector", lambda h: h.tensor_mul(out=Y_b[0:TW, 512:1024], in0=pot[0:TW, :], in1=sg[0:TW, 512:1024]), reads=[pob, bsg], writes=[b_Yb])
            tap("Y", Y_b[0:TW, :], b_Yb, [128, 1024], BF16)
            tap("gate", gate_f[0:TW, :], b_gatef, [128, 24])
            transpose_k(Y_b, b_Yb, YT, b_YT, TW, 8)
            pm = [bank(), bank()]
            for kc in range(8):
                w_t, w_b = stream_w(kc)
                for hf in range(2):
                    E("tensor", lambda h, hf=hf, kc=kc, w_t=w_t: h.matmul(pm[hf][0][0:TW, :], lhsT=YT[:, kc, 0:TW], rhs=w_t[:, hf * 512:(hf + 1) * 512],
                                                                  start=(kc == 0), stop=(kc == 7)), reads=[b_YT, w_b], writes=[pm[hf][1]])
            for hf in range(2):
                E("scalar", lambda h, hf=hf: h.activation(out=junk[0:TW, hf * 512:(hf + 1) * 512], in_=pm[hf][0][0:TW, :], func=AF.Square,
                                                          accum_out=ssq[0:TW, hf:hf + 1]), reads=[pm[hf][1]], writes=[b_junk, b_ssq])
            E("vector", lambda h: h.tensor_add(out=ssq[0:TW, 0:1], in0=ssq[0:TW, 0:1], in1=ssq[0:TW, 1:2]), reads=[b_ssq], writes=[b_ssq])
            E("scalar", lambda h: h.activation(out=ssq[0:TW, 0:1], in_=ssq[0:TW, 0:1], func=AF.Sqrt, scale=1.0 / D, bias=eps_t[0:TW, :]),
              reads=[b_ssq, b_eps], writes=[b_ssq])
            E("vector", lambda h: h.reciprocal(out=ssq[0:TW, 0:1], in_=ssq[0:TW, 0:1]), reads=[b_ssq], writes=[b_ssq])
            for hf in range(2):
                E("vector", lambda h, hf=hf: h.scalar_tensor_tensor(out=h1[0:TW, hf * 512:(hf + 1) * 512], in0=pm[hf][0][0:TW, :], scalar=ssq[0:TW, 0:1],
                                                                    in1=g_post_bc[0:TW, hf * 512:(hf + 1) * 512], op0=ALU.mult, op1=ALU.mult),
                  reads=[pm[hf][1], b_ssq, b_gpost], writes=[b_h1])
            x2_t, x2_b = next_stage()
            E("sync", lambda h: h.dma_start(out=x2_t[0:TW, :], in_=x_t), writes=[x2_b], dma=True)
            E("gpsimd", lambda h: h.tensor_add(out=h1[0:TW, :], in0=h1[0:TW, :], in1=x2_t[0:TW, :]), reads=[b_h1, x2_b], writes=[b_h1])
            tap("h1", h1[0:TW, :], b_h1, [128, 1024])
            E("vector", lambda h: h.tensor_copy(out=Y_b[0:TW, :], in_=h1[0:TW, :]), reads=[b_h1], writes=[b_Yb])
            transpose_k(Y_b, b_Yb, YT, b_YT, TW, 8)
            E("sync", lambda h: h.dma_start(out=junk[0:TW, 0:256], in_=p_src_ap), writes=[b_junk], dma=True)
            E("vector", lambda h: h.tensor_copy(out=p_b[0:TW, :], in_=junk[0:TW, 0:256]), reads=[b_junk], writes=[b_pb])
            transpose_k(p_b, b_pb, pT_b, b_pTb, TW, 2)
            pgs = [bank(), bank()]
            pps = [bank(), bank()]
            for kc in range(8):
                w_t, w_b = stream_w(8 + kc)
                for hf in range(2):
                    E("tensor", lambda h, hf=hf, kc=kc, w_t=w_t: h.matmul(pgs[hf][0][0:TW, :], lhsT=YT[:, kc, 0:TW], rhs=w_t[:, hf * 512:(hf + 1) * 512],
                                                                         start=(kc == 0), stop=(kc == 7)), reads=[b_YT, w_b], writes=[pgs[hf][1]])
            for kc in range(2):
                w_t, w_b = stream_w(16 + kc)
                for hf in range(2):
                    E("tensor", lambda h, hf=hf, kc=kc, w_t=w_t: h.matmul(pps[hf][0][0:TW, :], lhsT=pT_b[:, kc, 0:TW], rhs=w_t[:, hf * 512:(hf + 1) * 512],
                                                                           start=(kc == 0), stop=(kc == 1)), reads=[b_pTb, w_b], writes=[pps[hf][1]])
            for hf in range(2):
                pg, pgb = pgs[hf]
                pp_, ppb = pps[hf]
                sl = slice(hf * 512, (hf + 1) * 512)
                sigmoid_from(sg[0:TW, sl], bsg, pg[0:TW, :], pgb, TW, 512)
                E("vector", lambda h, sl=sl, pp_=pp_: h.tensor_mul(out=sg[0:TW, sl], in0=sg[0:TW, sl], in1=pp_[0:TW, :]), reads=[bsg, ppb], writes=[bsg])
                E("gpsimd", lambda h, sl=sl: h.tensor_add(out=h1[0:TW, sl], in0=h1[0:TW, sl], in1=sg[0:TW, sl]), reads=[b_h1, bsg], writes=[b_h1])
            E("sync", lambda h: h.dma_start(out=y_out_ap, in_=h1[0:TW, :]), reads=[b_h1], writes=[finish.ob], dma=True)

        finish.ob = P.buf("yout")
        outbufs.append(finish.ob)

        def own_proj(TW):
            pq, pqb = proj(O_Q, 512, TW)
            E("scalar", lambda h: h.activation(out=qa_b[0:TW, :], in_=pq[0:TW, :], func=AF.Copy, scale=0.125), reads=[pqb], writes=[b_qab])
            pz, pzb = proj(O_ZA, 512, TW)
            E("vector", lambda h: h.tensor_copy(out=za_f[0:TW, :], in_=pz[0:TW, :]), reads=[pzb], writes=[b_zaf])
            pz2, pz2b = proj(O_ZB, 512, TW)
            E("scalar", lambda h: h.activation(out=zb_f[0:TW, :], in_=pz2[0:TW, :], func=AF.Copy), reads=[pz2b], writes=[b_zbf])
            pgt, pgtb = proj(O_G, 24, TW)
            E("vector", lambda h: h.tensor_copy(out=gate_f[0:TW, :], in_=pgt[0:TW, 0:24]), reads=[pgtb], writes=[b_gatef])
            pq, pqb = proj(O_QB, 512, TW)
            qs, bqs = hw["qs"]
            sigmoid_from(qs[0:TW, :], bqs, pq[0:TW, :], pqb, TW, 512)
            E("vector", lambda h: h.tensor_mul(out=qs[0:TW, :], in0=qs[0:TW, :], in1=pq[0:TW, :]), reads=[bqs, pqb], writes=[bqs])

        def proj_fi(TW, par):
            pf, pfb = proj(O_F, 512, TW)
            sigmoid_from(e1buf[par][0][0:TW, :], e1buf[par][1], pf[0:TW, :], pfb, TW, 512)
            pi, pib = proj(O_I, 512, TW)
            E("scalar", lambda h: h.activation(out=vbuf[par][0][0:TW, :], in_=pi[0:TW, :], func=AF.Copy), reads=[pib], writes=[vbuf[par][1]])

        E("gpsimd", lambda h: h.memset(S_t[:], 0.0), writes=[b_S])
        for g in range(2):
            E("sync", lambda h, g=g: h.dma_start(out=kslcT[96:100, g, :], in_=T['kaugP']), writes=[b_kslcT], dma=True)
            E("sync", lambda h, g=g: h.dma_start(out=kcT[96:100, g, :], in_=T['caugP']), writes=[b_kcT], dma=True)
        def front(n):
            own = (n % 4 == 3)
            i = n // 4
            x_t, x_b = next_stage()
            E("sync", lambda h: h.dma_start(out=x_t[:], in_=T['xloc'][n * 128:(n + 1) * 128, :]), writes=[x_b], dma=True)
            rmsnorm_T(x_t, x_b, 128)
            slot = n % 8
            for g in range(2):
                E("sync", lambda h: h.dma_start(out=kwinT[96:100, slot, g, :], in_=T['kaugP'][:, n * 128:(n + 1) * 128]), writes=[b_kwinT], dma=True)
            kvout = None
            if own:
                kvout = (T['cmp_p'][i * 128:(i + 1) * 128, :], T['slc_p'][i * 128:(i + 1) * 128, :], T['win_p'] if i == 15 else None)
            kv_from_tokmajor(128, kslcT[0:64, :, n * 128:(n + 1) * 128], kwinT[0:64, slot, :, :],
                             vslc[:, n, :, 0:64], vwin[:, slot, :, 0:64], 16 + (n % 4) * 128, own, kvout)
            proj_fi(128, n % 2)

        if n_tiles > 0:
            front(0)
        for n in range(n_tiles):
            own = (n % 4 == 3)
            i = n // 4
            cur['i'] = i if own else -2
            if own:
                own_proj(128)
                compress_group(i)
                vc_tile(i // 4)
            cur['i'] = -2
            if n + 1 < n_tiles:
                front(n + 1)
            cur['i'] = i if own else -2
            hgrn_tile(128, own, e1buf[n % 2], vbuf[n % 2])
            if own:
                po_hg = hgrn_out(128)
                if stage >= 3:
                    win_slots = []
                    for kt in range(n - 4, n + 1):
                        if kt < 0:
                            continue
                        win_slots.append((kt % 8, 'hi' if kt == n - 4 else ('lo' if kt == n else None)))
                    attention(128, i, n + 1, win_slots, i // 4 + 1, i % 4, False)
                else:
                    E("vector", lambda h: h.memset(acc[:], 0.0), writes=[b_acc])
                finish(128, T['xloc'][n * 128:(n + 1) * 128, :], None, po_hg, T['pown'][i * 128:(i + 1) * 128, :], T['y_p'][i * 128:(i + 1) * 128, :])
        hg_ob = P.buf("hgout")
        outbufs.append(hg_ob)
        E("sync", lambda h: h.dma_start(out=T['hg_p'].rearrange("a k v -> k a v"), in_=S_t[:]), reads=[b_S], writes=[hg_ob], dma=True)

        if stage >= 4:
            idx_i, b_idxi = sb("idx_i", [128, 64], I32)
            idx_f, b_idxf = sb("idx_f", [128, 64])
            iota_p, b_iota = sb("iota_p", [128, 1])
            E("gpsimd", lambda h: h.iota(iota_p[:], pattern=[[0, 1]], base=0, channel_multiplier=1, allow_small_or_imprecise_dtypes=True), writes=[b_iota])
            pg_t = [(stage_t[i][0][:, 0:256], stage_t[i][1]) for i in range(2)]
            pgb_t, b_pgb = zt_b[:, 0:256], b_ztb
            pgc = [0]
            for bi in range(4):
                E("sync", lambda h, bi=bi: h.dma_start(out=idx_i[:], in_=T['ptab'][bi:bi + 1, :].partition_broadcast(128)), writes=[b_idxi], dma=True)
                E("vector", lambda h: h.tensor_copy(out=idx_f[:], in_=idx_i[:]), reads=[b_idxi], writes=[b_idxf])
                E("vector", lambda h: h.tensor_scalar(out=idx_f[:], in0=idx_f[:], scalar1=128.0, scalar2=iota_p[:, 0:1], op0=ALU.mult, op1=ALU.add),
                  reads=[b_idxf, b_iota], writes=[b_idxf])
                E("vector", lambda h: h.tensor_copy(out=idx_i[:], in_=idx_f[:]), reads=[b_idxf], writes=[b_idxi])
                for g in range(2):
                    E("sync", lambda h, g=g: h.dma_start(out=kslcT[96:100, g, :], in_=T['kaugS'][:, 0:8192]), writes=[b_kslcT], dma=True)
                    E("sync", lambda h, g=g: h.dma_start(out=kcT[96:100, g, :], in_=T['caugS']), writes=[b_kcT], dma=True)
                    for sw in range(2):
                        E("sync", lambda h, g=g, sw=sw: h.dma_start(out=knewT[96:100, sw, g, :], in_=T['kaugS'][:, 8192:8193]), writes=[b_knewT], dma=True)
                E("gpsimd", lambda h: h.memset(kc_raw[:, :, 0:16], 0.0), writes=[b_kcraw])

                rawv = vslc[:, :, :, :].rearrange("p a g d -> p (a g d)")
                R = rawv.rearrange("p (g c) -> p g c", g=2)
                castb = [(xn_b, b_xn), (xnT[:, :, :].rearrange("p k t -> p (k t)"), b_xnT)]
                cb_ctr = [0]

                pstage = [stage_t[0], stage_t[1], (Wflat[:, 2048:3072], b_h1), (Wflat[:, 3072:4096], b_w67)]
                ps_ctr = [0]

                def load4(src_kind, pg0):
                    st_t, st_b = pstage[ps_ctr[0] % 4]
                    ps_ctr[0] += 1
                    if src_kind == 'win':
                        E("sync", lambda h: h.dma_start(out=st_t[:, :].rearrange("p (k c) -> p k c", k=4),
                                                        in_=T['cache_win'][bi].rearrange("(k p) c -> p k c", p=128)), writes=[st_b], dma=True)
                    else:
                        src = T['cache_cmp'] if src_kind == 'cmp' else T['cache_slc']
                        for k in range(4):
                            E("gpsimd", lambda h: h.indirect_dma_start(out=st_t[:, k * 256:(k + 1) * 256], out_offset=None, in_=src[:, :],
                                                                       in_offset=bass.IndirectOffsetOnAxis(ap=idx_i[:, pg0 + k:pg0 + k + 1], axis=0)),
                              reads=[b_idxi], writes=[st_b], dma=True)
                    return st_t, st_b

                def cast4(st_t, st_b, permute):
                    c_t, c_b = castb[cb_ctr[0] % 2]
                    cb_ctr[0] += 1
                    if permute:
                        for k in range(4):
                            E("vector",
                              lambda h: h.tensor_copy(out=c_t[:, k * 256:(k + 1) * 256].rearrange("p (g s d) -> p g s d", g=2, s=2),
                                                      in_=st_t[:, k * 256:(k + 1) * 256].rearrange("p (s g d) -> p g s d", s=2, g=2)),
                              reads=[st_b], writes=[c_b])
                    else:
                        E("vector", lambda h: h.tensor_copy(out=c_t[:, 0:512], in_=st_t[:, 0:512]), reads=[st_b], writes=[c_b])
                        E("scalar", lambda h: h.activation(out=c_t[:, 512:1024], in_=st_t[:, 512:1024], func=AF.Copy), reads=[st_b], writes=[c_b])
                    return c_t, c_b

                for g in range(2):
                    E("gpsimd", lambda h: h.memset(R[:, g, 0:16], 0.0), writes=[b_vslc])
                DEPTH = 3
                pendq = [load4('cmp', qq * 4) for qq in range(DEPTH)]
                for half in range(2):
                    if half == 1:
                        E("vector", lambda h: h.tensor_copy(out=R[:, :, 0:16], in_=R[:, :, 4096:4112]), reads=[b_vslc], writes=[b_vslc])
                    for q in range(8):
                        st_t, st_b = pendq.pop(0)
                        nq_ = half * 8 + q + DEPTH
                        if nq_ < 16:
                            pendq.append(load4('cmp', nq_ * 4))
                        c_t, c_b = cast4(st_t, st_b, True)
                        for k0 in (0, 2):
                            pc_, pcb = bank()
                            for kk_ in range(2):
                                for g in range(2):
                                    k = k0 + kk_
                                    E("tensor", lambda h: h.matmul(pc_[:, (kk_ * 2 + g) * 128:(kk_ * 2 + g + 1) * 128],
                                                                   lhsT=c_t[:, k * 256 + g * 128:k * 256 + (g + 1) * 128], rhs=ident_b[:, :], start=True, stop=True),
                                      reads=[c_b, b_identb], writes=[pcb])
                            c0 = 16 + (q * 4 + k0) * 128
                            eng = "scalar" if k0 == 0 else "vector"
                            outv = R[:, :, c0:c0 + 256].rearrange("p g (k t) -> p k g t", k=2)
                            inv = pc_[:, 0:512].rearrange("p (k g t) -> p k g t", k=2, g=2)
                            if eng == "scalar":
                                E("scalar", lambda h: h.activation(out=outv, in_=inv, func=AF.Copy), reads=[pcb], writes=[b_vslc])
                            else:
                                E("vector", lambda h: h.tensor_copy(out=outv, in_=inv), reads=[pcb], writes=[b_vslc])
                    phs = [bank(), bank()]
                    for g in range(2):
                        for kv in range(2):
                            ph, phb = phs[kv]
                            for r in range(32):
                                E("tensor", lambda h: h.matmul(ph[:, g * 256:(g + 1) * 256], lhsT=w1_b[kv * 64:(kv + 1) * 64, r, :],
                                                               rhs=R[kv * 64:(kv + 1) * 64, g, r:r + 4081:16], start=(r == 0), stop=(r == 31)),
                                  reads=[b_w1, b_vslc], writes=[phb])
                    for kv in range(2):
                        ph, phb = phs[kv]
                        E("vector", lambda h: h.tensor_scalar(out=sg_t[:, kv * 512:(kv + 1) * 512], in0=ph[:, :], scalar1=b1e[:, kv:kv + 1], scalar2=None, op0=ALU.add),
                          reads=[phb, b_b1e], writes=[b_sgt])
                    E("scalar", lambda h: h.activation(out=junk[:, :], in_=sg_t[:, :], func=AF.Sigmoid), reads=[b_sgt], writes=[b_junk])
                    E("vector", lambda h: h.tensor_mul(out=Kd_b[:, :], in0=sg_t[:, 0:512], in1=junk[:, 0:512]), reads=[b_sgt, b_junk], writes=[b_Kd])
                    E("vector", lambda h: h.tensor_mul(out=v_b[:, :], in0=sg_t[:, 512:1024], in1=junk[:, 512:1024]), reads=[b_sgt, b_junk], writes=[b_vb])
                    pk, pkb = bank()
                    for g in range(2):
                        E("tensor", lambda h: h.matmul(pk[0:64, g * 256:(g + 1) * 256], lhsT=w2_b[:, 0, :], rhs=Kd_b[:, g * 256:(g + 1) * 256], start=True, stop=True),
                          reads=[b_w2, b_Kd], writes=[pkb])
                    E("vector", lambda h: h.tensor_copy(out=kcT[0:64, :, half * 256:(half + 1) * 256], in_=pk[0:64, 0:512].rearrange("p (g c) -> p g c", g=2)),
                      reads=[pkb], writes=[b_kcT])
                    E("vector", lambda h: h.tensor_copy(out=hv_all[:, :, half * 256:(half + 1) * 256], in_=v_b[:, :].rearrange("p (g c) -> p g c", g=2)),
                      reads=[b_vb], writes=[b_hvall])
                for ktc in range(4):
                    vc_tile(ktc)
                E("gpsimd", lambda h: h.memset(vslc[:, :, :, 64:65], 1.0), writes=[b_vslc])
                pendq = [load4('slc', qq * 4) for qq in range(DEPTH)]
                for q in range(16):
                    st_t, st_b = pendq.pop(0)
                    if q + DEPTH < 16:
                        pendq.append(load4('slc', (q + DEPTH) * 4))
                    c_t, c_b = cast4(st_t, st_b, False)
                    pg0 = q * 4
                    for k0 in (0, 2):
                        pt_, ptb = bank()
                        for kk_ in range(2):
                            for g in range(2):
                                k = k0 + kk_
                                transp(pt_[0:64, (kk_ * 2 + g) * 128:(kk_ * 2 + g + 1) * 128], ptb, c_t[:, k * 256 + g * 64:k * 256 + (g + 1) * 64], c_b, 128, 64)
                        outv = kslcT[0:64, :, (pg0 + k0) * 128:(pg0 + k0) * 128 + 256].rearrange("p g (k t) -> p k g t", k=2)
                        inv = pt_[0:64, 0:512].rearrange("p (k g t) -> p k g t", k=2, g=2)
                        if k0 == 0:
                            E("scalar", lambda h: h.activation(out=outv, in_=inv, func=AF.Copy), reads=[ptb], writes=[b_kslcT])
                        else:
                            E("vector", lambda h: h.tensor_copy(out=outv, in_=inv), reads=[ptb], writes=[b_kslcT])
                    E("vector", lambda h: h.tensor_copy(out=vslc[:, pg0:pg0 + 4, :, 0:64],
                                                        in_=c_t[:, :].rearrange("p (k c) -> p k c", k=4)[:, :, 128:256].rearrange("p k (g d) -> p k g d", g=2)),
                      reads=[c_b], writes=[b_vslc])
                wo_b = P.buf("winout")
                outbufs.append(wo_b)
                E("sync", lambda h: h.dma_start(out=T['win_s'][bi, 0:511, :], in_=T['cache_win'][bi, 1:512, :]), writes=[wo_b], dma=True)
                st_t, st_b = load4('win', 0)
                c_t, c_b = cast4(st_t, st_b, False)
                for k0 in (0, 2):
                    pt_, ptb = bank()
                    for kk_ in range(2):
                        for g in range(2):
                            k = k0 + kk_
                            transp(pt_[0:64, (kk_ * 2 + g) * 128:(kk_ * 2 + g + 1) * 128], ptb, c_t[:, k * 256 + g * 64:k * 256 + (g + 1) * 64], c_b, 128, 64)
                    E("scalar", lambda h: h.activation(out=kwinT[0:64, 4 + k0:6 + k0, :, :], in_=pt_[0:64, 0:512].rearrange("p (k g t) -> p k g t", k=2, g=2), func=AF.Copy),
                      reads=[ptb], writes=[b_kwinT])
                E("vector", lambda h: h.tensor_copy(out=vwin[:, 4:8, :, 0:64],
                                                    in_=c_t[:, :].rearrange("p (k c) -> p k c", k=4)[:, :, 128:256].rearrange("p k (g d) -> p k g d", g=2)),
                  reads=[c_b], writes=[b_vwin])
                for pgi in range(4):
                    for g in range(2):
                        E("sync", lambda h: h.dma_start(out=kwinT[96:100, 4 + pgi, g, :], in_=T['kaugS'][:, (60 + pgi) * 128:(61 + pgi) * 128]),
                          writes=[b_kwinT], dma=True)
                xs_t, b_xs = next_stage()
                E("sync", lambda h, bi=bi, xs_t=xs_t: h.dma_start(out=xs_t[0:1, :], in_=T['xs'][bi:bi + 1, :]), writes=[b_xs], dma=True)
                rmsnorm_T(xs_t, b_xs, 1)
                kv_from_tokmajor(1, knewT[0:64, 0, :, :], knewT[0:64, 1, :, :], vnew[0:1, 0, :, 0:64], vnew[0:1, 1, :, 0:64], None, True,
                                 (T['cmp_s'][bi:bi + 1, :], T['slc_s'][bi:bi + 1, :], T['win_s'][bi, 511:512, :]))
                E("sync", lambda h, bi=bi: h.dma_start(out=S_t[:], in_=T['state0'][bi].rearrange("a k v -> k a v")), writes=[b_S], dma=True)
                proj_fi(1, 0)
                hgrn_tile(1, True, e1buf[0], vbuf[0])
                own_proj(1)
                po_hg = hgrn_out(1)
                E("sync", lambda h, bi=bi: h.dma_start(out=T['hg_s'][bi].rearrange("a k v -> k a v"), in_=S_t[:]), reads=[b_S], writes=[hg_ob], dma=True)
                attention(1, 16, 64, [(4, None), (5, None), (6, None), (7, None)], 4, None, True)
                finish(1, T['xs'][bi:bi + 1, :], None, po_hg, T['psm'][bi:bi + 1, :], T['y_s'][bi:bi + 1, :])

        P.final_wait("sync", outbufs)
        P.build(nc, st)
    return nc, P


_NC_CACHE = {}


def _prep_inputs(inp, small_cache=False):
    perm = _perm_cols()
    w_in_p = np.ascontiguousarray(inp['w_in'][0][:, perm])
    shared = {
        'cache_cmp': inp['cache_cmp_kv'][0].reshape(NPOOLROWS, 256),
        'cache_slc': inp['cache_slc_kv'][0].reshape(NPOOLROWS, 256),
        'w_in': w_in_p, 'g_pre': inp['g_pre'], 'cmp_pe': inp['cmp_pe'][0].reshape(2, 2048),
        'cmp_w1': inp['cmp_w1'][0], 'cmp_b1': inp['cmp_b1'][0], 'cmp_w2': inp['cmp_w2'][0],
        'hg_lower': inp['hg_lower'], 'hg_norm': inp['hg_norm'], 'w_out': inp['w_out'][0], 'g_post': inp['g_post'],
        'ple_proj': inp['ple_proj'][0], 'ple_gate': inp['ple_gate'][0],
    }
    consts = [_consts(j) for j in range(4)]
    if small_cache:
        shared['cache_cmp'] = shared['cache_cmp'][:128]
        shared['cache_slc'] = shared['cache_slc'][:128]
    maps = []
    for c in range(8):
        b, j = c // 4, c % 4
        pad = 3 - j
        xloc = np.zeros((8192, D), np.float32)
        xloc[pad * 128:] = inp['x_prompt'][b, :8192 - pad * 128]
        own_rows = np.concatenate([np.arange((4 * i + j) * 128, (4 * i + j + 1) * 128) for i in range(16)])
        m = dict(shared)
        m['xloc'] = xloc
        m['pown'] = np.ascontiguousarray(inp['p_prompt'][0, b][own_rows])
        m['xs'] = np.ascontiguousarray(inp['x_sample'][4 * c:4 * c + 4, 0])
        m['psm'] = np.ascontiguousarray(inp['p_sample'][0, 4 * c:4 * c + 4, 0])
        m['cache_win'] = np.ascontiguousarray(inp['cache_win_kv'][0, 4 * c:4 * c + 4].reshape(4, 512, 256))
        m['state0'] = np.ascontiguousarray(inp['state_hgrn'][0, 4 * c:4 * c + 4])
        m['ptab'] = np.ascontiguousarray(inp['page_table'][4 * c:4 * c + 4]).astype(np.int32)
        for k, v in consts[j].items():
            m[k] = v
        maps.append({k: np.ascontiguousarray(v) for k, v in m.items()})
    return maps


def _assemble(res):
    y_p = np.zeros((2, 8192, D), np.float32)
    cmp_p = np.zeros((1, 2, 8192, 2, 2, 64), np.float32)
    slc_p = np.zeros_like(cmp_p)
    win_p = np.zeros((1, 2, 512, 2, 2, 64), np.float32)
    hg_p = np.zeros((1, 2, 4, 128, 128), np.float32)
    y_s = np.zeros((32, 1, D), np.float32)
    cmp_s = np.zeros((1, 32, 1, 2, 2, 64), np.float32)
    slc_s = np.zeros_like(cmp_s)
    win_s = np.zeros((1, 32, 512, 2, 2, 64), np.float32)
    hg_s = np.zeros((1, 32, 4, 128, 128), np.float32)
    for c in range(8):
        r = res[c]
        b, j = c // 4, c % 4
        for i in range(16):
            gt = 4 * i + j
            sl = slice(gt * 128, (gt + 1) * 128)
            y_p[b, sl] = r['y_p'][i * 128:(i + 1) * 128]
            cmp_p[0, b, sl] = r['cmp_p'][i * 128:(i + 1) * 128].reshape(128, 2, 2, 64)
            slc_p[0, b, sl] = r['slc_p'][i * 128:(i + 1) * 128].reshape(128, 2, 2, 64)
        win_p[0, b, j * 128:(j + 1) * 128] = r['win_p'].reshape(128, 2, 2, 64)
        if j == 3:
            hg_p[0, b] = r['hg_p']
        y_s[4 * c:4 * c + 4, 0] = r['y_s']
        cmp_s[0, 4 * c:4 * c + 4, 0] = r['cmp_s'].reshape(4, 2, 2, 64)
        slc_s[0, 4 * c:4 * c + 4, 0] = r['slc_s'].reshape(4, 2, 2, 64)
        win_s[0, 4 * c:4 * c + 4] = r['win_s'].reshape(4, 512, 2, 2, 64)
        hg_s[0, 4 * c:4 * c + 4] = r['hg_s']
    return (y_p, y_s, cmp_p, slc_p, win_p, hg_p, cmp_s, slc_s, win_s, hg_s)


def kernel(**inputs):
    inp = {k: np.asarray(v) for k, v in inputs.items()}
    if 'nc' not in _NC_CACHE:
        _NC_CACHE['nc'] = build_nc()[0]
    nc = _NC_CACHE['nc']
    maps = _prep_inputs(inp)
    res = run_bass_kernel_spmd(nc, maps, core_ids=list(range(8)))
    return _assemble(res.results)
```
